# Optimizing a Trainium2 kernel written in Bass

```python
import functools
import jax, jax.numpy as jnp
from jax import lax
import numpy as np

D_MODEL = 1024
BATCH = 4
SEQ = 4096
DEPTH = 1

PLE_DIM = 256
D_FF = 2816
NORM_EPS = 1e-6
GM_WIDTH = 1024
GM_GROUPS = 8
GM_GROUP_DIM = GM_WIDTH // GM_GROUPS
GM_CHUNK = 128
N_HEADS = 16
N_KV_HEADS = 4
HEAD_DIM = 64
Q_PER_KV = N_HEADS // N_KV_HEADS
KV_WIDTH = N_KV_HEADS * HEAD_DIM
ROPE_DIM = HEAD_DIM // 4
ROPE_THETA = 500000.0
CMP_LEN = 32
CMP_STRIDE = 16
CMP_HIDDEN = 256
SEL_LEN = 64
SEL_TOP = 16
WINDOW = 512
Q_BLOCK = 64
N_NSA_BRANCH = 3
N_MERGE = 2
MASK_VALUE = -1e30
FORCE_SCORE = 1e9
IN_SPLITS = [GM_WIDTH, GM_WIDTH, N_HEADS * HEAD_DIM, 6 * KV_WIDTH, N_HEADS * N_NSA_BRANCH, N_MERGE * D_MODEL]

kernel_name = 'hybrid_gmlp_nsa_macaron'


def rms_norm(x, g):
    xf = x.astype(jnp.float32)
    y = xf * lax.rsqrt(jnp.mean(xf * xf, axis=-1, keepdims=True) + NORM_EPS)
    return (y * g.astype(jnp.float32)).astype(x.dtype)


def layer_norm(x, g, b):
    xf = x.astype(jnp.float32)
    mu = jnp.mean(xf, axis=-1, keepdims=True)
    var = jnp.mean(jnp.square(xf - mu), axis=-1, keepdims=True)
    y = (xf - mu) * lax.rsqrt(var + NORM_EPS)
    return (y * g.astype(jnp.float32) + b.astype(jnp.float32)).astype(x.dtype)


def swiglu(x, w_in, w_out):
    gate, up = jnp.split(x @ w_in, 2, axis=-1)
    return (jax.nn.silu(gate) * up) @ w_out


def rotary(x, pos):
    inv_freq = ROPE_THETA ** (-jnp.arange(0, ROPE_DIM, 2, dtype=jnp.float32) / ROPE_DIM)
    ang = pos.astype(jnp.float32)[:, None] * inv_freq[None, :]
    cos = jnp.cos(ang)[None, :, None, :]
    sin = jnp.sin(ang)[None, :, None, :]
    xr = x[..., :ROPE_DIM].astype(jnp.float32)
    x1, x2 = jnp.split(xr, 2, axis=-1)
    rot = jnp.concatenate([x1 * cos - x2 * sin, x2 * cos + x1 * sin], axis=-1).astype(x.dtype)
    return jnp.concatenate([rot, x[..., ROPE_DIM:]], axis=-1)


def masked_softmax(s, mask):
    s = jnp.where(mask, s.astype(jnp.float32), MASK_VALUE)
    return jax.nn.softmax(s, axis=-1) * mask


def gmlp_mixer(u, v, ln_g, ln_b, w_s, b_s):
    B, S, _ = u.shape
    vn = layer_norm(v, ln_g, ln_b).reshape(B, S // GM_CHUNK, GM_CHUNK, GM_GROUPS, GM_GROUP_DIM)
    causal = jnp.tril(jnp.ones((GM_CHUNK, GM_CHUNK), dtype=bool))
    w = jnp.where(causal[None], w_s, jnp.zeros_like(w_s))
    mix = jnp.einsum('gts,bcsgd->bctgd', w, vn) + b_s.T[None, None, :, :, None]
    return u * mix.reshape(B, S, GM_WIDTH)


def compress(x, pos_emb, w1, w2):
    B, S = x.shape[:2]
    n_sub = CMP_LEN // CMP_STRIDE
    n_chunks = S // CMP_STRIDE
    n_cmp = n_chunks - n_sub + 1
    xc = x.reshape(B, n_chunks, CMP_STRIDE, N_KV_HEADS, HEAD_DIM)
    blocks = jnp.concatenate([xc[:, r:r + n_cmp] for r in range(n_sub)], axis=2)
    blocks = blocks + pos_emb[None, None, :, None, :]
    flat = blocks.transpose(0, 1, 3, 2, 4).reshape(B, n_cmp, N_KV_HEADS, CMP_LEN * HEAD_DIM)
    return jax.nn.gelu(flat @ w1) @ w2


def selection_importance(p_cmp, n_sel):
    r_sel = SEL_LEN // CMP_STRIDE
    l_cmp = CMP_LEN // CMP_STRIDE
    pad = ((0, 0),) * (p_cmp.ndim - 1) + ((l_cmp - 1, r_sel + l_cmp),)
    padded = jnp.pad(p_cmp, pad)
    imp = jnp.zeros(p_cmp.shape[:-1] + (n_sel,), p_cmp.dtype)
    for r in range(1 - l_cmp, r_sel):
        start = r + l_cmp - 1
        imp = imp + padded[..., start:start + r_sel * n_sel:r_sel]
    return imp


def nsa_query_block(args, k_cmp, v_cmp, k_sel_blocks, v_sel_blocks, k_win_pad, v_win_pad):
    q_raw, q_rot, gate_logits, s0 = args
    B = q_raw.shape[0]
    n_cmp = k_cmp.shape[1]
    n_sel = k_sel_blocks.shape[2]
    n_top = min(SEL_TOP, n_sel)
    t = s0 + jnp.arange(Q_BLOCK)
    scale = HEAD_DIM ** -0.5
    qg = q_raw.reshape(B, Q_BLOCK, N_KV_HEADS, Q_PER_KV, HEAD_DIM)
    s_c = jnp.einsum('bqkgd,bnkd->bkgqn', qg, k_cmp) * scale
    cmp_end = jnp.arange(n_cmp) * CMP_STRIDE + CMP_LEN - 1
    p_c = masked_softmax(s_c, cmp_end[None, :] <= t[:, None])
    o_cmp = jnp.einsum('bkgqn,bnkd->bqkgd', p_c.astype(v_cmp.dtype), v_cmp)
    imp = selection_importance(p_c.sum(axis=2), n_sel)
    blk = jnp.arange(n_sel)[None, :]
    cur = (t // SEL_LEN)[:, None]
    forced = (blk == 0) | (blk == cur) | (blk == cur - 1)
    future = blk > cur
    score = jnp.where(forced, FORCE_SCORE, jnp.where(future, -FORCE_SCORE, imp))
    _, idx = lax.top_k(score, n_top)
    bi = jnp.arange(B)[:, None, None, None]
    ki = jnp.arange(N_KV_HEADS)[None, :, None, None]
    k_g = k_sel_blocks[bi, ki, idx]
    v_g = v_sel_blocks[bi, ki, idx]
    qs = q_rot.reshape(B, Q_BLOCK, N_KV_HEADS, Q_PER_KV, HEAD_DIM)
    s_s = jnp.einsum('bqkgd,bkqnld->bkgqnl', qs, k_g) * scale
    key_pos = idx[..., None] * SEL_LEN + jnp.arange(SEL_LEN)
    sel_mask = (key_pos <= t[None, None, :, None, None]).reshape(B, N_KV_HEADS, 1, Q_BLOCK, n_top * SEL_LEN)
    p_s = masked_softmax(s_s.reshape(B, N_KV_HEADS, Q_PER_KV, Q_BLOCK, n_top * SEL_LEN), sel_mask)
    p_s = p_s.reshape(s_s.shape).astype(v_g.dtype)
    o_sel = jnp.einsum('bkgqnl,bkqnld->bqkgd', p_s, v_g)
    k_w = lax.dynamic_slice_in_dim(k_win_pad, s0, WINDOW + Q_BLOCK, axis=1)
    v_w = lax.dynamic_slice_in_dim(v_win_pad, s0, WINDOW + Q_BLOCK, axis=1)
    pos = s0 - WINDOW + jnp.arange(WINDOW + Q_BLOCK)
    dist = t[:, None] - pos[None, :]
    win_mask = (dist >= 0) & (dist < WINDOW) & (pos[None, :] >= 0)
    s_w = jnp.einsum('bqkgd,blkd->bkgql', qs, k_w) * scale
    p_w = masked_softmax(s_w, win_mask).astype(v_w.dtype)
    o_win = jnp.einsum('bkgql,blkd->bqkgd', p_w, v_w)
    g = jax.nn.sigmoid(gate_logits.astype(jnp.float32)).reshape(B, Q_BLOCK, N_KV_HEADS, Q_PER_KV, N_NSA_BRANCH)
    o = g[..., 0:1] * o_cmp + g[..., 1:2] * o_sel + g[..., 2:3] * o_win
    return o.reshape(B, Q_BLOCK, N_HEADS * HEAD_DIM).astype(q_raw.dtype)


def nsa_mixer(q, k_c, v_c, k_s, v_s, k_w, v_w, gate_logits,
              cmp_pos_k, cmp_k_w1, cmp_k_w2, cmp_pos_v, cmp_v_w1, cmp_v_w2):
    B, S = q.shape[:2]
    pos = jnp.arange(S)
    q_rot = rotary(q, pos)
    k_cmp = compress(k_c, cmp_pos_k, cmp_k_w1, cmp_k_w2)
    v_cmp = compress(v_c, cmp_pos_v, cmp_v_w1, cmp_v_w2)
    n_sel = S // SEL_LEN

    def to_blocks(a):
        return a.reshape(B, n_sel, SEL_LEN, N_KV_HEADS, HEAD_DIM).transpose(0, 3, 1, 2, 4)

    k_sel_blocks = to_blocks(rotary(k_s, pos))
    v_sel_blocks = to_blocks(v_s)
    pad = ((0, 0), (WINDOW, 0), (0, 0), (0, 0))
    k_win_pad = jnp.pad(rotary(k_w, pos), pad)
    v_win_pad = jnp.pad(v_w, pad)
    n_qb = S // Q_BLOCK

    def q_blocks(a):
        return a.reshape((B, n_qb, Q_BLOCK) + a.shape[2:]).swapaxes(0, 1)

    starts = jnp.arange(n_qb, dtype=jnp.int32) * Q_BLOCK
    step = functools.partial(nsa_query_block, k_cmp=k_cmp, v_cmp=v_cmp, k_sel_blocks=k_sel_blocks,
                             v_sel_blocks=v_sel_blocks, k_win_pad=k_win_pad, v_win_pad=v_win_pad)
    out = lax.map(step, (q_blocks(q), q_blocks(q_rot), q_blocks(gate_logits), starts))
    return out.swapaxes(0, 1).reshape(B, S, N_HEADS * HEAD_DIM)


def setup_inputs(seed: int = 0) -> dict:
    key = jax.random.key(seed)
    ks = iter(jax.random.split(key, 40))

    def nrm(shape, scale):
        return scale * jax.random.normal(next(ks), shape, jnp.float32)

    def gain(n):
        return 1.0 + nrm((DEPTH, n), 0.02)

    in_width = sum(IN_SPLITS)
    return {
        'x': nrm((BATCH, SEQ, D_MODEL), 1.0),
        'p': nrm((DEPTH, BATCH, SEQ, PLE_DIM), 1.0),
        'ffn1_norm': gain(D_MODEL),
        'ffn1_w_in': nrm((DEPTH, D_MODEL, 2 * D_FF), D_MODEL ** -0.5),
        'ffn1_w_out': nrm((DEPTH, D_FF, D_MODEL), D_FF ** -0.5),
        'mix_norm': gain(D_MODEL),
        'w_in': nrm((DEPTH, D_MODEL, in_width), D_MODEL ** -0.5),
        'gm_ln_g': gain(GM_WIDTH),
        'gm_ln_b': nrm((DEPTH, GM_WIDTH), 0.02),
        'gm_w_s': nrm((DEPTH, GM_GROUPS, GM_CHUNK, GM_CHUNK), GM_CHUNK ** -0.5),
        'gm_b_s': 1.0 + nrm((DEPTH, GM_GROUPS, GM_CHUNK), 0.02),
        'w_branch_a': nrm((DEPTH, GM_WIDTH, D_MODEL), GM_WIDTH ** -0.5),
        'cmp_pos_k': nrm((DEPTH, CMP_LEN, HEAD_DIM), 0.02),
        'cmp_k_w1': nrm((DEPTH, CMP_LEN * HEAD_DIM, CMP_HIDDEN), (CMP_LEN * HEAD_DIM) ** -0.5),
        'cmp_k_w2': nrm((DEPTH, CMP_HIDDEN, HEAD_DIM), CMP_HIDDEN ** -0.5),
        'cmp_pos_v': nrm((DEPTH, CMP_LEN, HEAD_DIM), 0.02),
        'cmp_v_w1': nrm((DEPTH, CMP_LEN * HEAD_DIM, CMP_HIDDEN), (CMP_LEN * HEAD_DIM) ** -0.5),
        'cmp_v_w2': nrm((DEPTH, CMP_HIDDEN, HEAD_DIM), CMP_HIDDEN ** -0.5),
        'w_branch_b': nrm((DEPTH, N_HEADS * HEAD_DIM, D_MODEL), (N_HEADS * HEAD_DIM) ** -0.5),
        'w_out': nrm((DEPTH, D_MODEL, D_MODEL), D_MODEL ** -0.5),
        'ffn2_norm': gain(D_MODEL),
        'ffn2_w_in': nrm((DEPTH, D_MODEL, 2 * D_FF), D_MODEL ** -0.5),
        'ffn2_w_out': nrm((DEPTH, D_FF, D_MODEL), D_FF ** -0.5),
        'ple_norm': gain(D_MODEL),
        'ple_w_gate': nrm((DEPTH, D_MODEL, D_MODEL), D_MODEL ** -0.5),
        'ple_w_proj': nrm((DEPTH, PLE_DIM, D_MODEL), PLE_DIM ** -0.5),
        'final_norm': 1.0 + nrm((D_MODEL,), 0.02),
    }


def reference(x, p, ffn1_norm, ffn1_w_in, ffn1_w_out, mix_norm, w_in, gm_ln_g, gm_ln_b, gm_w_s, gm_b_s,
              w_branch_a, cmp_pos_k, cmp_k_w1, cmp_k_w2, cmp_pos_v, cmp_v_w1, cmp_v_w2, w_branch_b, w_out,
              ffn2_norm, ffn2_w_in, ffn2_w_out, ple_norm, ple_w_gate, ple_w_proj, final_norm):
    B, S, _ = x.shape
    offsets = np.cumsum(IN_SPLITS)[:-1].tolist()
    h = x
    for i in range(DEPTH):
        h = h + 0.5 * swiglu(rms_norm(h, ffn1_norm[i]), ffn1_w_in[i], ffn1_w_out[i])
        n = rms_norm(h, mix_norm[i])
        u, v, q, kv, nsa_gate, merge_gate = jnp.split(n @ w_in[i], offsets, axis=-1)
        y_a = gmlp_mixer(jax.nn.gelu(u), jax.nn.gelu(v), gm_ln_g[i], gm_ln_b[i], gm_w_s[i], gm_b_s[i]) @ w_branch_a[i]
        k_c, v_c, k_s, v_s, k_w, v_w = [a.reshape(B, S, N_KV_HEADS, HEAD_DIM) for a in jnp.split(kv, 6, axis=-1)]
        o_b = nsa_mixer(q.reshape(B, S, N_HEADS, HEAD_DIM), k_c, v_c, k_s, v_s, k_w, v_w,
                        nsa_gate.reshape(B, S, N_HEADS, N_NSA_BRANCH),
                        cmp_pos_k[i], cmp_k_w1[i], cmp_k_w2[i], cmp_pos_v[i], cmp_v_w1[i], cmp_v_w2[i])
        y_b = o_b @ w_branch_b[i]
        g_a, g_b = jnp.split(jax.nn.sigmoid(merge_gate), 2, axis=-1)
        h = h + (g_a * y_a + g_b * y_b) @ w_out[i]
        h = h + 0.5 * swiglu(rms_norm(h, ffn2_norm[i]), ffn2_w_in[i], ffn2_w_out[i])
        gate = jax.nn.sigmoid(rms_norm(h, ple_norm[i]) @ ple_w_gate[i])
        h = h + gate * (p[i] @ ple_w_proj[i])
    return rms_norm(h, final_norm)
```

```python
from collections import defaultdict
from contextlib import ExitStack
import os
import numpy as np
import concourse.bass as bass
import concourse.mybir as mybir
from concourse.bass_utils import run_bass_kernel_spmd

F32 = mybir.dt.float32
BF16 = mybir.dt.bfloat16
ALU = mybir.AluOpType
AF = mybir.ActivationFunctionType
AX = mybir.AxisListType
NEG = -30000.0
EPS = 1e-6
DFF = 2816
NJ = 22


class Buf:
    __slots__ = ("name", "w", "r", "excl")

    def __init__(self, name="", excl=False):
        self.name = name
        self.w = {}
        self.r = {}
        self.excl = excl


class Prog:
    ENGS = ("pe", "act", "dve", "pool", "sp")
    DMA_POOL = {"w": 8, "i": 10, "s": 6, "c": 4, "o": 4, "wt": 6}

    def __init__(self, nc):
        self.nc = nc
        self.q = {e: [] for e in self.ENGS}
        self.cnt = {}
        self.seen = {e: defaultdict(int) for e in self.ENGS}
        self.epoch = defaultdict(int)
        self.semkeys = []
        self.dma_n = defaultdict(int)

    def _key(self, stream):
        k = (stream, self.epoch[stream])
        if k not in self.cnt:
            self.cnt[k] = 0
            self.semkeys.append(k)
        return k

    def _signal(self, stream, inc):
        k = self._key(stream)
        if self.cnt[k] + inc > 30000:
            self.epoch[stream] += 1
            k = self._key(stream)
        self.cnt[k] += inc
        return k, self.cnt[k]

    def emit(self, eng, fn, reads=(), writes=(), dma=None):
        need = {}
        own = eng if dma is None else None
        for b in reads:
            for s, v in b.w.items():
                if eng == "pe" and s[0] == "pe":
                    continue
                if need.get(s, 0) < v:
                    need[s] = v
            if b.excl:
                for s, v in b.r.items():
                    if s[0] == own:
                        continue
                    if need.get(s, 0) < v:
                        need[s] = v
        for b in writes:
            for d in (b.w, b.r):
                for s, v in d.items():
                    if s[0] == own and (own == "pe" or os.environ.get("KSAMEENG", "1") == "0"):
                        continue
                    if need.get(s, 0) < v:
                        need[s] = v
        waits = []
        for s, v in need.items():
            if self.seen[eng][s] < v:
                self.seen[eng][s] = v
                waits.append((s, v))
        if dma is None:
            k, val = self._signal(eng, 1)
            inc = 1
        else:
            g = self.DMA_POOL.get(dma, 4)
            n = self.dma_n[dma]; self.dma_n[dma] += 1
            k = ("dma_" + dma, n % g)
            if k not in self.cnt:
                self.cnt[k] = 0
                self.semkeys.append(k)
            if self.cnt[k] > 0 and self.seen[eng][k] < self.cnt[k]:
                self.seen[eng][k] = self.cnt[k]
                waits.append((k, self.cnt[k]))
            self.cnt[k] += 16
            val = self.cnt[k]
            inc = 16
        self.q[eng].append((waits, fn, k, inc))
        for b in reads:
            if b.r.get(k, 0) < val:
                b.r[k] = val
        for b in writes:
            if b.w.get(k, 0) < val:
                b.w[k] = val

    def wait_sems(self, eng, prefix):
        need = [(k, v) for k, v in self.cnt.items() if k[0].startswith(prefix) and v > 0]
        self.q[eng].append((need, None, None, 0))

    def build(self, stack):
        nc = self.nc
        sems = {}
        for k in self.semkeys:
            sems[k] = stack.enter_context(nc.semaphore("s_%s_%d" % k))
        block = stack.enter_context(nc.Block())
        handles = {"pe": block.tensor, "act": block.scalar, "dve": block.vector,
                   "pool": block.gpsimd, "sp": block.sync}
        for e in self.ENGS:
            lst = self.q[e]
            if not lst:
                continue

            def body(h, lst=lst):
                for waits, fn, k, inc in lst:
                    for s, v in waits:
                        h.wait_ge(sems[s], v)
                    if fn is not None:
                        fn(h).then_inc(sems[k], inc)
            handles[e](body)


def build_program():
    nc = bass.Bass("TRN2", target_bir_lowering=False)
    P = Prog(nc)

    def din(name, shape):
        return nc.dram_tensor(name, list(shape), F32, kind="ExternalInput").ap()

    xT = din("xT", [1024, 4096]); pT = din("pT", [256, 2048])
    f1_win = din("f1_win", [1024, 5632]); f1_wout = din("f1_wout", [2816, 1024])
    f2_win = din("f2_win", [1024, 5632]); f2_wout = din("f2_wout", [2816, 1024])
    w_in = din("w_in", [1024, 6704])
    vecs = din("vecs", [128, 40])
    ln_g = din("ln_g", [1024]); ln_b = din("ln_b", [1024])
    wsT = din("wsT", [128, 8, 128]); b_s = din("b_s", [1024])
    w_a = din("w_a", [1024, 1024]); w_b = din("w_b", [1024, 1024]); w_o = din("w_o", [1024, 1024])
    posk = din("posk", [128, 16]); posv = din("posv", [128, 16])
    cw1k = din("cw1k", [128, 16, 256]); cw1v = din("cw1v", [128, 16, 256])
    cw2k = din("cw2k", [256, 64]); cw2v = din("cw2v", [256, 64])
    w_pg = din("w_pg", [1024, 1024]); w_pp = din("w_pp", [256, 1024])
    c_ident = din("c_ident", [128, 128]); c_rmat = din("c_rmat", [128, 128])
    c_sel = din("c_sel", [48, 48 * 64]); c_aaug = din("c_aaug", [128, 2 * 65])
    c_E = din("c_E", [64, 4096]); c_tfirst = din("c_tfirst", [128, 128]); c_tdiag = din("c_tdiag", [128, 128])
    c_dfirst = din("c_dfirst", [128, 128]); c_dmid = din("c_dmid", [128, 128])
    c_cmpb = din("c_cmpb", [128, 2 * 2048])
    c_keep = din("c_keep", [128, 16 * 64]); c_bias = din("c_bias", [128, 16 * 64]); c_valid = din("c_valid", [128, 16 * 64])
    c_cos = din("c_cos", [128, 4096]); c_sin = din("c_sin", [128, 4096]); c_tril = din("c_tril", [128, 128])
    outT = nc.dram_tensor("outT", [1024, 2048], F32, kind="ExternalOutput").ap()
    N_s = nc.dram_tensor("N_s", [8, 128, 8, 512], BF16).ap()
    H_s = nc.dram_tensor("H_s", [4, 128, 8, 512], F32).ap()
    KW_s = nc.dram_tensor("KW_s", [64, 4, 4096], BF16).ap()
    VW_s = nc.dram_tensor("VW_s", [128, 32, 512], BF16).ap()
    WS = nc.dram_tensor("WS", [18, 128, 8, 512], BF16).ap()
    W1s = nc.dram_tensor("W1s", [128, 8, 2 * DFF], BF16).ap()
    W2s = nc.dram_tensor("W2s", [128, NJ, 1024], BF16).ap()
    WPGs = nc.dram_tensor("WPGs", [8, 128, 8, 128], BF16).ap()
    bWPGs = [Buf() for _ in range(8)]
    bW1s = [Buf() for _ in range(NJ)]; bW2s = [Buf() for _ in range(NJ)]
    bWS = [Buf() for _ in range(18)]
    bN = [Buf() for _ in range(8)]; bH = [[Buf() for _ in range(2)] for _ in range(4)]
    bKW = [Buf() for _ in range(8)]; bVW = [Buf() for _ in range(8)]

    st = ExitStack()
    with st:
        arena = {"p": 16640}
        barrier = {"snap": {}}

        def sb(name, shape, dt=F32):
            n = 1
            for d_ in shape[1:]:
                n *= d_
            nbytes = n * (4 if dt == F32 else 2)
            off = arena["p"]
            arena["p"] = off + ((nbytes + 63) // 64) * 64
            assert arena["p"] <= 229376, (name, arena["p"])
            return nc.alloc_sbuf_tensor_at(name, list(shape), dt, offset=off)

        def new_phase():
            barrier["snap"] = dict(P.cnt)

        def fresh():
            b = Buf(); b.w = dict(barrier["snap"]); return b

        def rebarrier(bufs):
            for b in bufs:
                for k_, v_ in barrier["snap"].items():
                    if b.w.get(k_, 0) < v_:
                        b.w[k_] = v_

        ps = [st.enter_context(nc.psum_tensor("ps%d" % i, [128, 512], F32)) for i in range(7)]
        psT = st.enter_context(nc.psum_tensor("psT", [128, 1024], BF16))
        bps = [Buf(excl=True) for _ in range(7)]; bpsT = Buf(excl=True)
        ring_state = {"i": 0, "n": 6}

        def slot():
            i = ring_state["i"] % ring_state["n"]
            ring_state["i"] += 1
            return ps[i], bps[i]

        def mm(out, lhsT, rhs, start, stop, reads, writes, sgc=False):
            if sgc:
                P.emit("pe", lambda e: e.matmul(out, lhsT=lhsT, rhs=rhs, start=start, stop=stop, skip_group_check=True), reads, writes)
            else:
                P.emit("pe", lambda e: e.matmul(out, lhsT=lhsT, rhs=rhs, start=start, stop=stop), reads, writes)

        def act(out, in_, func, reads, writes, **kw):
            P.emit("act", lambda e: e.activation(out=out, in_=in_, func=func, **kw), reads, writes)

        def tt(eng, out, in0, in1, op, reads, writes):
            P.emit(eng, lambda e: e.tensor_tensor(out=out, in0=in0, in1=in1, op=op), reads, writes)

        def tsc(eng, out, in0, s1, s2, op0, op1, reads, writes):
            if s2 is None:
                P.emit(eng, lambda e: e.tensor_scalar(out=out, in0=in0, scalar1=s1, scalar2=None, op0=op0), reads, writes)
            else:
                P.emit(eng, lambda e: e.tensor_scalar(out=out, in0=in0, scalar1=s1, scalar2=s2, op0=op0, op1=op1), reads, writes)

        def stt(eng, out, in0, scalar, in1, op0, op1, reads, writes):
            P.emit(eng, lambda e: e.scalar_tensor_tensor(out=out, in0=in0, scalar=scalar, in1=in1, op0=op0, op1=op1), reads, writes)

        def cp(eng, out, in_, reads, writes):
            P.emit(eng, lambda e: e.tensor_copy(out=out, in_=in_), reads, writes)

        def dma(eng, out, in_, reads, writes, stream):
            P.emit(eng, lambda e: e.dma_start(out=out, in_=in_), reads, writes, dma=stream)

        ident = sb("ident", [128, 128], BF16); onesb = sb("onesb", [128, 128], BF16)
        rmat = sb("rmat", [128, 128], BF16); vec = sb("vec", [128, 40])
        tfirst = sb("tfirst", [128, 128], BF16); tdiag = sb("tdiag", [128, 128], BF16)
        dfirst = sb("dfirst", [128, 128], BF16); dmid = sb("dmid", [128, 128], BF16)
        epsb = sb("epsb", [128, 1]); tinyb = sb("tinyb", [128, 1])
        bC = Buf()
        for t_, d_ in ((ident, c_ident), (rmat, c_rmat), (tfirst, c_tfirst), (tdiag, c_tdiag), (dfirst, c_dfirst), (dmid, c_dmid)):
            dma("pool", t_[:], d_, [], [bC], "c")
        dma("sp", vec[:], vecs, [], [bC], "i")
        P.emit("dve", lambda e: e.memset(onesb[:], 1.0), [], [bC])
        P.emit("dve", lambda e: e.memset(epsb[:], EPS), [], [bC])
        G1, GM, G2, GP, GF = 0, 8, 16, 24, 32
        mark0 = arena["p"]

        W1 = sb("W1", [128, 8, 2 * DFF], BF16)
        W2 = sb("W2", [128, NJ, 1024], BF16)
        bW1 = [Buf() for _ in range(NJ)]; bW2 = [Buf() for _ in range(NJ)]

        def load_ffn_scratch():
            for j0 in range(0, NJ, 2):
                for base in (0, DFF):
                    dma("sp", W1[:, :, base + j0 * 128: base + (j0 + 2) * 128], W1s[:, :, base + j0 * 128: base + (j0 + 2) * 128],
                        [bW1s[j0], bW1s[j0 + 1]], [bW1[j0], bW1[j0 + 1]], "wt")
            for j0 in range(0, NJ, 2):
                dma("sp", W2[:, j0:j0 + 2, :], W2s[:, j0:j0 + 2, :], [bW2s[j0], bW2s[j0 + 1]], [bW2[j0], bW2[j0 + 1]], "wt")

        def cast_ffn_to_scratch(win, wout):
            wv = win.rearrange("(c p) n -> p c n", p=128)
            for j0 in range(0, NJ, 2):
                for base in (0, DFF):
                    dma("pool", W1s[:, :, base + j0 * 128: base + (j0 + 2) * 128], wv[:, :, base + j0 * 128: base + (j0 + 2) * 128],
                        [], [bW1s[j0], bW1s[j0 + 1]], "w")
            wo = wout.rearrange("(j p) n -> p j n", p=128)
            for j0 in range(0, NJ, 2):
                dma("pool", W2s[:, j0:j0 + 2, :], wo[:, j0:j0 + 2, :], [], [bW2s[j0], bW2s[j0 + 1]], "w")

        def load_ffn(win, wout):
            wv = win.rearrange("(c p) n -> p c n", p=128)
            for j0 in range(0, NJ, 2):
                for base in (0, DFF):
                    dma("pool", W1[:, :, base + j0 * 128: base + (j0 + 2) * 128], wv[:, :, base + j0 * 128: base + (j0 + 2) * 128],
                        [], [bW1[j0], bW1[j0 + 1]], "w")
            wo = wout.rearrange("(j p) n -> p j n", p=128)
            for j0 in range(0, NJ, 2):
                dma("pool", W2[:, j0:j0 + 2, :], wo[:, j0:j0 + 2, :], [], [bW2[j0], bW2[j0 + 1]], "w")

        TB = 512
        rstd = sb("rstd", [128, TB]); brstd = Buf()
        nb_off = arena["p"]
        nb = [sb("nb0", [128, 8, TB], BF16)]; bnb = [Buf()]
        hid_off = arena["p"]
        hid = sb("hid", [128, NJ, TB], BF16); bhid = [Buf() for _ in range(NJ)]
        osa = nc.alloc_sbuf_tensor_at("osa", [128, 4, TB], F32, offset=nb_off)
        osb = nc.alloc_sbuf_tensor_at("osb", [128, 4, TB], F32, offset=hid_off + 8 * TB * 2)
        sgt = [sb("sgt%d" % i, [128, TB]) for i in range(3)]; bsgt = [Buf() for _ in range(3)]
        xr0 = sb("xr0", [128, 8, TB])
        xr1_off = arena["p"]
        xr1 = sb("xr1", [128, 8, TB])
        xr = [xr0, xr1]; bxr = [Buf(), Buf()]
        cnt = {"sg": 0, "nb": 0}
        ffn_end = arena["p"]
        print("SBUF FFN end", ffn_end)
        ffn_bufs = bW1 + bW2 + bxr + [brstd] + bnb + bhid + bsgt

        def rmsnorm(xt, bx, gcol, out_t, bout, T):
            tt("dve", hid[:, 0:8, 0:T], xt[:, :, 0:T], xt[:, :, 0:T], ALU.mult, [bx], bhid[0:8])
            pt_, bp = slot()
            for c in range(8):
                mm(pt_[:, 0:T], onesb[:], hid[:, c, 0:T], c == 0, c == 7, bhid[0:8] + [bC], [bp])
            act(rstd[:, 0:T], pt_[:, 0:T], AF.Sqrt, [bp, bC], [brstd], scale=1.0 / 1024, bias=EPS)
            P.emit("dve", lambda e: e.reciprocal(out=rstd[:, 0:T], in_=rstd[:, 0:T]), [brstd], [brstd])
            for c in range(8):
                stt("dve", out_t[:, c, 0:T], xt[:, c, 0:T], vec[:, gcol + c:gcol + c + 1], rstd[:, 0:T],
                    ALU.mult, ALU.mult, [bx, brstd, bC], [bout])

        def ffn(xt, bx, gcol, T):
            n1 = nb[0]; bn1 = bnb[0]
            rmsnorm(xt, bx, gcol, n1, bn1, T)
            for j in range(NJ):
                pt_, bp = slot()
                for k in range(8):
                    mm(pt_[:, 0:T], W1[:, k, j * 128:(j + 1) * 128], n1[:, k, 0:T], k == 0, k == 7, [bW1[j], bn1], [bp])
                pu, bpu = slot()
                for k in range(8):
                    mm(pu[:, 0:T], W1[:, k, DFF + j * 128:DFF + (j + 1) * 128], n1[:, k, 0:T], k == 0, k == 7, [bW1[j], bn1], [bpu])
                i = cnt["sg"] % 3; cnt["sg"] += 1
                act(sgt[i][:, 0:T], pt_[:, 0:T], AF.Silu, [bp], [bsgt[i]])
                tt("dve", hid[:, j, 0:T], sgt[i][:, 0:T], pu[:, 0:T], ALU.mult, [bsgt[i], bpu], [bhid[j]])
            for c in range(8):
                pt_, bp = slot()
                for j in range(NJ):
                    mm(pt_[:, 0:T], W2[:, j, c * 128:(c + 1) * 128], hid[:, j, 0:T], j == 0, j == NJ - 1, [bW2[j], bhid[j]], [bp])
                stt("dve", xt[:, c, 0:T], pt_[:, 0:T], 0.5, xt[:, c, 0:T], ALU.mult, ALU.add, [bp, bx], [bx])

        xv = xT.rearrange("(c p) t -> p c t", p=128)
        dma("sp", xr[0][:], xv[:, :, 0:TB], [], [bxr[0]], "i")
        load_ffn(f1_win, f1_wout)
        winv0 = w_in.rearrange("(c p) n -> p c n", p=128)
        wav0 = w_a.rearrange("(c p) n -> p c n", p=128); wbv0 = w_b.rearrange("(c p) n -> p c n", p=128); wov0 = w_o.rearrange("(c p) n -> p c n", p=128)
        ws_src = [(winv0, 0, 512), (winv0, 512, 512), (winv0, 1024, 512), (winv0, 1536, 512),
                  (winv0, 4656, 512), (wav0, 0, 512), (winv0, 5168, 512), (wav0, 512, 512),
                  (winv0, 2048, 256), (winv0, 2304, 256), (winv0, 2560, 256), (winv0, 2816, 256),
                  (winv0, 5680, 512), (wbv0, 0, 512), (winv0, 6192, 512), (wbv0, 512, 512),
                  (wov0, 0, 512), (wov0, 512, 512)]
        for ti, (src, c0, ncol) in enumerate(ws_src):
            dma("pool", WS[ti][:, :, 0:ncol], src[:, :, c0:c0 + ncol], [], [bWS[ti]], "w")
        cast_ffn_to_scratch(f2_win, f2_wout)
        wpgv = w_pg.rearrange("(c p) n -> p c n", p=128)
        for c in range(8):
            dma("pool", WPGs[c], wpgv[:, :, c * 128:(c + 1) * 128], [], [bWPGs[c]], "w")
        for blk in range(8):
            s = blk
            xt, bx = xr[blk % 2], bxr[blk % 2]
            if blk + 1 < 8:
                dma("sp", xr[(blk + 1) % 2][:], xv[:, :, (blk + 1) * TB:(blk + 2) * TB], [], [bxr[(blk + 1) % 2]], "i")
            ffn(xt, bx, G1, TB)
            if s % 2 == 1:
                dma("sp", H_s[s // 2], xt[:], [bx], bH[s // 2], "s")
            n2 = nb[0]; bn2 = bnb[0]
            rmsnorm(xt, bx, GM, n2, bn2, TB)
            dma("sp", N_s[s], n2[:], [bn2], [bN[s]], "s")

        KSTOP = os.environ.get("KSTOP", "")
        if KSTOP == "A":
            P.wait_sems("sp", "dma_"); P.build(st); return nc
        arena["p"] = mark0
        new_phase()
        KS = sb("KS", [128, 4, 4096], BF16); bKS = fresh()
        VS = sb("VS", [128, 32, 4, 128], BF16); bVS = fresh()
        KCMP = sb("KCMP", [64, 4, 256], BF16); VCMP = sb("VCMP", [128, 2, 4, 128], BF16); bCMP = fresh()
        mark1 = arena["p"]
        for k in range(4):
            dma("pool", KS[64:128, k, :], c_E, [], [bKS], "c")
        P.emit("pool", lambda e: e.memset(VS[:, :, :, 64:128], 1.0), [], [bVS])
        P.emit("pool", lambda e: e.memset(VCMP[:, :, :, 64:128], 1.0), [], [bCMP])
        P.emit("pool", lambda e: e.memset(KCMP[:], 0.0), [], [bCMP])
        if True:
            sbB = sb
            ctab = sb("ctab", [128, 512]); stab = sb("stab", [128, 512]); btab = fresh()
            tq = [sb("tq%d" % i, [128, 512], BF16) for i in range(2)]; btq = [fresh(), fresh()]
            t1 = sb("t1", [128, 512]); t2 = sb("t2", [128, 512]); bt1 = fresh(); bt2 = fresh()
            Wkv = sbB("Wkv", [128, 8, 1536], BF16); bWkv = fresh()
            CW1 = [sbB("CW1k", [128, 16, 256], BF16), sbB("CW1v", [128, 16, 256], BF16)]
            CW2 = [sbB("CW2k", [128, 2, 64], BF16), sbB("CW2v", [128, 2, 64], BF16)]
            POS = [sbB("POSk", [128, 16], BF16), sbB("POSv", [128, 16], BF16)]
            bCW = fresh()
            Hpre = sbB("Hpre", [128, 16, 256]); bHp = fresh()
            HT = sbB("HT", [128, 16, 256], BF16); bHT = fresh()
            constv = sbB("constv", [128, 4]); bcv = fresh()
            ntB = [sbB("ntB%d" % i, [128, 8, 512], BF16) for i in range(2)]; bntB = [fresh(), fresh()]
            KCt = [sbB("KCt", [128, 4, 512], BF16), sbB("VCt", [128, 4, 512], BF16)]; bKCt = [fresh(), fresh()]
            KWst = sbB("KWst", [64, 4, 512], BF16); bKWst = fresh()
            VWst = sbB("VWst", [128, 4, 4, 128], BF16); bVWst = fresh()
            wkvv = w_in.rearrange("(c p) n -> p c n", p=128)
            dma("pool", Wkv[:, :, 0:1024], wkvv[:, :, 3072:4096], [], [bWkv], "w")
            dma("pool", Wkv[:, :, 1024:1280], wkvv[:, :, 4352:4608], [], [bWkv], "w")
            dma("pool", Wkv[:, :, 1280:1536], wkvv[:, :, 4096:4352], [], [bWkv], "w")
            dma("pool", CW1[0][:], cw1k, [], [bCW], "w"); dma("pool", CW1[1][:], cw1v, [], [bCW], "w")
            dma("pool", CW2[0][:], cw2k.rearrange("(c p) n -> p c n", p=128), [], [bCW], "w")
            dma("pool", CW2[1][:], cw2v.rearrange("(c p) n -> p c n", p=128), [], [bCW], "w")
            dma("pool", POS[0][:], posk, [], [bCW], "w"); dma("pool", POS[1][:], posv, [], [bCW], "w")
            P.emit("pool", lambda e: e.memset(Hpre[:], 0.0), [], [bHp])
            P.emit("pool", lambda e: e.memset(VWst[:, :, :, 64:128], 1.0), [], [bVWst])
            KB = int(os.environ.get("KB", "99"))
            def kb_stop(level):
                if KB == level:
                    P.wait_sems("sp", "dma_"); P.build(st); return True
                return False
            if kb_stop(0): return nc
            for s in range(8):
                nt_, bnt = ntB[s % 2], bntB[s % 2]
                dma("sp", nt_[:], N_s[s], [bN[s]], [bnt], "i")
                dma("sp", ctab[:], c_cos[:, s * 512:(s + 1) * 512], [], [btab], "i")
                dma("sp", stab[:], c_sin[:, s * 512:(s + 1) * 512], [], [btab], "i")
                if kb_stop(1): return nc
                for kv, col0 in ((0, 0), (1, 256)):
                    for cc in range(2):
                        pt_, bp = slot()
                        for k in range(8):
                            mm(pt_[:], Wkv[:, k, col0 + cc * 128:col0 + (cc + 1) * 128], nt_[:, k, :], k == 0, k == 7, [bWkv, bnt], [bp])
                        act(KCt[kv][0:64, 2 * cc, :], pt_[0:64, :], AF.Copy, [bp], [bKCt[kv]])
                        cp("dve", KCt[kv][0:64, 2 * cc + 1, :], pt_[64:128, :], [bp], [bKCt[kv]])
                        act(KCt[kv][64:128, 2 * cc, 0:511], pt_[0:64, 1:512], AF.Copy, [bp], [bKCt[kv]])
                        cp("dve", KCt[kv][64:128, 2 * cc + 1, 0:511], pt_[64:128, 1:512], [bp], [bKCt[kv]])
                if kb_stop(2): return nc
                for half in range(2):
                    pt_, bp = slot()
                    for mc in range(2):
                        for kv in range(2):
                            for kh in range(4):
                                gi = (mc * 2 + kv) * 4 + kh
                                for lp in range(8):
                                    rhs = bass.AP(KCt[kv], kh * 512 + 2 * lp, [[4 * 512, 128], [16, 32]])
                                    mm(pt_[:, gi * 32:(gi + 1) * 32], CW1[kv][:, half * 8 + lp, mc * 128:(mc + 1) * 128], rhs,
                                       lp == 0, lp == 7, [bCW, bKCt[kv]], [bp])
                    if half == 0:
                        o_ap = Hpre[:, :, 32 * s:32 * s + 32]
                        i_ap = bass.AP(pt_, 0, [[512, 128], [32, 16], [1, 32]])
                    elif s == 0:
                        o_ap = Hpre[:, :, 0:31]
                        i_ap = bass.AP(pt_, 1, [[512, 128], [32, 16], [1, 31]])
                    else:
                        o_ap = Hpre[:, :, 32 * s - 1:32 * s + 31]
                        i_ap = bass.AP(pt_, 0, [[512, 128], [32, 16], [1, 32]])
                    tt("dve", o_ap, o_ap, i_ap, ALU.add, [bp, bHp], [bHp])
                if kb_stop(3): return nc
                for which, col0 in ((0, 512), (1, 1280)):
                    for cc in range(2):
                        pt_, bp = slot()
                        for k in range(8):
                            mm(pt_[:], Wkv[:, k, col0 + cc * 128:col0 + (cc + 1) * 128], nt_[:, k, :], k == 0, k == 7, [bWkv, bnt], [bp])
                        i = (which * 2 + cc) % 2
                        act(tq[i][:], pt_[:], AF.Copy, [bp], [btq[i]])
                        p2, bp2 = slot()
                        mm(p2[:], rmat[:], tq[i][:], True, True, [bC, btq[i]], [bp2])
                        tt("dve", t1[:], p2[:], stab[:], ALU.mult, [bp2, btab], [bt1])
                        tt("pool", t2[:], tq[i][:], ctab[:], ALU.mult, [btq[i], btab], [bt2])
                        if which == 0:
                            tt("dve", KS[0:64, 2 * cc, s * 512:(s + 1) * 512], t1[0:64, :], t2[0:64, :], ALU.add, [bt1, bt2], [bKS])
                            tt("pool", KS[0:64, 2 * cc + 1, s * 512:(s + 1) * 512], t1[64:128, :], t2[64:128, :], ALU.add, [bt1, bt2], [bKS])
                        else:
                            tt("dve", KWst[0:64, 2 * cc, :], t1[0:64, :], t2[0:64, :], ALU.add, [bt1, bt2], [bKWst])
                            tt("pool", KWst[0:64, 2 * cc + 1, :], t1[64:128, :], t2[64:128, :], ALU.add, [bt1, bt2], [bKWst])
                dma("sp", KW_s[:, :, s * 512:(s + 1) * 512], KWst[:], [bKWst], [bKW[s]], "s")
                if kb_stop(4): return nc
                for tti in range(4):
                    pt_, bp = slot()
                    for k in range(8):
                        mm(pt_[:], nt_[:, k, tti * 128:(tti + 1) * 128], Wkv[:, k, 768:1280], k == 0, k == 7, [bWkv, bnt], [bp])
                    KV = os.environ.get("KV", "")
                    if "a" not in KV:
                        act(VS[:, 4 * s + tti, :, 0:64], bass.AP(pt_, 0, [[512, 128], [64, 4], [1, 64]]), AF.Copy, [bp], [bVS])
                    if "b" not in KV:
                        cp("dve", VWst[:, tti, :, 0:64], bass.AP(pt_, 256, [[512, 128], [64, 4], [1, 64]]), [bp], [bVWst])
                if "c" not in KV:
                    dma("sp", VW_s[:, 4 * s:4 * s + 4, :], VWst[:].rearrange("p a b c -> p a (b c)"), [bVWst], [bVW[s]], "s")
                if kb_stop(5): return nc
            if kb_stop(6): return nc
            for kv in range(2):
                for mc in range(2):
                    pt_, bp = slot()
                    for lp in range(16):
                        mm(pt_[:, 0:1], CW1[kv][:, lp, mc * 128:(mc + 1) * 128], POS[kv][:, lp:lp + 1], lp == 0, lp == 15, [bCW], [bp])
                    cp("dve", constv[:, mc * 2 + kv:mc * 2 + kv + 1], pt_[:, 0:1], [bp], [bcv])
            for kv in range(2):
                for mc in range(2):
                    g0 = (mc * 2 + kv) * 4
                    act(HT[:, g0:g0 + 4, :], Hpre[:, g0:g0 + 4, :], AF.Gelu_apprx_tanh, [bHp, bcv], [bHT], bias=constv[:, mc * 2 + kv:mc * 2 + kv + 1])
            for kh in range(4):
                pt_, bp = slot()
                for mc in range(2):
                    mm(pt_[0:64, 0:256], CW2[0][:, mc, :], HT[:, (mc * 2 + 0) * 4 + kh, :], mc == 0, mc == 1, [bCW, bHT], [bp])
                cp("dve", KCMP[0:64, kh, 0:255], pt_[0:64, 0:255], [bp], [bCMP])
                for nti in range(2):
                    pt_, bp = slot()
                    for mc in range(2):
                        mm(pt_[:, 0:64], HT[:, (mc * 2 + 1) * 4 + kh, nti * 128:(nti + 1) * 128], CW2[1][:, mc, :], mc == 0, mc == 1, [bCW, bHT], [bp])
                    cp("dve", VCMP[:, nti, kh, 0:64], pt_[:, 0:64], [bp], [bCMP])

        if KSTOP == "B":
            P.wait_sems("sp", "dma_"); P.build(st); return nc
        TC = 256
        arena["p"] = mark1
        new_phase()
        if True:
            sbC = sb
            WST = sbC("WST", [128, 8, 128], BF16); bsrow = sbC("bsrow", [1, 2, 1024], BF16)
            markc = arena["p"]
            wstf = sbC("wstf", [128, 8, 128]); tril = sbC("tril", [128, 128]); bsf = sbC("bsf", [1, 1024]); bsf2 = sbC("bsf2", [1, 1024])
            bCC = fresh()
            dma("sp", wstf[:], wsT, [], [bCC], "i"); dma("sp", tril[:], c_tril, [], [bCC], "i")
            dma("sp", bsf[:], b_s.rearrange("(o n) -> o n", o=1), [], [bCC], "i")
            tt("dve", WST[:], wstf[:], bass.AP(tril, 0, [[128, 128], [0, 8], [1, 128]]), ALU.mult, [bCC], [bCC])
            cp("dve", bsrow[0:1, 0, :], bsf[:], [bCC], [bCC])
            cp("dve", bsf2[:], bsrow[0:1, 0, :], [bCC], [bCC])
            tt("dve", bsrow[0:1, 1, :], bsf[:], bsf2[:], ALU.subtract, [bCC], [bCC])
            arena["p"] = markc
            new_phase()
            rebarrier([bCC])
            aaug = sbC("aaug", [128, 2 * 65], BF16)
            LNG = sbC("LNG", [128, 1024]); LNB = sbC("LNB", [128, 1024])
            Wg = sbC("Wg", [128, 8, 48], BF16)
            cmpbu = sbC("cmpbu", [128, 2, 256], BF16); keepu = sbC("keepu", [128, 128]); biasu = sbC("biasu", [128, 128]); validu = sbC("validu", [128, 128])
            bUT = fresh()
            tq = [sb("tqc%d" % i, [128, 256], BF16) for i in range(2)]; btq = [fresh(), fresh()]
            t1 = sb("t1c", [128, 256]); t2 = sb("t2c", [128, 256]); bt1 = fresh(); bt2 = fresh()
            dma("pool", aaug[:], c_aaug, [], [bCC], "c")
            dma("sp", LNG[:], ln_g.partition_broadcast(128), [], [bCC], "i"); dma("sp", LNB[:], ln_b.partition_broadcast(128), [], [bCC], "i")
            dma("pool", Wg[:], w_in.rearrange("(c p) n -> p c n", p=128)[:, :, 4608:4656], [], [bCC], "c")
            P.emit("dve", lambda e: e.memset(tinyb[:], 1e-30), [], [bCC])
            ntC = [sbC("ntC%d" % i, [128, 8, TC], BF16) for i in range(2)]; bntC = [fresh(), fresh()]
            U = sbC("U", [128, 8, TC], BF16); bU = fresh()
            vt = [sbC("vt0", [128, 1024])] * 2; bvt = [fresh()] * 2
            vn = [sbC("vn%d" % i, [128, 1024], BF16) for i in range(2)]; bvn = [fresh(), fresh()]
            stats = sbC("stats", [128, 2, 6]); mv = sbC("mv", [128, 2]); rs = sbC("rs", [128, 1]); bst = fresh()
            gtmp = sbC("gtmp", [128, TC]); bgtmp = fresh()
            gt = [sbC("gt%d" % i, [128, TC], BF16) for i in range(2)]; bgt = [fresh(), fresh()]
            M = sbC("M", [128, 8, TC], BF16); bM = fresh()
            SGq = sbC("SGq", [128, 2, 48]); bSG = fresh()
            QRAW = [sbC("QRAW%d" % i, [64, 4, TC], BF16) for i in range(2)]; bQRAW = [fresh(), fresh()]
            QR = [sbC("QR%d" % i, [128, 4, TC], BF16) for i in range(2)]; bQR = [fresh(), fresh()]
            PT = [sbC("PT%d" % i, [128, 512], BF16) for i in range(4)]; bPT = [fresh() for _ in range(4)]
            OB = sbC("OB", [128, 8, TC], BF16); bOB = fresh()
            KWw = sbC("KWw", [64, 4, 1024], BF16); VWw = sbC("VWw", [128, 8, 4, 128], BF16); bWw = fresh()
            Wt = [sbC("Wt%d" % i, [128, 8, 512], BF16) for i in range(3)]; bWt = [fresh() for _ in range(3)]
            HCb = [sbC("HCb%d" % i, [128, 8, TC]) for i in range(2)]; bHCb = [fresh(), fresh()]
            ctC = sbC("ctC", [128, TC]); stC_ = sbC("stC_", [128, TC]); btabC = fresh()
            rzq = sbC("rzq", [128, 24]); cq = sbC("cq", [128, 24]); bcq = fresh()
            accq = [sbC("accq%d" % i, [128, 256]) for i in range(2)]; tqa = sbC("tqa", [128, 256]); tqb = sbC("tqb", [128, 256])
            obq = [sbC("obq%d" % i, [128, 256], BF16) for i in range(2)]
            baccq = [fresh(), fresh()]; btqa = fresh(); btqb = fresh(); bobq = [fresh(), fresh()]
            rz4 = sbC("rz4", [128, 4]); imp = sbC("imp", [128, 64]); score = sbC("score", [128, 64]); wk = sbC("wk", [128, 64])
            m8 = sbC("m8", [128, 8]); m8b = sbC("m8b", [128, 8]); thr = sbC("thr", [128, 1]); MB = sbC("MB", [128, 128], BF16)
            MBq = [MB, sbC("MB1", [128, 128], BF16)]
            bsm = fresh(); bMBq = [fresh(), fresh()]
            print("SBUF phase C end", arena["p"])
            P.emit("pool", lambda e: e.memset(MBq[0][:], 0.0), [], [bMBq[0]])
            P.emit("pool", lambda e: e.memset(MBq[1][:], 0.0), [], [bMBq[1]])
            wcnt = {"i": 0, "pt": 0, "gt": 0, "hc": 0}
            winv = w_in.rearrange("(c p) n -> p c n", p=128)

            WORDER = [2, 3, 0, 1] + list(range(4, 18))
            wq = []
            wfree = [0, 1, 2]
            wstate = {"next": 0}

            def wfill():
                while wfree and wstate["next"] < 8 * 18:
                    ti = WORDER[wstate["next"] % 18]; wstate["next"] += 1
                    i = wfree.pop(0)
                    ncols = 256 if 8 <= ti < 12 else 512
                    dma("sp", Wt[i][:, :, 0:ncols], WS[ti][:, :, 0:ncols], [bWS[ti]], [bWt[i]], "wt")
                    wq.append((ti, i))

            def wload(ti, ncols):
                if not wq:
                    wfill()
                t_, i = wq.pop(0)
                assert t_ == ti, (t_, ti)
                return Wt[i], (bWt[i], i)

            def wdone(bw):
                wfree.append(bw[1])
                wfill()

            def w1024(mat):
                return mat.rearrange("(c p) n -> p c n", p=128)

            def unit_loads(u):
                o, hf = u // 2, u % 2
                sl = 2 * o + 1
                pos0 = sl * 512 + hf * TC
                dma("sp", ntC[u % 2][:], N_s[sl][:, :, hf * TC:(hf + 1) * TC], [bN[sl]], [bntC[u % 2]], "i")
                dma("sp", ctC[:], c_cos[:, pos0:pos0 + TC], [], [btabC], "i")
                dma("sp", stC_[:], c_sin[:, pos0:pos0 + TC], [], [btabC], "i")
                if hf == 0:
                    dma("sp", KWw[:], KW_s[:, :, (2 * o) * 512:(2 * o + 2) * 512], [bKW[2 * o], bKW[2 * o + 1]], [bWw], "i")
                    dma("sp", VWw[:].rearrange("p a b c -> p a (b c)"), VW_s[:, 8 * o:8 * o + 8, :], [bVW[2 * o], bVW[2 * o + 1]], [bWw], "i")
                dma("pool", cmpbu[:], c_cmpb.rearrange("p (n t) -> p n t", n=2)[:, :, u * 256:(u + 1) * 256], [], [bUT], "c")
                dma("sp", HCb[u % 2][:], H_s[o][:, :, hf * TC:(hf + 1) * TC], [bH[o][hf]], [bHCb[u % 2]], "i")
                dma("sp", keepu[:], c_keep[:, 2 * u * 64:(2 * u + 2) * 64], [], [bUT], "i")
                dma("sp", biasu[:], c_bias[:, 2 * u * 64:(2 * u + 2) * 64], [], [bUT], "i")
                dma("sp", validu[:], c_valid[:, 2 * u * 64:(2 * u + 2) * 64], [], [bUT], "i")

            unit_loads(0)
            for u in range(8):
                o, hf = u // 2, u % 2
                sl = 2 * o + 1
                nt_, bnt = ntC[u % 2], bntC[u % 2]
                ring_state["n"] = 6
                wv0, bwv0x = wload(2, 512)
                wv1, bwv1x = wload(3, 512)
                for tti in range(2):
                    for half, (wt, bw) in enumerate(((wv0, bwv0x[0]), (wv1, bwv1x[0]))):
                        pt_, bp = slot()
                        for k in range(8):
                            mm(pt_[:], nt_[:, k, tti * 128:(tti + 1) * 128], wt[:, k, :], k == 0, k == 7, [bw, bnt], [bp])
                        act(vt[tti][:, half * 512:(half + 1) * 512], pt_[:], AF.Gelu_apprx_tanh, [bp], [bvt[tti]])
                    for hh in range(2):
                        P.emit("dve", lambda e, tti=tti, hh=hh: e.bn_stats(out=stats[:, hh, :], in_=vt[tti][:, hh * 512:(hh + 1) * 512]), [bvt[tti]], [bst])
                    P.emit("dve", lambda e: e.bn_aggr(out=mv[:], in_=stats[:]), [bst], [bst])
                    act(rs[:], mv[:, 1:2], AF.Sqrt, [bst, bC], [bst], bias=EPS)
                    P.emit("dve", lambda e: e.reciprocal(out=rs[:], in_=rs[:]), [bst], [bst])
                    tsc("dve", vt[tti][:], vt[tti][:], mv[:, 0:1], rs[:, 0:1], ALU.subtract, ALU.mult, [bvt[tti], bst], [bvt[tti]])
                    tt("pool", vt[tti][:], vt[tti][:], LNG[:], ALU.mult, [bvt[tti], bCC], [bvt[tti]])
                    tt("dve", vn[tti][:], vt[tti][:], LNB[:], ALU.add, [bvt[tti], bCC], [bvn[tti]])
                wdone(bwv0x); wdone(bwv1x)
                for half in range(2):
                    wt, bwx = wload(half, 512)
                    bw = bwx[0]
                    for c4 in range(4):
                        c = half * 4 + c4
                        pt_, bp = slot()
                        for k in range(8):
                            mm(pt_[:, 0:TC], wt[:, k, c4 * 128:(c4 + 1) * 128], nt_[:, k, :], k == 0, k == 7, [bw, bnt], [bp])
                        act(U[:, c, :], pt_[:, 0:TC], AF.Gelu_apprx_tanh, [bp], [bU])
                    wdone(bwx)
                for g in range(8):
                    pt_, bp = slot()
                    for tti in range(2):
                        mm(pt_[:, tti * 128:(tti + 1) * 128], vn[tti][:, g * 128:(g + 1) * 128], WST[:, g, :], True, False, [bvn[tti], bCC], [bp])
                        mm(pt_[:, tti * 128:(tti + 1) * 128], onesb[0:1, :], bsrow[0:1, 0, g * 128:(g + 1) * 128], False, False, [bC, bCC], [bp])
                        mm(pt_[:, tti * 128:(tti + 1) * 128], onesb[0:1, :], bsrow[0:1, 1, g * 128:(g + 1) * 128], False, True, [bC, bCC], [bp])
                    tt("dve", U[:, g, :], pt_[:, 0:TC], U[:, g, :], ALU.mult, [bp, bU], [bU])
                for half in range(2):
                    wtg, bwgx = wload(4 + 2 * half, 512)
                    wta, bwax = wload(5 + 2 * half, 512)
                    bwg, bwa = bwgx[0], bwax[0]
                    for c4 in range(4):
                        c = half * 4 + c4
                        pg, bpg = slot()
                        for k in range(8):
                            mm(pg[:, 0:TC], wtg[:, k, c4 * 128:(c4 + 1) * 128], nt_[:, k, :], k == 0, k == 7, [bwg, bnt], [bpg])
                        i = wcnt["gt"] % 2; wcnt["gt"] += 1
                        act(gt[i][:], pg[:, 0:TC], AF.Sigmoid, [bpg], [bgt[i]])
                        py, bpy = slot()
                        for k in range(8):
                            mm(py[:, 0:TC], wta[:, k, c4 * 128:(c4 + 1) * 128], U[:, k, :], k == 0, k == 7, [bwa, bU], [bpy])
                        tt("dve", M[:, c, :], py[:, 0:TC], gt[i][:], ALU.mult, [bpy, bgt[i]], [bM])
                    wdone(bwgx); wdone(bwax)
                for qq in range(2):
                    pt_, bp = slot()
                    for k in range(8):
                        mm(pt_[:, 0:48], nt_[:, k, qq * 128:(qq + 1) * 128], Wg[:, k, :], k == 0, k == 7, [bCC, bnt], [bp])
                    act(SGq[:, qq, :], pt_[:, 0:48], AF.Sigmoid, [bp], [bSG])
                ring_state["n"] = 3

                def q_proj(kh):
                    qraw, bqraw = QRAW[kh % 2], bQRAW[kh % 2]
                    qr, bqr = QR[kh % 2], bQR[kh % 2]
                    wq_, bwqx = wload(8 + kh, 256)
                    bwq = bwqx[0]
                    for cc in range(2):
                        pt_, bp = slot()
                        for k in range(8):
                            mm(pt_[:, 0:TC], wq_[:, k, cc * 128:(cc + 1) * 128], nt_[:, k, :], k == 0, k == 7, [bwq, bnt], [bp])
                        i = cc
                        act(tq[i][:, 0:TC], pt_[:, 0:TC], AF.Copy, [bp], [btq[i]])
                        cp("pool", qraw[0:64, 2 * cc, :], tq[i][0:64, 0:TC], [btq[i]], [bqraw])
                        cp("pool", qraw[0:64, 2 * cc + 1, :], tq[i][64:128, 0:TC], [btq[i]], [bqraw])
                        p2, bp2 = slot()
                        mm(p2[:, 0:TC], rmat[:], tq[i][:, 0:TC], True, True, [bC, btq[i]], [bp2])
                        tt("dve", t1[:, 0:TC], p2[:, 0:TC], stC_[:], ALU.mult, [bp2, btabC], [bt1])
                        tt("pool", t2[:, 0:TC], tq[i][:, 0:TC], ctC[:], ALU.mult, [btq[i], btabC], [bt2])
                        tt("dve", qr[0:64, 2 * cc, :], t1[0:64, 0:TC], t2[0:64, 0:TC], ALU.add, [bt1, bt2], [bqr])
                        tt("pool", qr[0:64, 2 * cc + 1, :], t1[64:128, 0:TC], t2[64:128, 0:TC], ALU.add, [bt1, bt2], [bqr])
                    wdone(bwqx)

                def attention(kh):
                    wfill()
                    qraw, bqraw = QRAW[kh % 2], bQRAW[kh % 2]
                    qr, bqr = QR[kh % 2], bQR[kh % 2]

                    def qap(t, npart, qs):
                        return bass.AP(t, qs, [[4 * TC, npart], [TC, 4], [1, 128]])

                    tiles = []
                    for qi_ in range(2):
                        qs = qi_ * 128
                        for nti in range(2):
                            tiles.append(dict(kind="cmp", q=qi_, nti=nti, lhsT=KCMP[0:64, kh, nti * 128:(nti + 1) * 128], rhs=qap(qraw, 64, qs),
                                              rr=[bCMP, bqraw], btab=cmpbu, boff=nti * 256 + qi_ * 128, v=VCMP[:, nti, kh, 0:65], vr=[bCMP],
                                              bank=3, br=0, first=nti == 0, last=nti == 1))
                    for qi_ in range(2):
                        qs = qi_ * 128
                        for off in range(5):
                            lt = hf * 2 + qi_ + off
                            if o == 0 and lt < 4:
                                btab_ = dfirst if off == 0 else dmid
                            elif off == 0:
                                btab_ = tfirst
                            elif off == 4:
                                btab_ = tdiag
                            else:
                                btab_ = None
                            tiles.append(dict(kind="win", q=qi_, lhsT=KWw[0:64, kh, lt * 128:(lt + 1) * 128], rhs=qap(qr, 64, qs), rr=[bWw, bqr],
                                              btab=btab_, boff=0, v=VWw[:, lt, kh, 0:65], vr=[bWw], bank=5, br=2, first=off == 0, last=off == 4))
                    for qi_ in range(2):
                        qs = qi_ * 128
                        QT = sl * 4 + hf * 2 + qi_
                        for kt in range(QT + 1):
                            tiles.append(dict(kind="sel", q=qi_, kt=kt, lhsT=KS[:, kh, kt * 128:(kt + 1) * 128], rhs=qap(qr, 128, qs), rr=[bKS, bqr],
                                              btab=tdiag if kt == QT else None, boff=0, v=VS[:, kt, kh, 0:65], vr=[bVS], bank=4, br=1,
                                              first=kt == 0, last=kt == QT))

                    def emit_S(t):
                        pS, bpS = slot()
                        mm(pS[:], t["lhsT"], t["rhs"], True, t["btab"] is None, t["rr"], [bpS])
                        if t["btab"] is not None:
                            ncol = 1
                            for d_ in t["btab"].shape[1:]:
                                ncol *= d_
                            mm(pS[:], ident[:], bass.AP(t["btab"], t["boff"], [[ncol, 128], [0, 4], [1, 128]]), False, True, [bC, bCC, bUT], [bpS])
                        i = wcnt["pt"] % 4; wcnt["pt"] += 1
                        act(PT[i][:], pS[:], AF.Exp, [bpS], [bPT[i]], scale=0.125)
                        t["pt"] = i

                    def emit_mask_dve(qi_):
                        tsc("dve", rz4[:], bass.AP(ps[6], 64, [[512, 128], [65, 4]]), 1e-30, None, ALU.max, None, [bps[6], bCC], [bsm])
                        P.emit("dve", lambda e: e.reciprocal(out=rz4[:], in_=rz4[:]), [bsm], [bsm])
                        tsc("dve", imp[:], ps[6][:, 0:64], rz4[:, 0:1], None, ALU.mult, None, [bps[6], bsm], [bsm])
                        for g in range(1, 4):
                            stt("dve", imp[:], ps[6][:, g * 65:g * 65 + 64], rz4[:, g:g + 1], imp[:], ALU.mult, ALU.add, [bps[6], bsm], [bsm])
                        tt("dve", score[:], imp[:], keepu[:, qi_ * 64:(qi_ + 1) * 64], ALU.mult, [bsm, bUT], [bsm])
                        tt("dve", score[:], score[:], biasu[:, qi_ * 64:(qi_ + 1) * 64], ALU.add, [bsm, bUT], [bsm])
                        P.emit("dve", lambda e: e.max(out=m8[:], in_=score[:]), [bsm], [bsm])
                        P.emit("dve", lambda e: e.match_replace(out=wk[:], in_to_replace=m8[:], in_values=score[:], imm_value=-3e9), [bsm], [bsm])
                        P.emit("dve", lambda e: e.max(out=m8b[:], in_=wk[:]), [bsm], [bsm])
                        P.emit("dve", lambda e: e.tensor_reduce(out=thr[:], in_=m8b[:], axis=AX.X, op=ALU.min), [bsm], [bsm])
                        stt("dve", wk[:], score[:], thr[:, 0:1], validu[:, qi_ * 64:(qi_ + 1) * 64], ALU.is_ge, ALU.mult, [bsm, bUT], [bsm])
                        tsc("dve", MBq[qi_][:, 64:128], wk[:], 1.0, -NEG, ALU.subtract, ALU.mult, [bsm], [bMBq[qi_]])

                    def emit_mask_pe(qi_):
                        qs = qi_ * 128
                        P.emit("pe", lambda e: e.transpose(out=psT[:, qi_ * 128:(qi_ + 1) * 128], in_=MBq[qi_][:], identity=ident[:]), [bMBq[qi_], bC], [bpsT])
                        cp("dve", bass.AP(qr, 64 * 4 * TC + qs, [[4 * TC, 64], [TC, 4], [1, 128]]),
                           bass.AP(psT, 64 * 1024 + qi_ * 128, [[1024, 64], [0, 4], [1, 128]]), [bpsT], [bqr])

                    pending = []

                    def emit_combine(qi_, br, bank):
                        rz_ = rzq[:, qi_ * 12 + br * 4:qi_ * 12 + br * 4 + 4]
                        c_ = cq[:, qi_ * 12 + br * 4:qi_ * 12 + br * 4 + 4]
                        tsc("dve", rz_, bass.AP(ps[bank], 64, [[512, 128], [65, 4]]), 1e-30, None, ALU.max, None, [bps[bank]], [bcq])
                        P.emit("dve", lambda e: e.reciprocal(out=rz_, in_=rz_), [bcq], [bcq])
                        tt("dve", c_, rz_, bass.AP(SGq, qi_ * 48 + 12 * kh + br, [[96, 128], [3, 4]]), ALU.mult, [bcq, bSG], [bcq])
                        ov = bass.AP(ps[bank], 0, [[512, 128], [65, 4], [1, 64]])
                        cb = bass.AP(cq, qi_ * 12 + br * 4, [[24, 128], [1, 4], [0, 64]])

                        def v3(t):
                            return bass.AP(t, 0, [[256, 128], [64, 4], [1, 64]])
                        if br == 0:
                            tt("dve", v3(accq[qi_]), ov, cb, ALU.mult, [bps[bank], bcq], [baccq[qi_]])
                        elif br == 2:
                            tt("dve", v3(tqa), ov, cb, ALU.mult, [bps[bank], bcq], [btqa])
                            tt("pool", accq[qi_][:], accq[qi_][:], tqa[:], ALU.add, [baccq[qi_], btqa], [baccq[qi_]])
                        else:
                            tt("dve", v3(tqb), ov, cb, ALU.mult, [bps[bank], bcq], [btqb])
                            tt("pool", obq[qi_][:], accq[qi_][:], tqb[:], ALU.add, [baccq[qi_], btqb], [bobq[qi_]])
                            pending.append(qi_)

                    def emit_writeback(qi_):
                        qs = qi_ * 128
                        for hp in range(2):
                            P.emit("pe", lambda e, hp=hp: e.transpose(out=psT[:, 256 + qi_ * 256 + hp * 128:256 + qi_ * 256 + (hp + 1) * 128],
                                                                    in_=obq[qi_][:, hp * 128:(hp + 1) * 128], identity=ident[:]), [bobq[qi_], bC], [bpsT])
                        cp("dve", bass.AP(OB, (2 * kh) * TC + qs, [[8 * TC, 128], [TC, 2], [1, 128]]),
                           bass.AP(psT, 256 + qi_ * 256, [[1024, 128], [128, 2], [1, 128]]), [bpsT], [bOB])

                    def emit_PV(t):
                        i = t["pt"]
                        for g in range(4):
                            mm(ps[t["bank"]][:, g * 65:(g + 1) * 65], PT[i][:, g * 128:(g + 1) * 128], t["v"],
                               t["first"] and g == 0, t["last"], t["vr"] + [bPT[i]], [bps[t["bank"]]], sgc=True)
                        if t["kind"] == "cmp":
                            nti = t["nti"]
                            for g in range(4):
                                mm(ps[6][:, g * 65:(g + 1) * 65], PT[i][:, g * 128:(g + 1) * 128], aaug[:, nti * 65:(nti + 1) * 65],
                                   nti == 0 and g == 0, nti == 1, [bPT[i], bCC], [bps[6]], sgc=True)
                            if nti == 1:
                                emit_mask_dve(t["q"])
                        if t["last"]:
                            emit_combine(t["q"], t["br"], t["bank"])

                    LA = 2
                    age = 0
                    for idx in range(len(tiles) + LA):
                        if idx < len(tiles):
                            if tiles[idx]["kind"] == "sel" and tiles[idx]["kt"] == 0:
                                emit_mask_pe(tiles[idx]["q"])
                            emit_S(tiles[idx])
                        if idx >= LA:
                            emit_PV(tiles[idx - LA])
                        if pending:
                            age += 1
                            if age >= 4:
                                emit_writeback(pending.pop(0)); age = 0
                    while pending:
                        emit_writeback(pending.pop(0))

                q_proj(0)
                for kh in range(4):
                    if kh + 1 < 4:
                        q_proj(kh + 1)
                    attention(kh)
                if u + 1 < 8:
                    unit_loads(u + 1)
                ring_state["n"] = 6
                for half in range(2):
                    wtg, bwgx = wload(12 + 2 * half, 512)
                    wtb, bwbx = wload(13 + 2 * half, 512)
                    bwg, bwb = bwgx[0], bwbx[0]
                    for c4 in range(4):
                        c = half * 4 + c4
                        pg, bpg = slot()
                        for k in range(8):
                            mm(pg[:, 0:TC], wtg[:, k, c4 * 128:(c4 + 1) * 128], nt_[:, k, :], k == 0, k == 7, [bwg, bnt], [bpg])
                        i = wcnt["gt"] % 2; wcnt["gt"] += 1
                        act(gt[i][:], pg[:, 0:TC], AF.Sigmoid, [bpg], [bgt[i]])
                        py, bpy = slot()
                        for k in range(8):
                            mm(py[:, 0:TC], wtb[:, k, c4 * 128:(c4 + 1) * 128], OB[:, k, :], k == 0, k == 7, [bwb, bOB], [bpy])
                        tt("dve", gtmp[:], py[:, 0:TC], gt[i][:], ALU.mult, [bpy, bgt[i]], [bgtmp])
                        tt("pool", M[:, c, :], M[:, c, :], gtmp[:], ALU.add, [bM, bgtmp], [bM])
                    wdone(bwgx); wdone(bwbx)
                for half in range(2):
                    wto, bwox = wload(16 + half, 512)
                    bwo = bwox[0]
                    for c4 in range(4):
                        c = half * 4 + c4
                        pt_, bp = slot()
                        for k in range(8):
                            mm(pt_[:, 0:TC], wto[:, k, c4 * 128:(c4 + 1) * 128], M[:, k, :], k == 0, k == 7, [bwo, bM], [bp])
                        tt("dve", HCb[u % 2][:, c, :], pt_[:, 0:TC], HCb[u % 2][:, c, :], ALU.add, [bp, bHCb[u % 2]], [bHCb[u % 2]])
                    wdone(bwox)
                dma("sp", H_s[o][:, :, hf * TC:(hf + 1) * TC], HCb[u % 2][:], [bHCb[u % 2]], [bH[o][hf]], "s")

        if KSTOP == "C":
            P.wait_sems("sp", "dma_"); P.build(st); return nc
        arena["p"] = xr1_off
        new_phase()
        rebarrier(ffn_bufs)
        load_ffn_scratch()
        if True:
            Wpp = sb("Wpp", [128, 2, 1024], BF16); bWp = fresh()
            Wpgt = [sb("Wpgt%d" % i, [128, 8, 128], BF16) for i in range(3)]; bWpgt = [fresh() for _ in range(3)]
            pTb = sb("pTb", [128, 2, TB], BF16); bpTb = fresh()
            print("SBUF phase D end", arena["p"])
            dma("pool", Wpp[:], w_pp.rearrange("(c p) n -> p c n", p=128), [], [bWp], "w")
            pv = pT.rearrange("(c p) t -> p c t", p=128)
            wpg_n = {"i": 0}
            dma("sp", xr[0][:], H_s[0], bH[0], [bxr[0]], "i")
            for blk in range(4):
                o = blk
                xt, bx = xr[0], bxr[0]
                dma("pool", pTb[:], pv[:, :, blk * TB:(blk + 1) * TB], [], [bpTb], "c")
                ffn(xt, bx, G2, TB)
                n3 = nb[0]; bn3 = bnb[0]
                rmsnorm(xt, bx, GP, n3, bn3, TB)
                for c in range(8):
                    wi = wpg_n["i"] % 3; wpg_n["i"] += 1
                    dma("sp", Wpgt[wi][:], WPGs[c], [bWPGs[c]], [bWpgt[wi]], "wt")
                    pg, bpg = slot()
                    for k in range(8):
                        mm(pg[:, 0:TB], Wpgt[wi][:, k, :], n3[:, k, :], k == 0, k == 7, [bWpgt[wi], bn3], [bpg])
                    pp, bpp = slot()
                    for k in range(2):
                        mm(pp[:, 0:TB], Wpp[:, k, c * 128:(c + 1) * 128], pTb[:, k, :], k == 0, k == 1, [bWp, bpTb], [bpp])
                    i = cnt["sg"] % 3; cnt["sg"] += 1
                    act(sgt[i][:], pg[:, 0:TB], AF.Sigmoid, [bpg], [bsgt[i]])
                    tt("dve", sgt[i][:], sgt[i][:], pp[:, 0:TB], ALU.mult, [bsgt[i], bpp], [bsgt[i]])
                    tt("pool", xt[:, c, :], xt[:, c, :], sgt[i][:], ALU.add, [bx, bsgt[i]], [bx])
                tt("dve", hid[:, 0:8, :], xt[:], xt[:], ALU.mult, [bx], bhid[0:8])
                pt_, bp = slot()
                for c in range(8):
                    mm(pt_[:, 0:TB], onesb[:], hid[:, c, :], c == 0, c == 7, bhid[0:8] + [bC], [bp])
                act(rstd[:], pt_[:, 0:TB], AF.Sqrt, [bp, bC], [brstd], scale=1.0 / 1024, bias=EPS)
                P.emit("dve", lambda e: e.reciprocal(out=rstd[:], in_=rstd[:]), [brstd], [brstd])
                for c in range(8):
                    if c < 4:
                        stt("dve", osa[:, c, :], xt[:, c, :], vec[:, GF + c:GF + c + 1], rstd[:], ALU.mult, ALU.mult,
                            [bx, brstd, bC], [bnb[0]])
                    else:
                        stt("dve", osb[:, c - 4, :], xt[:, c, :], vec[:, GF + c:GF + c + 1], rstd[:], ALU.mult, ALU.mult,
                            [bx, brstd, bC], bhid[8:16])
                if blk + 1 < 4:
                    dma("sp", xr[0][:], H_s[blk + 1], bH[blk + 1], [bxr[0]], "i")
                ov_ = outT.rearrange("(c p) t -> p c t", p=128)
                dma("sp", ov_[:, 0:4, blk * TB:(blk + 1) * TB], osa[:], [bnb[0]], [], "o")
                dma("sp", ov_[:, 4:8, blk * TB:(blk + 1) * TB], osb[:], bhid[8:16], [], "o")
        P.wait_sems("sp", "dma_")
        P.build(st)
    return nc


def _host_constants(r):
    shift = 512 if r == 0 else 0
    c = {}
    c["c_ident"] = np.eye(128, dtype=np.float32)
    R = np.zeros((128, 128), np.float32)
    for hb in (0, 64):
        for m in range(8):
            R[hb + m + 8, hb + m] = -1.0
            R[hb + m, hb + m + 8] = 1.0
    c["c_rmat"] = R
    sel = np.zeros((48, 48, 64), np.float32)
    for r_ in range(48):
        sel[r_, r_, :] = 1.0
    c["c_sel"] = sel.reshape(48, 48 * 64)
    A = np.zeros((256, 65), np.float32)
    for j in range(64):
        for s in range(5):
            i = 4 * j + s - 1
            if 0 <= i < 256:
                A[i, j] = 1.0
    A[:, 64] = 1.0
    c["c_aaug"] = A.reshape(2, 128, 65).transpose(1, 0, 2).reshape(128, 130).copy()
    E = np.zeros((64, 4096), np.float32)
    E[np.arange(4096) // 64, np.arange(4096)] = 1.0
    c["c_E"] = E
    ii = np.arange(128)[:, None]; jj = np.arange(128)[None, :]
    tfirst = np.where(ii > jj, 0.0, NEG).astype(np.float32)
    tdiag = np.where(ii <= jj, 0.0, NEG).astype(np.float32)
    c["c_tfirst"] = tfirst; c["c_tdiag"] = tdiag
    c["c_dfirst"] = np.full((128, 128), NEG, np.float32) if r == 0 else tfirst
    c["c_dmid"] = np.full((128, 128), NEG, np.float32) if r == 0 else np.zeros((128, 128), np.float32)
    c["c_tril"] = (ii <= jj).astype(np.float32)
    own_pos = np.concatenate([np.arange((2 * o + 1) * 512, (2 * o + 2) * 512) for o in range(4)])
    n_idx = np.arange(256)[:, None]
    ok = (16 * n_idx + 31 <= own_pos[None, :]) & (n_idx >= shift // 16) & (n_idx <= 254)
    cmpb = np.where(ok, 0.0, NEG).astype(np.float32)
    c["c_cmpb"] = cmpb.reshape(2, 128, 2048).transpose(1, 0, 2).reshape(128, 4096).copy()
    t_real = own_pos - shift
    cur = t_real // 64
    jp = np.arange(64)[None, :]
    j_real = jp - shift // 64
    dummy = j_real < 0
    forced = ((j_real == 0) | (j_real == cur[:, None]) | (j_real == cur[:, None] - 1)) & ~dummy
    future = j_real > cur[:, None]
    keep = (~(forced | future | dummy)).astype(np.float32)
    bias = np.where(forced, 1e9, np.where(future | dummy, -1e9, 0.0)).astype(np.float32)
    valid = (~(dummy | future)).astype(np.float32)

    def qlay(a):
        return a.reshape(16, 128, 64).transpose(1, 0, 2).reshape(128, 1024).copy()
    c["c_keep"] = qlay(keep); c["c_bias"] = qlay(bias); c["c_valid"] = qlay(valid)
    pos = (np.arange(4096) - shift).astype(np.float32)
    inv_freq = (500000.0 ** (-np.arange(0, 16, 2, dtype=np.float32) / 16)).astype(np.float32)
    ang = pos[None, :] * inv_freq[:, None]
    C = np.ones((64, 4096), np.float32); S = np.zeros((64, 4096), np.float32)
    C[0:8] = np.cos(ang); C[8:16] = np.cos(ang); S[0:8] = np.sin(ang); S[8:16] = np.sin(ang)
    c["c_cos"] = np.concatenate([C, C], 0); c["c_sin"] = np.concatenate([S, S], 0)
    return c


_NC_CACHE = {}


def kernel(x, p, ffn1_norm, ffn1_w_in, ffn1_w_out, mix_norm, w_in, gm_ln_g, gm_ln_b, gm_w_s, gm_b_s,
           w_branch_a, cmp_pos_k, cmp_k_w1, cmp_k_w2, cmp_pos_v, cmp_v_w1, cmp_v_w2, w_branch_b, w_out,
           ffn2_norm, ffn2_w_in, ffn2_w_out, ple_norm, ple_w_gate, ple_w_proj, final_norm):
    f = lambda a: np.ascontiguousarray(np.asarray(a, dtype=np.float32))
    x = f(x); p = f(p)
    if "nc" not in _NC_CACHE:
        _NC_CACHE["nc"] = build_program()
    nc = _NC_CACHE["nc"]

    def pcol(v):
        return f(v).reshape(8, 128).T
    vecs = np.ascontiguousarray(np.concatenate([pcol(ffn1_norm[0]), pcol(mix_norm[0]), pcol(ffn2_norm[0]),
                                                pcol(ple_norm[0]), pcol(final_norm)], axis=1))
    shared = {
        "f1_win": f(ffn1_w_in[0]), "f1_wout": f(ffn1_w_out[0]), "f2_win": f(ffn2_w_in[0]), "f2_wout": f(ffn2_w_out[0]),
        "w_in": f(w_in[0]), "vecs": vecs, "ln_g": f(gm_ln_g[0]), "ln_b": f(gm_ln_b[0]),
        "wsT": f(np.transpose(np.asarray(gm_w_s[0]), (2, 0, 1))),
        "b_s": f(np.asarray(gm_b_s[0]).reshape(1024)),
        "w_a": f(w_branch_a[0]), "w_b": f(w_branch_b[0]), "w_o": f(w_out[0]),
        "posk": f(np.asarray(cmp_pos_k[0]).reshape(16, 2, 64).transpose(1, 2, 0).reshape(128, 16)),
        "posv": f(np.asarray(cmp_pos_v[0]).reshape(16, 2, 64).transpose(1, 2, 0).reshape(128, 16)),
        "cw1k": f(np.asarray(cmp_k_w1[0]).reshape(16, 2, 64, 256).transpose(1, 2, 0, 3).reshape(128, 16, 256)),
        "cw1v": f(np.asarray(cmp_v_w1[0]).reshape(16, 2, 64, 256).transpose(1, 2, 0, 3).reshape(128, 16, 256)),
        "cw2k": f(cmp_k_w2[0]), "cw2v": f(cmp_v_w2[0]),
        "w_pg": f(ple_w_gate[0]), "w_pp": f(ple_w_proj[0]),
    }
    consts = [_host_constants(0), _host_constants(1)]
    in_maps = []
    for c in range(8):
        b, r = c // 2, c % 2
        xb = x[b]
        if r == 0:
            xs = np.concatenate([np.zeros((512, 1024), np.float32), xb[:3584]], 0)
            own = [0, 2, 4, 6]
        else:
            xs = xb
            own = [1, 3, 5, 7]
        pb = np.concatenate([p[0, b, o * 512:(o + 1) * 512] for o in own], 0)
        m = dict(shared)
        m.update(consts[r])
        m["xT"] = np.ascontiguousarray(xs.T)
        m["pT"] = np.ascontiguousarray(pb.T)
        in_maps.append(m)
    res = run_bass_kernel_spmd(nc, in_maps, core_ids=list(range(8)))
    out = np.empty((4, 4096, 1024), np.float32)
    for c in range(8):
        b, r = c // 2, c % 2
        oT = res.results[c]["outT"]
        own = [0, 2, 4, 6] if r == 0 else [1, 3, 5, 7]
        for i, o in enumerate(own):
            out[b, o * 512:(o + 1) * 512] = oT[:, i * 512:(i + 1) * 512].T
    return out
```

```python
from collections import defaultdict
from contextlib import ExitStack
import os
import numpy as np
import concourse.bass as bass
import concourse.mybir as mybir
from concourse.bass_utils import run_bass_kernel_spmd

F32 = mybir.dt.float32
BF16 = mybir.dt.bfloat16
ALU = mybir.AluOpType
AF = mybir.ActivationFunctionType
AX = mybir.AxisListType
NEG = -30000.0
EPS = 1e-6
DFF = 2816
NJ = 22


class Buf:
    __slots__ = ("name", "w", "r", "excl")

    def __init__(self, name="", excl=False):
        self.name = name
        self.w = {}
        self.r = {}
        self.excl = excl


class Prog:
    ENGS = ("pe", "act", "dve", "pool", "sp")
    DMA_POOL = {"w": 8, "i": 10, "s": 6, "c": 4, "o": 4, "wt": 6}

    def __init__(self, nc):
        self.nc = nc
        self.q = {e: [] for e in self.ENGS}
        self.cnt = {}
        self.seen = {e: defaultdict(int) for e in self.ENGS}
        self.epoch = defaultdict(int)
        self.semkeys = []
        self.dma_n = defaultdict(int)

    def _key(self, stream):
        k = (stream, self.epoch[stream])
        if k not in self.cnt:
            self.cnt[k] = 0
            self.semkeys.append(k)
        return k

    def _signal(self, stream, inc):
        k = self._key(stream)
        if self.cnt[k] + inc > 30000:
            self.epoch[stream] += 1
            k = self._key(stream)
        self.cnt[k] += inc
        return k, self.cnt[k]

    def emit(self, eng, fn, reads=(), writes=(), dma=None):
        need = {}
        own = eng if dma is None else None
        for b in reads:
            for s, v in b.w.items():
                if eng == "pe" and s[0] == "pe":
                    continue
                if need.get(s, 0) < v:
                    need[s] = v
            if b.excl:
                for s, v in b.r.items():
                    if s[0] == own:
                        continue
                    if need.get(s, 0) < v:
                        need[s] = v
        for b in writes:
            for d in (b.w, b.r):
                for s, v in d.items():
                    if s[0] == own and (own == "pe" or False):
                        continue
                    if need.get(s, 0) < v:
                        need[s] = v
        waits = []
        for s, v in need.items():
            if self.seen[eng][s] < v:
                self.seen[eng][s] = v
                waits.append((s, v))
        if dma is None:
            k, val = self._signal(eng, 1)
            inc = 1
        else:
            g = self.DMA_POOL.get(dma, 4)
            n = self.dma_n[dma]; self.dma_n[dma] += 1
            k = ("dma_" + dma, n % g)
            if k not in self.cnt:
                self.cnt[k] = 0
                self.semkeys.append(k)
            if self.cnt[k] > 0 and self.seen[eng][k] < self.cnt[k]:
                self.seen[eng][k] = self.cnt[k]
                waits.append((k, self.cnt[k]))
            self.cnt[k] += 16
            val = self.cnt[k]
            inc = 16
        self.q[eng].append((waits, fn, k, inc))
        for b in reads:
            if b.r.get(k, 0) < val:
                b.r[k] = val
        for b in writes:
            if b.w.get(k, 0) < val:
                b.w[k] = val

    def wait_sems(self, eng, prefix):
        need = [(k, v) for k, v in self.cnt.items() if k[0].startswith(prefix) and v > 0]
        self.q[eng].append((need, None, None, 0))

    def build(self, stack):
        nc = self.nc
        sems = {}
        for k in self.semkeys:
            sems[k] = stack.enter_context(nc.semaphore("s_%s_%d" % k))
        block = stack.enter_context(nc.Block())
        handles = {"pe": block.tensor, "act": block.scalar, "dve": block.vector,
                   "pool": block.gpsimd, "sp": block.sync}
        for e in self.ENGS:
            lst = self.q[e]
            if not lst:
                continue

            def body(h, lst=lst):
                for waits, fn, k, inc in lst:
                    for s, v in waits:
                        h.wait_ge(sems[s], v)
                    if fn is not None:
                        fn(h).then_inc(sems[k], inc)
            handles[e](body)


def build_program():
    nc = bass.Bass("TRN2", target_bir_lowering=False)
    P = Prog(nc)

    def din(name, shape):
        return nc.dram_tensor(name, list(shape), F32, kind="ExternalInput").ap()

    xT = din("xT", [1024, 4096]); pT = din("pT", [256, 2048])
    f1_win = din("f1_win", [1024, 5632]); f1_wout = din("f1_wout", [2816, 1024])
    f2_win = din("f2_win", [1024, 5632]); f2_wout = din("f2_wout", [2816, 1024])
    w_in = din("w_in", [1024, 6704])
    vecs = din("vecs", [128, 40])
    ln_g = din("ln_g", [1024]); ln_b = din("ln_b", [1024])
    wsT = din("wsT", [128, 8, 128]); b_s = din("b_s", [1024])
    w_a = din("w_a", [1024, 1024]); w_b = din("w_b", [1024, 1024]); w_o = din("w_o", [1024, 1024])
    posk = din("posk", [128, 16]); posv = din("posv", [128, 16])
    cw1k = din("cw1k", [128, 16, 256]); cw1v = din("cw1v", [128, 16, 256])
    cw2k = din("cw2k", [256, 64]); cw2v = din("cw2v", [256, 64])
    w_pg = din("w_pg", [1024, 1024]); w_pp = din("w_pp", [256, 1024])
    c_ident = din("c_ident", [128, 128]); c_rmat = din("c_rmat", [128, 128])
    c_sel = din("c_sel", [48, 48 * 64]); c_aaug = din("c_aaug", [128, 2 * 65])
    c_E = din("c_E", [64, 4096]); c_tfirst = din("c_tfirst", [128, 128]); c_tdiag = din("c_tdiag", [128, 128])
    c_dfirst = din("c_dfirst", [128, 128]); c_dmid = din("c_dmid", [128, 128])
    c_cmpb = din("c_cmpb", [128, 2 * 2048])
    c_keep = din("c_keep", [128, 16 * 64]); c_bias = din("c_bias", [128, 16 * 64]); c_valid = din("c_valid", [128, 16 * 64])
    c_cos = din("c_cos", [128, 4096]); c_sin = din("c_sin", [128, 4096]); c_tril = din("c_tril", [128, 128])
    outT = nc.dram_tensor("outT", [1024, 2048], F32, kind="ExternalOutput").ap()
    N_s = nc.dram_tensor("N_s", [8, 128, 8, 512], BF16).ap()
    H_s = nc.dram_tensor("H_s", [4, 128, 8, 512], F32).ap()
    KW_s = nc.dram_tensor("KW_s", [64, 4, 4096], BF16).ap()
    VW_s = nc.dram_tensor("VW_s", [128, 32, 512], BF16).ap()
    WS = nc.dram_tensor("WS", [18, 128, 8, 512], BF16).ap()
    W1s = nc.dram_tensor("W1s", [128, 8, 2 * DFF], BF16).ap()
    W2s = nc.dram_tensor("W2s", [128, NJ, 1024], BF16).ap()
    WPGs = nc.dram_tensor("WPGs", [8, 128, 8, 128], BF16).ap()
    bWPGs = [Buf() for _ in range(8)]
    bW1s = [Buf() for _ in range(NJ)]; bW2s = [Buf() for _ in range(NJ)]
    bWS = [Buf() for _ in range(18)]
    bN = [Buf() for _ in range(8)]; bH = [[Buf() for _ in range(2)] for _ in range(4)]
    bKW = [Buf() for _ in range(8)]; bVW = [Buf() for _ in range(8)]

    st = ExitStack()
    with st:
        arena = {"p": 16640}
        barrier = {"snap": {}}

        def sb(name, shape, dt=F32):
            n = 1
            for d_ in shape[1:]:
                n *= d_
            nbytes = n * (4 if dt == F32 else 2)
            off = arena["p"]
            arena["p"] = off + ((nbytes + 63) // 64) * 64
            assert arena["p"] <= 229376, (name, arena["p"])
            return nc.alloc_sbuf_tensor_at(name, list(shape), dt, offset=off)

        def new_phase():
            barrier["snap"] = dict(P.cnt)

        def fresh():
            b = Buf(); b.w = dict(barrier["snap"]); return b

        def rebarrier(bufs):
            for b in bufs:
                for k_, v_ in barrier["snap"].items():
                    if b.w.get(k_, 0) < v_:
                        b.w[k_] = v_

        ps = [st.enter_context(nc.psum_tensor("ps%d" % i, [128, 512], F32)) for i in range(7)]
        psT = st.enter_context(nc.psum_tensor("psT", [128, 1024], BF16))
        bps = [Buf(excl=True) for _ in range(7)]; bpsT = Buf(excl=True)
        ring_state = {"i": 0, "n": 6}

        def slot():
            i = ring_state["i"] % ring_state["n"]
            ring_state["i"] += 1
            return ps[i], bps[i]

        def mm(out, lhsT, rhs, start, stop, reads, writes, sgc=False):
            if sgc:
                P.emit("pe", lambda e: e.matmul(out, lhsT=lhsT, rhs=rhs, start=start, stop=stop, skip_group_check=True), reads, writes)
            else:
                P.emit("pe", lambda e: e.matmul(out, lhsT=lhsT, rhs=rhs, start=start, stop=stop), reads, writes)

        def act(out, in_, func, reads, writes, **kw):
            P.emit("act", lambda e: e.activation(out=out, in_=in_, func=func, **kw), reads, writes)

        def tt(eng, out, in0, in1, op, reads, writes):
            P.emit(eng, lambda e: e.tensor_tensor(out=out, in0=in0, in1=in1, op=op), reads, writes)

        def tsc(eng, out, in0, s1, s2, op0, op1, reads, writes):
            if s2 is None:
                P.emit(eng, lambda e: e.tensor_scalar(out=out, in0=in0, scalar1=s1, scalar2=None, op0=op0), reads, writes)
            else:
                P.emit(eng, lambda e: e.tensor_scalar(out=out, in0=in0, scalar1=s1, scalar2=s2, op0=op0, op1=op1), reads, writes)

        def stt(eng, out, in0, scalar, in1, op0, op1, reads, writes):
            P.emit(eng, lambda e: e.scalar_tensor_tensor(out=out, in0=in0, scalar=scalar, in1=in1, op0=op0, op1=op1), reads, writes)

        def cp(eng, out, in_, reads, writes):
            P.emit(eng, lambda e: e.tensor_copy(out=out, in_=in_), reads, writes)

        def dma(eng, out, in_, reads, writes, stream):
            P.emit(eng, lambda e: e.dma_start(out=out, in_=in_), reads, writes, dma=stream)

        ident = sb("ident", [128, 128], BF16); onesb = sb("onesb", [128, 128], BF16)
        rmat = sb("rmat", [128, 128], BF16); vec = sb("vec", [128, 40])
        tfirst = sb("tfirst", [128, 128], BF16); tdiag = sb("tdiag", [128, 128], BF16)
        dfirst = sb("dfirst", [128, 128], BF16); dmid = sb("dmid", [128, 128], BF16)
        epsb = sb("epsb", [128, 1]); tinyb = sb("tinyb", [128, 1])
        bC = Buf()
        for t_, d_ in ((ident, c_ident), (rmat, c_rmat), (tfirst, c_tfirst), (tdiag, c_tdiag), (dfirst, c_dfirst), (dmid, c_dmid)):
            dma("pool", t_[:], d_, [], [bC], "c")
        dma("sp", vec[:], vecs, [], [bC], "i")
        P.emit("dve", lambda e: e.memset(onesb[:], 1.0), [], [bC])
        P.emit("dve", lambda e: e.memset(epsb[:], EPS), [], [bC])
        G1, GM, G2, GP, GF = 0, 8, 16, 24, 32
        mark0 = arena["p"]

        W1 = sb("W1", [128, 8, 2 * DFF], BF16)
        W2 = sb("W2", [128, NJ, 1024], BF16)
        bW1 = [Buf() for _ in range(NJ)]; bW2 = [Buf() for _ in range(NJ)]

        def load_ffn_scratch():
            for j0 in range(0, NJ, 2):
                for base in (0, DFF):
                    dma("sp", W1[:, :, base + j0 * 128: base + (j0 + 2) * 128], W1s[:, :, base + j0 * 128: base + (j0 + 2) * 128],
                        [bW1s[j0], bW1s[j0 + 1]], [bW1[j0], bW1[j0 + 1]], "wt")
            for j0 in range(0, NJ, 2):
                dma("sp", W2[:, j0:j0 + 2, :], W2s[:, j0:j0 + 2, :], [bW2s[j0], bW2s[j0 + 1]], [bW2[j0], bW2[j0 + 1]], "wt")

        def cast_ffn_to_scratch(win, wout):
            wv = win.rearrange("(c p) n -> p c n", p=128)
            for j0 in range(0, NJ, 2):
                for base in (0, DFF):
                    dma("pool", W1s[:, :, base + j0 * 128: base + (j0 + 2) * 128], wv[:, :, base + j0 * 128: base + (j0 + 2) * 128],
                        [], [bW1s[j0], bW1s[j0 + 1]], "w")
            wo = wout.rearrange("(j p) n -> p j n", p=128)
            for j0 in range(0, NJ, 2):
                dma("pool", W2s[:, j0:j0 + 2, :], wo[:, j0:j0 + 2, :], [], [bW2s[j0], bW2s[j0 + 1]], "w")

        def load_ffn(win, wout):
            wv = win.rearrange("(c p) n -> p c n", p=128)
            for j0 in range(0, NJ, 2):
                for base in (0, DFF):
                    dma("pool", W1[:, :, base + j0 * 128: base + (j0 + 2) * 128], wv[:, :, base + j0 * 128: base + (j0 + 2) * 128],
                        [], [bW1[j0], bW1[j0 + 1]], "w")
            wo = wout.rearrange("(j p) n -> p j n", p=128)
            for j0 in range(0, NJ, 2):
                dma("pool", W2[:, j0:j0 + 2, :], wo[:, j0:j0 + 2, :], [], [bW2[j0], bW2[j0 + 1]], "w")

        TB = 512
        rstd = sb("rstd", [128, TB]); brstd = Buf()
        nb_off = arena["p"]
        nb = [sb("nb0", [128, 8, TB], BF16)]; bnb = [Buf()]
        hid_off = arena["p"]
        hid = sb("hid", [128, NJ, TB], BF16); bhid = [Buf() for _ in range(NJ)]
        osa = nc.alloc_sbuf_tensor_at("osa", [128, 4, TB], F32, offset=nb_off)
        osb = nc.alloc_sbuf_tensor_at("osb", [128, 4, TB], F32, offset=hid_off + 8 * TB * 2)
        sgt = [sb("sgt%d" % i, [128, TB]) for i in range(3)]; bsgt = [Buf() for _ in range(3)]
        xr0 = sb("xr0", [128, 8, TB])
        xr1_off = arena["p"]
        xr1 = sb("xr1", [128, 8, TB])
        xr = [xr0, xr1]; bxr = [Buf(), Buf()]
        cnt = {"sg": 0, "nb": 0}
        ffn_end = arena["p"]
        print("SBUF FFN end", ffn_end)
        ffn_bufs = bW1 + bW2 + bxr + [brstd] + bnb + bhid + bsgt

        def rmsnorm(xt, bx, gcol, out_t, bout, T):
            tt("dve", hid[:, 0:8, 0:T], xt[:, :, 0:T], xt[:, :, 0:T], ALU.mult, [bx], bhid[0:8])
            pt_, bp = slot()
            for c in range(8):
                mm(pt_[:, 0:T], onesb[:], hid[:, c, 0:T], c == 0, c == 7, bhid[0:8] + [bC], [bp])
            act(rstd[:, 0:T], pt_[:, 0:T], AF.Sqrt, [bp, bC], [brstd], scale=1.0 / 1024, bias=EPS)
            P.emit("dve", lambda e: e.reciprocal(out=rstd[:, 0:T], in_=rstd[:, 0:T]), [brstd], [brstd])
            for c in range(8):
                stt("dve", out_t[:, c, 0:T], xt[:, c, 0:T], vec[:, gcol + c:gcol + c + 1], rstd[:, 0:T],
                    ALU.mult, ALU.mult, [bx, brstd, bC], [bout])

        def ffn(xt, bx, gcol, T):
            n1 = nb[0]; bn1 = bnb[0]
            rmsnorm(xt, bx, gcol, n1, bn1, T)
            for j in range(NJ):
                pt_, bp = slot()
                for k in range(8):
                    mm(pt_[:, 0:T], W1[:, k, j * 128:(j + 1) * 128], n1[:, k, 0:T], k == 0, k == 7, [bW1[j], bn1], [bp])
                pu, bpu = slot()
                for k in range(8):
                    mm(pu[:, 0:T], W1[:, k, DFF + j * 128:DFF + (j + 1) * 128], n1[:, k, 0:T], k == 0, k == 7, [bW1[j], bn1], [bpu])
                i = cnt["sg"] % 3; cnt["sg"] += 1
                act(sgt[i][:, 0:T], pt_[:, 0:T], AF.Silu, [bp], [bsgt[i]])
                tt("dve", hid[:, j, 0:T], sgt[i][:, 0:T], pu[:, 0:T], ALU.mult, [bsgt[i], bpu], [bhid[j]])
            for c in range(8):
                pt_, bp = slot()
                for j in range(NJ):
                    mm(pt_[:, 0:T], W2[:, j, c * 128:(c + 1) * 128], hid[:, j, 0:T], j == 0, j == NJ - 1, [bW2[j], bhid[j]], [bp])
                stt("dve", xt[:, c, 0:T], pt_[:, 0:T], 0.5, xt[:, c, 0:T], ALU.mult, ALU.add, [bp, bx], [bx])

        xv = xT.rearrange("(c p) t -> p c t", p=128)
        dma("sp", xr[0][:], xv[:, :, 0:TB], [], [bxr[0]], "i")
        load_ffn(f1_win, f1_wout)
        winv0 = w_in.rearrange("(c p) n -> p c n", p=128)
        wav0 = w_a.rearrange("(c p) n -> p c n", p=128); wbv0 = w_b.rearrange("(c p) n -> p c n", p=128); wov0 = w_o.rearrange("(c p) n -> p c n", p=128)
        ws_src = [(winv0, 0, 512), (winv0, 512, 512), (winv0, 1024, 512), (winv0, 1536, 512),
                  (winv0, 4656, 512), (wav0, 0, 512), (winv0, 5168, 512), (wav0, 512, 512),
                  (winv0, 2048, 256), (winv0, 2304, 256), (winv0, 2560, 256), (winv0, 2816, 256),
                  (winv0, 5680, 512), (wbv0, 0, 512), (winv0, 6192, 512), (wbv0, 512, 512),
                  (wov0, 0, 512), (wov0, 512, 512)]
        for ti, (src, c0, ncol) in enumerate(ws_src):
            dma("pool", WS[ti][:, :, 0:ncol], src[:, :, c0:c0 + ncol], [], [bWS[ti]], "w")
        cast_ffn_to_scratch(f2_win, f2_wout)
        wpgv = w_pg.rearrange("(c p) n -> p c n", p=128)
        for c in range(8):
            dma("pool", WPGs[c], wpgv[:, :, c * 128:(c + 1) * 128], [], [bWPGs[c]], "w")
        for blk in range(8):
            s = blk
            xt, bx = xr[blk % 2], bxr[blk % 2]
            if blk + 1 < 8:
                dma("sp", xr[(blk + 1) % 2][:], xv[:, :, (blk + 1) * TB:(blk + 2) * TB], [], [bxr[(blk + 1) % 2]], "i")
            ffn(xt, bx, G1, TB)
            if s % 2 == 1:
                dma("sp", H_s[s // 2], xt[:], [bx], bH[s // 2], "s")
            n2 = nb[0]; bn2 = bnb[0]
            rmsnorm(xt, bx, GM, n2, bn2, TB)
            dma("sp", N_s[s], n2[:], [bn2], [bN[s]], "s")

        KSTOP = ""
        if KSTOP == "A":
            P.wait_sems("sp", "dma_"); P.build(st); return nc
        arena["p"] = mark0
        new_phase()
        KS = sb("KS", [128, 4, 4096], BF16); bKS = fresh()
        VS = sb("VS", [128, 32, 4, 128], BF16); bVS = fresh()
        KCMP = sb("KCMP", [64, 4, 256], BF16); VCMP = sb("VCMP", [128, 2, 4, 128], BF16); bCMP = fresh()
        mark1 = arena["p"]
        for k in range(4):
            dma("pool", KS[64:128, k, :], c_E, [], [bKS], "c")
        P.emit("pool", lambda e: e.memset(VS[:, :, :, 64:128], 1.0), [], [bVS])
        P.emit("pool", lambda e: e.memset(VCMP[:, :, :, 64:128], 1.0), [], [bCMP])
        P.emit("pool", lambda e: e.memset(KCMP[:], 0.0), [], [bCMP])
        if True:
            sbB = sb
            ctab = sb("ctab", [128, 512]); stab = sb("stab", [128, 512]); btab = fresh()
            tq = [sb("tq%d" % i, [128, 512], BF16) for i in range(2)]; btq = [fresh(), fresh()]
            t1 = sb("t1", [128, 512]); t2 = sb("t2", [128, 512]); bt1 = fresh(); bt2 = fresh()
            Wkv = sbB("Wkv", [128, 8, 1536], BF16); bWkv = fresh()
            CW1 = [sbB("CW1k", [128, 16, 256], BF16), sbB("CW1v", [128, 16, 256], BF16)]
            CW2 = [sbB("CW2k", [128, 2, 64], BF16), sbB("CW2v", [128, 2, 64], BF16)]
            POS = [sbB("POSk", [128, 16], BF16), sbB("POSv", [128, 16], BF16)]
            bCW = fresh()
            Hpre = sbB("Hpre", [128, 16, 256]); bHp = fresh()
            HT = sbB("HT", [128, 16, 256], BF16); bHT = fresh()
            constv = sbB("constv", [128, 4]); bcv = fresh()
            ntB = [sbB("ntB%d" % i, [128, 8, 512], BF16) for i in range(2)]; bntB = [fresh(), fresh()]
            KCt = [sbB("KCt", [128, 4, 512], BF16), sbB("VCt", [128, 4, 512], BF16)]; bKCt = [fresh(), fresh()]
            KWst = sbB("KWst", [64, 4, 512], BF16); bKWst = fresh()
            VWst = sbB("VWst", [128, 4, 4, 128], BF16); bVWst = fresh()
            dma("pool", Wkv[:], w_in.rearrange("(c p) n -> p c n", p=128)[:, :, 3072:4608], [], [bWkv], "w")
            dma("pool", CW1[0][:], cw1k, [], [bCW], "w"); dma("pool", CW1[1][:], cw1v, [], [bCW], "w")
            dma("pool", CW2[0][:], cw2k.rearrange("(c p) n -> p c n", p=128), [], [bCW], "w")
            dma("pool", CW2[1][:], cw2v.rearrange("(c p) n -> p c n", p=128), [], [bCW], "w")
            dma("pool", POS[0][:], posk, [], [bCW], "w"); dma("pool", POS[1][:], posv, [], [bCW], "w")
            P.emit("pool", lambda e: e.memset(Hpre[:], 0.0), [], [bHp])
            P.emit("pool", lambda e: e.memset(VWst[:, :, :, 64:128], 1.0), [], [bVWst])
            KB = 99
            def kb_stop(level):
                if KB == level:
                    P.wait_sems("sp", "dma_"); P.build(st); return True
                return False
            if kb_stop(0): return nc
            for s in range(8):
                nt_, bnt = ntB[s % 2], bntB[s % 2]
                dma("sp", nt_[:], N_s[s], [bN[s]], [bnt], "i")
                dma("sp", ctab[:], c_cos[:, s * 512:(s + 1) * 512], [], [btab], "i")
                dma("sp", stab[:], c_sin[:, s * 512:(s + 1) * 512], [], [btab], "i")
                if kb_stop(1): return nc
                for kv, col0 in ((0, 0), (1, 256)):
                    for cc in range(2):
                        pt_, bp = slot()
                        for k in range(8):
                            mm(pt_[:], Wkv[:, k, col0 + cc * 128:col0 + (cc + 1) * 128], nt_[:, k, :], k == 0, k == 7, [bWkv, bnt], [bp])
                        act(KCt[kv][0:64, 2 * cc, :], pt_[0:64, :], AF.Copy, [bp], [bKCt[kv]])
                        cp("dve", KCt[kv][0:64, 2 * cc + 1, :], pt_[64:128, :], [bp], [bKCt[kv]])
                        act(KCt[kv][64:128, 2 * cc, 0:511], pt_[0:64, 1:512], AF.Copy, [bp], [bKCt[kv]])
                        cp("dve", KCt[kv][64:128, 2 * cc + 1, 0:511], pt_[64:128, 1:512], [bp], [bKCt[kv]])
                if kb_stop(2): return nc
                for half in range(2):
                    pt_, bp = slot()
                    for mc in range(2):
                        for kv in range(2):
                            for kh in range(4):
                                gi = (mc * 2 + kv) * 4 + kh
                                for lp in range(8):
                                    rhs = bass.AP(KCt[kv], kh * 512 + 2 * lp, [[4 * 512, 128], [16, 32]])
                                    mm(pt_[:, gi * 32:(gi + 1) * 32], CW1[kv][:, half * 8 + lp, mc * 128:(mc + 1) * 128], rhs,
                                       lp == 0, lp == 7, [bCW, bKCt[kv]], [bp])
                    if half == 0:
                        o_ap = Hpre[:, :, 32 * s:32 * s + 32]
                        i_ap = bass.AP(pt_, 0, [[512, 128], [32, 16], [1, 32]])
                    elif s == 0:
                        o_ap = Hpre[:, :, 0:31]
                        i_ap = bass.AP(pt_, 1, [[512, 128], [32, 16], [1, 31]])
                    else:
                        o_ap = Hpre[:, :, 32 * s - 1:32 * s + 31]
                        i_ap = bass.AP(pt_, 0, [[512, 128], [32, 16], [1, 32]])
                    tt("dve", o_ap, o_ap, i_ap, ALU.add, [bp, bHp], [bHp])
                if kb_stop(3): return nc
                for which, col0 in ((0, 512), (1, 1024)):
                    for cc in range(2):
                        pt_, bp = slot()
                        for k in range(8):
                            mm(pt_[:], Wkv[:, k, col0 + cc * 128:col0 + (cc + 1) * 128], nt_[:, k, :], k == 0, k == 7, [bWkv, bnt], [bp])
                        i = (which * 2 + cc) % 2
                        act(tq[i][:], pt_[:], AF.Copy, [bp], [btq[i]])
                        p2, bp2 = slot()
                        mm(p2[:], rmat[:], tq[i][:], True, True, [bC, btq[i]], [bp2])
                        tt("dve", t1[:], p2[:], stab[:], ALU.mult, [bp2, btab], [bt1])
                        tt("pool", t2[:], tq[i][:], ctab[:], ALU.mult, [btq[i], btab], [bt2])
                        if which == 0:
                            tt("dve", KS[0:64, 2 * cc, s * 512:(s + 1) * 512], t1[0:64, :], t2[0:64, :], ALU.add, [bt1, bt2], [bKS])
                            tt("pool", KS[0:64, 2 * cc + 1, s * 512:(s + 1) * 512], t1[64:128, :], t2[64:128, :], ALU.add, [bt1, bt2], [bKS])
                        else:
                            tt("dve", KWst[0:64, 2 * cc, :], t1[0:64, :], t2[0:64, :], ALU.add, [bt1, bt2], [bKWst])
                            tt("pool", KWst[0:64, 2 * cc + 1, :], t1[64:128, :], t2[64:128, :], ALU.add, [bt1, bt2], [bKWst])
                dma("sp", KW_s[:, :, s * 512:(s + 1) * 512], KWst[:], [bKWst], [bKW[s]], "s")
                if kb_stop(4): return nc
                for tti in range(4):
                    pt_, bp = slot()
                    for k in range(8):
                        mm(pt_[:, 0:256], nt_[:, k, tti * 128:(tti + 1) * 128], Wkv[:, k, 768:1024], k == 0, k == 7, [bWkv, bnt], [bp])
                    for k in range(8):
                        mm(pt_[:, 256:512], nt_[:, k, tti * 128:(tti + 1) * 128], Wkv[:, k, 1280:1536], k == 0, k == 7, [bWkv, bnt], [bp])
                    KV = ""
                    if "a" not in KV:
                        act(VS[:, 4 * s + tti, :, 0:64], bass.AP(pt_, 0, [[512, 128], [64, 4], [1, 64]]), AF.Copy, [bp], [bVS])
                    if "b" not in KV:
                        cp("dve", VWst[:, tti, :, 0:64], bass.AP(pt_, 256, [[512, 128], [64, 4], [1, 64]]), [bp], [bVWst])
                if "c" not in KV:
                    dma("sp", VW_s[:, 4 * s:4 * s + 4, :], VWst[:].rearrange("p a b c -> p a (b c)"), [bVWst], [bVW[s]], "s")
                if kb_stop(5): return nc
            if kb_stop(6): return nc
            for kv in range(2):
                for mc in range(2):
                    pt_, bp = slot()
                    for lp in range(16):
                        mm(pt_[:, 0:1], CW1[kv][:, lp, mc * 128:(mc + 1) * 128], POS[kv][:, lp:lp + 1], lp == 0, lp == 15, [bCW], [bp])
                    cp("dve", constv[:, mc * 2 + kv:mc * 2 + kv + 1], pt_[:, 0:1], [bp], [bcv])
            for kv in range(2):
                for mc in range(2):
                    g0 = (mc * 2 + kv) * 4
                    act(HT[:, g0:g0 + 4, :], Hpre[:, g0:g0 + 4, :], AF.Gelu_apprx_tanh, [bHp, bcv], [bHT], bias=constv[:, mc * 2 + kv:mc * 2 + kv + 1])
            for kh in range(4):
                pt_, bp = slot()
                for mc in range(2):
                    mm(pt_[0:64, 0:256], CW2[0][:, mc, :], HT[:, (mc * 2 + 0) * 4 + kh, :], mc == 0, mc == 1, [bCW, bHT], [bp])
                cp("dve", KCMP[0:64, kh, 0:255], pt_[0:64, 0:255], [bp], [bCMP])
                for nti in range(2):
                    pt_, bp = slot()
                    for mc in range(2):
                        mm(pt_[:, 0:64], HT[:, (mc * 2 + 1) * 4 + kh, nti * 128:(nti + 1) * 128], CW2[1][:, mc, :], mc == 0, mc == 1, [bCW, bHT], [bp])
                    cp("dve", VCMP[:, nti, kh, 0:64], pt_[:, 0:64], [bp], [bCMP])

        if KSTOP == "B":
            P.wait_sems("sp", "dma_"); P.build(st); return nc
        TC = 256
        arena["p"] = mark1
        new_phase()
        if True:
            sbC = sb
            WST = sbC("WST", [128, 8, 128], BF16); bsrow = sbC("bsrow", [1, 2, 1024], BF16)
            markc = arena["p"]
            wstf = sbC("wstf", [128, 8, 128]); tril = sbC("tril", [128, 128]); bsf = sbC("bsf", [1, 1024]); bsf2 = sbC("bsf2", [1, 1024])
            bCC = fresh()
            dma("sp", wstf[:], wsT, [], [bCC], "i"); dma("sp", tril[:], c_tril, [], [bCC], "i")
            dma("sp", bsf[:], b_s.rearrange("(o n) -> o n", o=1), [], [bCC], "i")
            tt("dve", WST[:], wstf[:], bass.AP(tril, 0, [[128, 128], [0, 8], [1, 128]]), ALU.mult, [bCC], [bCC])
            cp("dve", bsrow[0:1, 0, :], bsf[:], [bCC], [bCC])
            cp("dve", bsf2[:], bsrow[0:1, 0, :], [bCC], [bCC])
            tt("dve", bsrow[0:1, 1, :], bsf[:], bsf2[:], ALU.subtract, [bCC], [bCC])
            arena["p"] = markc
            new_phase()
            rebarrier([bCC])
            aaug = sbC("aaug", [128, 2 * 65], BF16)
            LNG = sbC("LNG", [128, 1024]); LNB = sbC("LNB", [128, 1024])
            Wg = sbC("Wg", [128, 8, 48], BF16)
            cmpbu = sbC("cmpbu", [128, 2, 256], BF16); keepu = sbC("keepu", [128, 128]); biasu = sbC("biasu", [128, 128]); validu = sbC("validu", [128, 128])
            bUT = fresh()
            tq = [sb("tqc%d" % i, [128, 256], BF16) for i in range(2)]; btq = [fresh(), fresh()]
            t1 = sb("t1c", [128, 256]); t2 = sb("t2c", [128, 256]); bt1 = fresh(); bt2 = fresh()
            dma("pool", aaug[:], c_aaug, [], [bCC], "c")
            dma("sp", LNG[:], ln_g.partition_broadcast(128), [], [bCC], "i"); dma("sp", LNB[:], ln_b.partition_broadcast(128), [], [bCC], "i")
            dma("pool", Wg[:], w_in.rearrange("(c p) n -> p c n", p=128)[:, :, 4608:4656], [], [bCC], "c")
            P.emit("dve", lambda e: e.memset(tinyb[:], 1e-30), [], [bCC])
            ntC = [sbC("ntC%d" % i, [128, 8, TC], BF16) for i in range(2)]; bntC = [fresh(), fresh()]
            U = sbC("U", [128, 8, TC], BF16); bU = fresh()
            vt = [sbC("vt0", [128, 1024])] * 2; bvt = [fresh()] * 2
            vn = [sbC("vn%d" % i, [128, 1024], BF16) for i in range(2)]; bvn = [fresh(), fresh()]
            stats = sbC("stats", [128, 2, 6]); mv = sbC("mv", [128, 2]); rs = sbC("rs", [128, 1]); bst = fresh()
            gtmp = sbC("gtmp", [128, TC]); bgtmp = fresh()
            gt = [sbC("gt%d" % i, [128, TC], BF16) for i in range(2)]; bgt = [fresh(), fresh()]
            M = sbC("M", [128, 8, TC], BF16); bM = fresh()
            SGq = sbC("SGq", [128, 2, 48]); bSG = fresh()
            QRAW = [sbC("QRAW%d" % i, [64, 4, TC], BF16) for i in range(2)]; bQRAW = [fresh(), fresh()]
            QR = [sbC("QR%d" % i, [128, 4, TC], BF16) for i in range(2)]; bQR = [fresh(), fresh()]
            PT = [sbC("PT%d" % i, [128, 512], BF16) for i in range(4)]; bPT = [fresh() for _ in range(4)]
            OB = sbC("OB", [128, 8, TC], BF16); bOB = fresh()
            KWw = sbC("KWw", [64, 4, 1024], BF16); VWw = sbC("VWw", [128, 8, 4, 128], BF16); bWw = fresh()
            Wt = [sbC("Wt%d" % i, [128, 8, 512], BF16) for i in range(3)]; bWt = [fresh() for _ in range(3)]
            HCb = [sbC("HCb%d" % i, [128, 8, TC]) for i in range(2)]; bHCb = [fresh(), fresh()]
            ctC = sbC("ctC", [128, TC]); stC_ = sbC("stC_", [128, TC]); btabC = fresh()
            rzq = sbC("rzq", [128, 24]); cq = sbC("cq", [128, 24]); bcq = fresh()
            accq = [sbC("accq%d" % i, [128, 256]) for i in range(2)]; tqa = sbC("tqa", [128, 256]); tqb = sbC("tqb", [128, 256])
            obq = [sbC("obq%d" % i, [128, 256], BF16) for i in range(2)]
            baccq = [fresh(), fresh()]; btqa = fresh(); btqb = fresh(); bobq = [fresh(), fresh()]
            rz4 = sbC("rz4", [128, 4]); imp = sbC("imp", [128, 64]); score = sbC("score", [128, 64]); wk = sbC("wk", [128, 64])
            m8 = sbC("m8", [128, 8]); m8b = sbC("m8b", [128, 8]); thr = sbC("thr", [128, 1]); MB = sbC("MB", [128, 128], BF16)
            MBq = [MB, sbC("MB1", [128, 128], BF16)]
            bsm = fresh(); bMBq = [fresh(), fresh()]
            print("SBUF phase C end", arena["p"])
            P.emit("pool", lambda e: e.memset(MBq[0][:], 0.0), [], [bMBq[0]])
            P.emit("pool", lambda e: e.memset(MBq[1][:], 0.0), [], [bMBq[1]])
            wcnt = {"i": 0, "pt": 0, "gt": 0, "hc": 0}
            winv = w_in.rearrange("(c p) n -> p c n", p=128)

            WORDER = [2, 3, 0, 1] + list(range(4, 18))
            wq = []
            wfree = [0, 1, 2]
            wstate = {"next": 0}

            def wfill():
                while wfree and wstate["next"] < 8 * 18:
                    ti = WORDER[wstate["next"] % 18]; wstate["next"] += 1
                    i = wfree.pop(0)
                    ncols = 256 if 8 <= ti < 12 else 512
                    dma("sp", Wt[i][:, :, 0:ncols], WS[ti][:, :, 0:ncols], [bWS[ti]], [bWt[i]], "wt")
                    wq.append((ti, i))

            def wload(ti, ncols):
                if not wq:
                    wfill()
                t_, i = wq.pop(0)
                assert t_ == ti, (t_, ti)
                return Wt[i], (bWt[i], i)

            def wdone(bw):
                wfree.append(bw[1])
                wfill()

            def w1024(mat):
                return mat.rearrange("(c p) n -> p c n", p=128)

            def unit_loads(u):
                o, hf = u // 2, u % 2
                sl = 2 * o + 1
                pos0 = sl * 512 + hf * TC
                dma("sp", ntC[u % 2][:], N_s[sl][:, :, hf * TC:(hf + 1) * TC], [bN[sl]], [bntC[u % 2]], "i")
                dma("sp", ctC[:], c_cos[:, pos0:pos0 + TC], [], [btabC], "i")
                dma("sp", stC_[:], c_sin[:, pos0:pos0 + TC], [], [btabC], "i")
                if hf == 0:
                    dma("sp", KWw[:], KW_s[:, :, (2 * o) * 512:(2 * o + 2) * 512], [bKW[2 * o], bKW[2 * o + 1]], [bWw], "i")
                    dma("sp", VWw[:].rearrange("p a b c -> p a (b c)"), VW_s[:, 8 * o:8 * o + 8, :], [bVW[2 * o], bVW[2 * o + 1]], [bWw], "i")
                dma("pool", cmpbu[:], c_cmpb.rearrange("p (n t) -> p n t", n=2)[:, :, u * 256:(u + 1) * 256], [], [bUT], "c")
                dma("sp", HCb[u % 2][:], H_s[o][:, :, hf * TC:(hf + 1) * TC], [bH[o][hf]], [bHCb[u % 2]], "i")
                dma("sp", keepu[:], c_keep[:, 2 * u * 64:(2 * u + 2) * 64], [], [bUT], "i")
                dma("sp", biasu[:], c_bias[:, 2 * u * 64:(2 * u + 2) * 64], [], [bUT], "i")
                dma("sp", validu[:], c_valid[:, 2 * u * 64:(2 * u + 2) * 64], [], [bUT], "i")

            unit_loads(0)
            for u in range(8):
                o, hf = u // 2, u % 2
                sl = 2 * o + 1
                nt_, bnt = ntC[u % 2], bntC[u % 2]
                ring_state["n"] = 6
                wv0, bwv0x = wload(2, 512)
                wv1, bwv1x = wload(3, 512)
                for tti in range(2):
                    for half, (wt, bw) in enumerate(((wv0, bwv0x[0]), (wv1, bwv1x[0]))):
                        pt_, bp = slot()
                        for k in range(8):
                            mm(pt_[:], nt_[:, k, tti * 128:(tti + 1) * 128], wt[:, k, :], k == 0, k == 7, [bw, bnt], [bp])
                        act(vt[tti][:, half * 512:(half + 1) * 512], pt_[:], AF.Gelu_apprx_tanh, [bp], [bvt[tti]])
                    for hh in range(2):
                        P.emit("dve", lambda e, tti=tti, hh=hh: e.bn_stats(out=stats[:, hh, :], in_=vt[tti][:, hh * 512:(hh + 1) * 512]), [bvt[tti]], [bst])
                    P.emit("dve", lambda e: e.bn_aggr(out=mv[:], in_=stats[:]), [bst], [bst])
                    act(rs[:], mv[:, 1:2], AF.Sqrt, [bst, bC], [bst], bias=EPS)
                    P.emit("dve", lambda e: e.reciprocal(out=rs[:], in_=rs[:]), [bst], [bst])
                    tsc("dve", vt[tti][:], vt[tti][:], mv[:, 0:1], rs[:, 0:1], ALU.subtract, ALU.mult, [bvt[tti], bst], [bvt[tti]])
                    tt("pool", vt[tti][:], vt[tti][:], LNG[:], ALU.mult, [bvt[tti], bCC], [bvt[tti]])
                    tt("dve", vn[tti][:], vt[tti][:], LNB[:], ALU.add, [bvt[tti], bCC], [bvn[tti]])
                wdone(bwv0x); wdone(bwv1x)
                for half in range(2):
                    wt, bwx = wload(half, 512)
                    bw = bwx[0]
                    for c4 in range(4):
                        c = half * 4 + c4
                        pt_, bp = slot()
                        for k in range(8):
                            mm(pt_[:, 0:TC], wt[:, k, c4 * 128:(c4 + 1) * 128], nt_[:, k, :], k == 0, k == 7, [bw, bnt], [bp])
                        act(U[:, c, :], pt_[:, 0:TC], AF.Gelu_apprx_tanh, [bp], [bU])
                    wdone(bwx)
                for g in range(8):
                    pt_, bp = slot()
                    for tti in range(2):
                        mm(pt_[:, tti * 128:(tti + 1) * 128], vn[tti][:, g * 128:(g + 1) * 128], WST[:, g, :], True, False, [bvn[tti], bCC], [bp])
                        mm(pt_[:, tti * 128:(tti + 1) * 128], onesb[0:1, :], bsrow[0:1, 0, g * 128:(g + 1) * 128], False, False, [bC, bCC], [bp])
                        mm(pt_[:, tti * 128:(tti + 1) * 128], onesb[0:1, :], bsrow[0:1, 1, g * 128:(g + 1) * 128], False, True, [bC, bCC], [bp])
                    tt("dve", U[:, g, :], pt_[:, 0:TC], U[:, g, :], ALU.mult, [bp, bU], [bU])
                for half in range(2):
                    wtg, bwgx = wload(4 + 2 * half, 512)
                    wta, bwax = wload(5 + 2 * half, 512)
                    bwg, bwa = bwgx[0], bwax[0]
                    for c4 in range(4):
                        c = half * 4 + c4
                        pg, bpg = slot()
                        for k in range(8):
                            mm(pg[:, 0:TC], wtg[:, k, c4 * 128:(c4 + 1) * 128], nt_[:, k, :], k == 0, k == 7, [bwg, bnt], [bpg])
                        i = wcnt["gt"] % 2; wcnt["gt"] += 1
                        act(gt[i][:], pg[:, 0:TC], AF.Sigmoid, [bpg], [bgt[i]])
                        py, bpy = slot()
                        for k in range(8):
                            mm(py[:, 0:TC], wta[:, k, c4 * 128:(c4 + 1) * 128], U[:, k, :], k == 0, k == 7, [bwa, bU], [bpy])
                        tt("dve", M[:, c, :], py[:, 0:TC], gt[i][:], ALU.mult, [bpy, bgt[i]], [bM])
                    wdone(bwgx); wdone(bwax)
                for qq in range(2):
                    pt_, bp = slot()
                    for k in range(8):
                        mm(pt_[:, 0:48], nt_[:, k, qq * 128:(qq + 1) * 128], Wg[:, k, :], k == 0, k == 7, [bCC, bnt], [bp])
                    act(SGq[:, qq, :], pt_[:, 0:48], AF.Sigmoid, [bp], [bSG])
                ring_state["n"] = 3

                def q_proj(kh):
                    qraw, bqraw = QRAW[kh % 2], bQRAW[kh % 2]
                    qr, bqr = QR[kh % 2], bQR[kh % 2]
                    wq_, bwqx = wload(8 + kh, 256)
                    bwq = bwqx[0]
                    for cc in range(2):
                        pt_, bp = slot()
                        for k in range(8):
                            mm(pt_[:, 0:TC], wq_[:, k, cc * 128:(cc + 1) * 128], nt_[:, k, :], k == 0, k == 7, [bwq, bnt], [bp])
                        i = cc
                        act(tq[i][:, 0:TC], pt_[:, 0:TC], AF.Copy, [bp], [btq[i]])
                        cp("pool", qraw[0:64, 2 * cc, :], tq[i][0:64, 0:TC], [btq[i]], [bqraw])
                        cp("pool", qraw[0:64, 2 * cc + 1, :], tq[i][64:128, 0:TC], [btq[i]], [bqraw])
                        p2, bp2 = slot()
                        mm(p2[:, 0:TC], rmat[:], tq[i][:, 0:TC], True, True, [bC, btq[i]], [bp2])
                        tt("dve", t1[:, 0:TC], p2[:, 0:TC], stC_[:], ALU.mult, [bp2, btabC], [bt1])
                        tt("pool", t2[:, 0:TC], tq[i][:, 0:TC], ctC[:], ALU.mult, [btq[i], btabC], [bt2])
                        tt("dve", qr[0:64, 2 * cc, :], t1[0:64, 0:TC], t2[0:64, 0:TC], ALU.add, [bt1, bt2], [bqr])
                        tt("pool", qr[0:64, 2 * cc + 1, :], t1[64:128, 0:TC], t2[64:128, 0:TC], ALU.add, [bt1, bt2], [bqr])
                    wdone(bwqx)

                def attention(kh):
                    wfill()
                    qraw, bqraw = QRAW[kh % 2], bQRAW[kh % 2]
                    qr, bqr = QR[kh % 2], bQR[kh % 2]

                    def qap(t, npart, qs):
                        return bass.AP(t, qs, [[4 * TC, npart], [TC, 4], [1, 128]])

                    tiles = []
                    for qi_ in range(2):
                        qs = qi_ * 128
                        for nti in range(2):
                            tiles.append(dict(kind="cmp", q=qi_, nti=nti, lhsT=KCMP[0:64, kh, nti * 128:(nti + 1) * 128], rhs=qap(qraw, 64, qs),
                                              rr=[bCMP, bqraw], btab=cmpbu, boff=nti * 256 + qi_ * 128, v=VCMP[:, nti, kh, 0:65], vr=[bCMP],
                                              bank=3, br=0, first=nti == 0, last=nti == 1))
                    for qi_ in range(2):
                        qs = qi_ * 128
                        for off in range(5):
                            lt = hf * 2 + qi_ + off
                            if o == 0 and lt < 4:
                                btab_ = dfirst if off == 0 else dmid
                            elif off == 0:
                                btab_ = tfirst
                            elif off == 4:
                                btab_ = tdiag
                            else:
                                btab_ = None
                            tiles.append(dict(kind="win", q=qi_, lhsT=KWw[0:64, kh, lt * 128:(lt + 1) * 128], rhs=qap(qr, 64, qs), rr=[bWw, bqr],
                                              btab=btab_, boff=0, v=VWw[:, lt, kh, 0:65], vr=[bWw], bank=5, br=2, first=off == 0, last=off == 4))
                    for qi_ in range(2):
                        qs = qi_ * 128
                        QT = sl * 4 + hf * 2 + qi_
                        for kt in range(QT + 1):
                            tiles.append(dict(kind="sel", q=qi_, kt=kt, lhsT=KS[:, kh, kt * 128:(kt + 1) * 128], rhs=qap(qr, 128, qs), rr=[bKS, bqr],
                                              btab=tdiag if kt == QT else None, boff=0, v=VS[:, kt, kh, 0:65], vr=[bVS], bank=4, br=1,
                                              first=kt == 0, last=kt == QT))

                    def emit_S(t):
                        pS, bpS = slot()
                        mm(pS[:], t["lhsT"], t["rhs"], True, t["btab"] is None, t["rr"], [bpS])
                        if t["btab"] is not None:
                            ncol = 1
                            for d_ in t["btab"].shape[1:]:
                                ncol *= d_
                            mm(pS[:], ident[:], bass.AP(t["btab"], t["boff"], [[ncol, 128], [0, 4], [1, 128]]), False, True, [bC, bCC, bUT], [bpS])
                        i = wcnt["pt"] % 4; wcnt["pt"] += 1
                        act(PT[i][:], pS[:], AF.Exp, [bpS], [bPT[i]], scale=0.125)
                        t["pt"] = i

                    def emit_mask_dve(qi_):
                        tsc("dve", rz4[:], bass.AP(ps[6], 64, [[512, 128], [65, 4]]), 1e-30, None, ALU.max, None, [bps[6], bCC], [bsm])
                        P.emit("dve", lambda e: e.reciprocal(out=rz4[:], in_=rz4[:]), [bsm], [bsm])
                        tsc("dve", imp[:], ps[6][:, 0:64], rz4[:, 0:1], None, ALU.mult, None, [bps[6], bsm], [bsm])
                        for g in range(1, 4):
                            stt("dve", imp[:], ps[6][:, g * 65:g * 65 + 64], rz4[:, g:g + 1], imp[:], ALU.mult, ALU.add, [bps[6], bsm], [bsm])
                        tt("dve", score[:], imp[:], keepu[:, qi_ * 64:(qi_ + 1) * 64], ALU.mult, [bsm, bUT], [bsm])
                        tt("dve", score[:], score[:], biasu[:, qi_ * 64:(qi_ + 1) * 64], ALU.add, [bsm, bUT], [bsm])
                        P.emit("dve", lambda e: e.max(out=m8[:], in_=score[:]), [bsm], [bsm])
                        P.emit("dve", lambda e: e.match_replace(out=wk[:], in_to_replace=m8[:], in_values=score[:], imm_value=-3e9), [bsm], [bsm])
                        P.emit("dve", lambda e: e.max(out=m8b[:], in_=wk[:]), [bsm], [bsm])
                        P.emit("dve", lambda e: e.tensor_reduce(out=thr[:], in_=m8b[:], axis=AX.X, op=ALU.min), [bsm], [bsm])
                        stt("dve", wk[:], score[:], thr[:, 0:1], validu[:, qi_ * 64:(qi_ + 1) * 64], ALU.is_ge, ALU.mult, [bsm, bUT], [bsm])
                        tsc("dve", MBq[qi_][:, 64:128], wk[:], 1.0, -NEG, ALU.subtract, ALU.mult, [bsm], [bMBq[qi_]])

                    def emit_mask_pe(qi_):
                        qs = qi_ * 128
                        P.emit("pe", lambda e: e.transpose(out=psT[:, qi_ * 128:(qi_ + 1) * 128], in_=MBq[qi_][:], identity=ident[:]), [bMBq[qi_], bC], [bpsT])
                        cp("dve", bass.AP(qr, 64 * 4 * TC + qs, [[4 * TC, 64], [TC, 4], [1, 128]]),
                           bass.AP(psT, 64 * 1024 + qi_ * 128, [[1024, 64], [0, 4], [1, 128]]), [bpsT], [bqr])

                    pending = []

                    def emit_combine(qi_, br, bank):
                        rz_ = rzq[:, qi_ * 12 + br * 4:qi_ * 12 + br * 4 + 4]
                        c_ = cq[:, qi_ * 12 + br * 4:qi_ * 12 + br * 4 + 4]
                        tsc("dve", rz_, bass.AP(ps[bank], 64, [[512, 128], [65, 4]]), 1e-30, None, ALU.max, None, [bps[bank]], [bcq])
                        P.emit("dve", lambda e: e.reciprocal(out=rz_, in_=rz_), [bcq], [bcq])
                        tt("dve", c_, rz_, bass.AP(SGq, qi_ * 48 + 12 * kh + br, [[96, 128], [3, 4]]), ALU.mult, [bcq, bSG], [bcq])
                        ov = bass.AP(ps[bank], 0, [[512, 128], [65, 4], [1, 64]])
                        cb = bass.AP(cq, qi_ * 12 + br * 4, [[24, 128], [1, 4], [0, 64]])

                        def v3(t):
                            return bass.AP(t, 0, [[256, 128], [64, 4], [1, 64]])
                        if br == 0:
                            tt("dve", v3(accq[qi_]), ov, cb, ALU.mult, [bps[bank], bcq], [baccq[qi_]])
                        elif br == 2:
                            tt("dve", v3(tqa), ov, cb, ALU.mult, [bps[bank], bcq], [btqa])
                            tt("pool", accq[qi_][:], accq[qi_][:], tqa[:], ALU.add, [baccq[qi_], btqa], [baccq[qi_]])
                        else:
                            tt("dve", v3(tqb), ov, cb, ALU.mult, [bps[bank], bcq], [btqb])
                            tt("pool", obq[qi_][:], accq[qi_][:], tqb[:], ALU.add, [baccq[qi_], btqb], [bobq[qi_]])
                            pending.append(qi_)

                    def emit_writeback(qi_):
                        qs = qi_ * 128
                        for hp in range(2):
                            P.emit("pe", lambda e, hp=hp: e.transpose(out=psT[:, 256 + qi_ * 256 + hp * 128:256 + qi_ * 256 + (hp + 1) * 128],
                                                                    in_=obq[qi_][:, hp * 128:(hp + 1) * 128], identity=ident[:]), [bobq[qi_], bC], [bpsT])
                        cp("dve", bass.AP(OB, (2 * kh) * TC + qs, [[8 * TC, 128], [TC, 2], [1, 128]]),
                           bass.AP(psT, 256 + qi_ * 256, [[1024, 128], [128, 2], [1, 128]]), [bpsT], [bOB])

                    def emit_PV(t):
                        i = t["pt"]
                        for g in range(4):
                            mm(ps[t["bank"]][:, g * 65:(g + 1) * 65], PT[i][:, g * 128:(g + 1) * 128], t["v"],
                               t["first"] and g == 0, t["last"], t["vr"] + [bPT[i]], [bps[t["bank"]]], sgc=True)
                        if t["kind"] == "cmp":
                            nti = t["nti"]
                            for g in range(4):
                                mm(ps[6][:, g * 65:(g + 1) * 65], PT[i][:, g * 128:(g + 1) * 128], aaug[:, nti * 65:(nti + 1) * 65],
                                   nti == 0 and g == 0, nti == 1, [bPT[i], bCC], [bps[6]], sgc=True)
                            if nti == 1:
                                emit_mask_dve(t["q"])
                        if t["last"]:
                            emit_combine(t["q"], t["br"], t["bank"])

                    LA = 2
                    age = 0
                    for idx in range(len(tiles) + LA):
                        if idx < len(tiles):
                            if tiles[idx]["kind"] == "sel" and tiles[idx]["kt"] == 0:
                                emit_mask_pe(tiles[idx]["q"])
                            emit_S(tiles[idx])
                        if idx >= LA:
                            emit_PV(tiles[idx - LA])
                        if pending:
                            age += 1
                            if age >= 4:
                                emit_writeback(pending.pop(0)); age = 0
                    while pending:
                        emit_writeback(pending.pop(0))

                q_proj(0)
                for kh in range(4):
                    if kh + 1 < 4:
                        q_proj(kh + 1)
                    attention(kh)
                if u + 1 < 8:
                    unit_loads(u + 1)
                ring_state["n"] = 6
                for half in range(2):
                    wtg, bwgx = wload(12 + 2 * half, 512)
                    wtb, bwbx = wload(13 + 2 * half, 512)
                    bwg, bwb = bwgx[0], bwbx[0]
                    for c4 in range(4):
                        c = half * 4 + c4
                        pg, bpg = slot()
                        for k in range(8):
                            mm(pg[:, 0:TC], wtg[:, k, c4 * 128:(c4 + 1) * 128], nt_[:, k, :], k == 0, k == 7, [bwg, bnt], [bpg])
                        i = wcnt["gt"] % 2; wcnt["gt"] += 1
                        act(gt[i][:], pg[:, 0:TC], AF.Sigmoid, [bpg], [bgt[i]])
                        py, bpy = slot()
                        for k in range(8):
                            mm(py[:, 0:TC], wtb[:, k, c4 * 128:(c4 + 1) * 128], OB[:, k, :], k == 0, k == 7, [bwb, bOB], [bpy])
                        tt("dve", gtmp[:], py[:, 0:TC], gt[i][:], ALU.mult, [bpy, bgt[i]], [bgtmp])
                        tt("pool", M[:, c, :], M[:, c, :], gtmp[:], ALU.add, [bM, bgtmp], [bM])
                    wdone(bwgx); wdone(bwbx)
                for half in range(2):
                    wto, bwox = wload(16 + half, 512)
                    bwo = bwox[0]
                    for c4 in range(4):
                        c = half * 4 + c4
                        pt_, bp = slot()
                        for k in range(8):
                            mm(pt_[:, 0:TC], wto[:, k, c4 * 128:(c4 + 1) * 128], M[:, k, :], k == 0, k == 7, [bwo, bM], [bp])
                        tt("dve", HCb[u % 2][:, c, :], pt_[:, 0:TC], HCb[u % 2][:, c, :], ALU.add, [bp, bHCb[u % 2]], [bHCb[u % 2]])
                    wdone(bwox)
                dma("sp", H_s[o][:, :, hf * TC:(hf + 1) * TC], HCb[u % 2][:], [bHCb[u % 2]], [bH[o][hf]], "s")

        if KSTOP == "C":
            P.wait_sems("sp", "dma_"); P.build(st); return nc
        arena["p"] = xr1_off
        new_phase()
        rebarrier(ffn_bufs)
        load_ffn_scratch()
        if True:
            Wpp = sb("Wpp", [128, 2, 1024], BF16); bWp = fresh()
            Wpgt = [sb("Wpgt%d" % i, [128, 8, 128], BF16) for i in range(3)]; bWpgt = [fresh() for _ in range(3)]
            pTb = sb("pTb", [128, 2, TB], BF16); bpTb = fresh()
            print("SBUF phase D end", arena["p"])
            dma("pool", Wpp[:], w_pp.rearrange("(c p) n -> p c n", p=128), [], [bWp], "w")
            pv = pT.rearrange("(c p) t -> p c t", p=128)
            wpg_n = {"i": 0}
            dma("sp", xr[0][:], H_s[0], bH[0], [bxr[0]], "i")
            for blk in range(4):
                o = blk
                xt, bx = xr[0], bxr[0]
                dma("pool", pTb[:], pv[:, :, blk * TB:(blk + 1) * TB], [], [bpTb], "c")
                ffn(xt, bx, G2, TB)
                n3 = nb[0]; bn3 = bnb[0]
                rmsnorm(xt, bx, GP, n3, bn3, TB)
                for c in range(8):
                    wi = wpg_n["i"] % 3; wpg_n["i"] += 1
                    dma("sp", Wpgt[wi][:], WPGs[c], [bWPGs[c]], [bWpgt[wi]], "wt")
                    pg, bpg = slot()
                    for k in range(8):
                        mm(pg[:, 0:TB], Wpgt[wi][:, k, :], n3[:, k, :], k == 0, k == 7, [bWpgt[wi], bn3], [bpg])
                    pp, bpp = slot()
                    for k in range(2):
                        mm(pp[:, 0:TB], Wpp[:, k, c * 128:(c + 1) * 128], pTb[:, k, :], k == 0, k == 1, [bWp, bpTb], [bpp])
                    i = cnt["sg"] % 3; cnt["sg"] += 1
                    act(sgt[i][:], pg[:, 0:TB], AF.Sigmoid, [bpg], [bsgt[i]])
                    tt("dve", sgt[i][:], sgt[i][:], pp[:, 0:TB], ALU.mult, [bsgt[i], bpp], [bsgt[i]])
                    tt("pool", xt[:, c, :], xt[:, c, :], sgt[i][:], ALU.add, [bx, bsgt[i]], [bx])
                tt("dve", hid[:, 0:8, :], xt[:], xt[:], ALU.mult, [bx], bhid[0:8])
                pt_, bp = slot()
                for c in range(8):
                    mm(pt_[:, 0:TB], onesb[:], hid[:, c, :], c == 0, c == 7, bhid[0:8] + [bC], [bp])
                act(rstd[:], pt_[:, 0:TB], AF.Sqrt, [bp, bC], [brstd], scale=1.0 / 1024, bias=EPS)
                P.emit("dve", lambda e: e.reciprocal(out=rstd[:], in_=rstd[:]), [brstd], [brstd])
                for c in range(8):
                    if c < 4:
                        stt("dve", osa[:, c, :], xt[:, c, :], vec[:, GF + c:GF + c + 1], rstd[:], ALU.mult, ALU.mult,
                            [bx, brstd, bC], [bnb[0]])
                    else:
                        stt("dve", osb[:, c - 4, :], xt[:, c, :], vec[:, GF + c:GF + c + 1], rstd[:], ALU.mult, ALU.mult,
                            [bx, brstd, bC], bhid[8:16])
                if blk + 1 < 4:
                    dma("sp", xr[0][:], H_s[blk + 1], bH[blk + 1], [bxr[0]], "i")
                ov_ = outT.rearrange("(c p) t -> p c t", p=128)
                dma("sp", ov_[:, 0:4, blk * TB:(blk + 1) * TB], osa[:], [bnb[0]], [], "o")
                dma("sp", ov_[:, 4:8, blk * TB:(blk + 1) * TB], osb[:], bhid[8:16], [], "o")
        P.wait_sems("sp", "dma_")
        P.build(st)
    return nc


def _host_constants(r):
    shift = 512 if r == 0 else 0
    c = {}
    c["c_ident"] = np.eye(128, dtype=np.float32)
    R = np.zeros((128, 128), np.float32)
    for hb in (0, 64):
        for m in range(8):
            R[hb + m + 8, hb + m] = -1.0
            R[hb + m, hb + m + 8] = 1.0
    c["c_rmat"] = R
    sel = np.zeros((48, 48, 64), np.float32)
    for r_ in range(48):
        sel[r_, r_, :] = 1.0
    c["c_sel"] = sel.reshape(48, 48 * 64)
    A = np.zeros((256, 65), np.float32)
    for j in range(64):
        for s in range(5):
            i = 4 * j + s - 1
            if 0 <= i < 256:
                A[i, j] = 1.0
    A[:, 64] = 1.0
    c["c_aaug"] = A.reshape(2, 128, 65).transpose(1, 0, 2).reshape(128, 130).copy()
    E = np.zeros((64, 4096), np.float32)
    E[np.arange(4096) // 64, np.arange(4096)] = 1.0
    c["c_E"] = E
    ii = np.arange(128)[:, None]; jj = np.arange(128)[None, :]
    tfirst = np.where(ii > jj, 0.0, NEG).astype(np.float32)
    tdiag = np.where(ii <= jj, 0.0, NEG).astype(np.float32)
    c["c_tfirst"] = tfirst; c["c_tdiag"] = tdiag
    c["c_dfirst"] = np.full((128, 128), NEG, np.float32) if r == 0 else tfirst
    c["c_dmid"] = np.full((128, 128), NEG, np.float32) if r == 0 else np.zeros((128, 128), np.float32)
    c["c_tril"] = (ii <= jj).astype(np.float32)
    own_pos = np.concatenate([np.arange((2 * o + 1) * 512, (2 * o + 2) * 512) for o in range(4)])
    n_idx = np.arange(256)[:, None]
    ok = (16 * n_idx + 31 <= own_pos[None, :]) & (n_idx >= shift // 16) & (n_idx <= 254)
    cmpb = np.where(ok, 0.0, NEG).astype(np.float32)
    c["c_cmpb"] = cmpb.reshape(2, 128, 2048).transpose(1, 0, 2).reshape(128, 4096).copy()
    t_real = own_pos - shift
    cur = t_real // 64
    jp = np.arange(64)[None, :]
    j_real = jp - shift // 64
    dummy = j_real < 0
    forced = ((j_real == 0) | (j_real == cur[:, None]) | (j_real == cur[:, None] - 1)) & ~dummy
    future = j_real > cur[:, None]
    keep = (~(forced | future | dummy)).astype(np.float32)
    bias = np.where(forced, 1e9, np.where(future | dummy, -1e9, 0.0)).astype(np.float32)
    valid = (~(dummy | future)).astype(np.float32)

    def qlay(a):
        return a.reshape(16, 128, 64).transpose(1, 0, 2).reshape(128, 1024).copy()
    c["c_keep"] = qlay(keep); c["c_bias"] = qlay(bias); c["c_valid"] = qlay(valid)
    pos = (np.arange(4096) - shift).astype(np.float32)
    inv_freq = (500000.0 ** (-np.arange(0, 16, 2, dtype=np.float32) / 16)).astype(np.float32)
    ang = pos[None, :] * inv_freq[:, None]
    C = np.ones((64, 4096), np.float32); S = np.zeros((64, 4096), np.float32)
    C[0:8] = np.cos(ang); C[8:16] = np.cos(ang); S[0:8] = np.sin(ang); S[8:16] = np.sin(ang)
    c["c_cos"] = np.concatenate([C, C], 0); c["c_sin"] = np.concatenate([S, S], 0)
    return c


_NC_CACHE = {}


def kernel(x, p, ffn1_norm, ffn1_w_in, ffn1_w_out, mix_norm, w_in, gm_ln_g, gm_ln_b, gm_w_s, gm_b_s,
           w_branch_a, cmp_pos_k, cmp_k_w1, cmp_k_w2, cmp_pos_v, cmp_v_w1, cmp_v_w2, w_branch_b, w_out,
           ffn2_norm, ffn2_w_in, ffn2_w_out, ple_norm, ple_w_gate, ple_w_proj, final_norm):
    f = lambda a: np.ascontiguousarray(np.asarray(a, dtype=np.float32))
    x = f(x); p = f(p)
    if "nc" not in _NC_CACHE:
        _NC_CACHE["nc"] = build_program()
    nc = _NC_CACHE["nc"]

    def pcol(v):
        return f(v).reshape(8, 128).T
    vecs = np.ascontiguousarray(np.concatenate([pcol(ffn1_norm[0]), pcol(mix_norm[0]), pcol(ffn2_norm[0]),
                                                pcol(ple_norm[0]), pcol(final_norm)], axis=1))
    shared = {
        "f1_win": f(ffn1_w_in[0]), "f1_wout": f(ffn1_w_out[0]), "f2_win": f(ffn2_w_in[0]), "f2_wout": f(ffn2_w_out[0]),
        "w_in": f(w_in[0]), "vecs": vecs, "ln_g": f(gm_ln_g[0]), "ln_b": f(gm_ln_b[0]),
        "wsT": f(np.transpose(np.asarray(gm_w_s[0]), (2, 0, 1))),
        "b_s": f(np.asarray(gm_b_s[0]).reshape(1024)),
        "w_a": f(w_branch_a[0]), "w_b": f(w_branch_b[0]), "w_o": f(w_out[0]),
        "posk": f(np.asarray(cmp_pos_k[0]).reshape(16, 2, 64).transpose(1, 2, 0).reshape(128, 16)),
        "posv": f(np.asarray(cmp_pos_v[0]).reshape(16, 2, 64).transpose(1, 2, 0).reshape(128, 16)),
        "cw1k": f(np.asarray(cmp_k_w1[0]).reshape(16, 2, 64, 256).transpose(1, 2, 0, 3).reshape(128, 16, 256)),
        "cw1v": f(np.asarray(cmp_v_w1[0]).reshape(16, 2, 64, 256).transpose(1, 2, 0, 3).reshape(128, 16, 256)),
        "cw2k": f(cmp_k_w2[0]), "cw2v": f(cmp_v_w2[0]),
        "w_pg": f(ple_w_gate[0]), "w_pp": f(ple_w_proj[0]),
    }
    consts = [_host_constants(0), _host_constants(1)]
    in_maps = []
    for c in range(8):
        b, r = c // 2, c % 2
        xb = x[b]
        if r == 0:
            xs = np.concatenate([np.zeros((512, 1024), np.float32), xb[:3584]], 0)
            own = [0, 2, 4, 6]
        else:
            xs = xb
            own = [1, 3, 5, 7]
        pb = np.concatenate([p[0, b, o * 512:(o + 1) * 512] for o in own], 0)
        m = dict(shared)
        m.update(consts[r])
        m["xT"] = np.ascontiguousarray(xs.T)
        m["pT"] = np.ascontiguousarray(pb.T)
        in_maps.append(m)
    res = run_bass_kernel_spmd(nc, in_maps, core_ids=list(range(8)))
    out = np.empty((4, 4096, 1024), np.float32)
    for c in range(8):
        b, r = c // 2, c % 2
        oT = res.results[c]["outT"]
        own = [0, 2, 4, 6] if r == 0 else [1, 3, 5, 7]
        for i, o in enumerate(own):
            out[b, o * 512:(o + 1) * 512] = oT[:, i * 512:(i + 1) * 512].T
    return out
```

```python
from collections import defaultdict
from contextlib import ExitStack
import os
import numpy as np
import concourse.bass as bass
import concourse.mybir as mybir
from concourse.bass_utils import run_bass_kernel_spmd

F32 = mybir.dt.float32
BF16 = mybir.dt.bfloat16
ALU = mybir.AluOpType
AF = mybir.ActivationFunctionType
AX = mybir.AxisListType
NEG = -30000.0
EPS = 1e-6
DFF = 2816
NJ = 22


class Buf:
    __slots__ = ("name", "w", "r", "excl")

    def __init__(self, name="", excl=False):
        self.name = name
        self.w = {}
        self.r = {}
        self.excl = excl


class Prog:
    ENGS = ("pe", "act", "dve", "pool", "sp")
    DMA_POOL = {"w": 8, "i": 10, "s": 6, "c": 4, "o": 4, "wt": 6}

    def __init__(self, nc):
        self.nc = nc
        self.q = {e: [] for e in self.ENGS}
        self.cnt = {}
        self.seen = {e: defaultdict(int) for e in self.ENGS}
        self.epoch = defaultdict(int)
        self.semkeys = []
        self.dma_n = defaultdict(int)

    def _key(self, stream):
        k = (stream, self.epoch[stream])
        if k not in self.cnt:
            self.cnt[k] = 0
            self.semkeys.append(k)
        return k

    def _signal(self, stream, inc):
        k = self._key(stream)
        if self.cnt[k] + inc > 30000:
            self.epoch[stream] += 1
            k = self._key(stream)
        self.cnt[k] += inc
        return k, self.cnt[k]

    def emit(self, eng, fn, reads=(), writes=(), dma=None):
        need = {}
        own = eng if dma is None else None
        for b in reads:
            for s, v in b.w.items():
                if eng == "pe" and s[0] == "pe":
                    continue
                if need.get(s, 0) < v:
                    need[s] = v
            if b.excl:
                for s, v in b.r.items():
                    if s[0] == own:
                        continue
                    if need.get(s, 0) < v:
                        need[s] = v
        for b in writes:
            for d in (b.w, b.r):
                for s, v in d.items():
                    if s[0] == own and (own == "pe" or False):
                        continue
                    if need.get(s, 0) < v:
                        need[s] = v
        waits = []
        for s, v in need.items():
            if self.seen[eng][s] < v:
                self.seen[eng][s] = v
                waits.append((s, v))
        if dma is None:
            k, val = self._signal(eng, 1)
            inc = 1
        else:
            g = self.DMA_POOL.get(dma, 4)
            n = self.dma_n[dma]; self.dma_n[dma] += 1
            k = ("dma_" + dma, n % g)
            if k not in self.cnt:
                self.cnt[k] = 0
                self.semkeys.append(k)
            if self.cnt[k] > 0 and self.seen[eng][k] < self.cnt[k]:
                self.seen[eng][k] = self.cnt[k]
                waits.append((k, self.cnt[k]))
            self.cnt[k] += 16
            val = self.cnt[k]
            inc = 16
        self.q[eng].append((waits, fn, k, inc))
        for b in reads:
            if b.r.get(k, 0) < val:
                b.r[k] = val
        for b in writes:
            if b.w.get(k, 0) < val:
                b.w[k] = val

    def wait_sems(self, eng, prefix):
        need = [(k, v) for k, v in self.cnt.items() if k[0].startswith(prefix) and v > 0]
        self.q[eng].append((need, None, None, 0))

    def build(self, stack):
        nc = self.nc
        sems = {}
        for k in self.semkeys:
            sems[k] = stack.enter_context(nc.semaphore("s_%s_%d" % k))
        block = stack.enter_context(nc.Block())
        handles = {"pe": block.tensor, "act": block.scalar, "dve": block.vector,
                   "pool": block.gpsimd, "sp": block.sync}
        for e in self.ENGS:
            lst = self.q[e]
            if not lst:
                continue

            def body(h, lst=lst):
                for waits, fn, k, inc in lst:
                    for s, v in waits:
                        h.wait_ge(sems[s], v)
                    if fn is not None:
                        fn(h).then_inc(sems[k], inc)
            handles[e](body)


def build_program():
    nc = bass.Bass("TRN2", target_bir_lowering=False)
    P = Prog(nc)

    def din(name, shape):
        return nc.dram_tensor(name, list(shape), F32, kind="ExternalInput").ap()

    xT = din("xT", [1024, 4096]); pT = din("pT", [256, 2048])
    f1_win = din("f1_win", [1024, 5632]); f1_wout = din("f1_wout", [2816, 1024])
    f2_win = din("f2_win", [1024, 5632]); f2_wout = din("f2_wout", [2816, 1024])
    w_in = din("w_in", [1024, 6704])
    vecs = din("vecs", [128, 40])
    ln_g = din("ln_g", [1024]); ln_b = din("ln_b", [1024])
    wsT = din("wsT", [128, 8, 128]); b_s = din("b_s", [1024])
    w_a = din("w_a", [1024, 1024]); w_b = din("w_b", [1024, 1024]); w_o = din("w_o", [1024, 1024])
    posk = din("posk", [128, 16]); posv = din("posv", [128, 16])
    cw1k = din("cw1k", [128, 16, 256]); cw1v = din("cw1v", [128, 16, 256])
    cw2k = din("cw2k", [256, 64]); cw2v = din("cw2v", [256, 64])
    w_pg = din("w_pg", [1024, 1024]); w_pp = din("w_pp", [256, 1024])
    c_ident = din("c_ident", [128, 128]); c_rmat = din("c_rmat", [128, 128])
    c_sel = din("c_sel", [48, 48 * 64]); c_aaug = din("c_aaug", [128, 2 * 65])
    c_E = din("c_E", [64, 4096]); c_tfirst = din("c_tfirst", [128, 128]); c_tdiag = din("c_tdiag", [128, 128])
    c_dfirst = din("c_dfirst", [128, 128]); c_dmid = din("c_dmid", [128, 128])
    c_cmpb = din("c_cmpb", [128, 2 * 2048])
    c_keep = din("c_keep", [128, 16 * 64]); c_bias = din("c_bias", [128, 16 * 64]); c_valid = din("c_valid", [128, 16 * 64])
    c_cos = din("c_cos", [128, 4096]); c_sin = din("c_sin", [128, 4096]); c_tril = din("c_tril", [128, 128])
    outT = nc.dram_tensor("outT", [1024, 2048], F32, kind="ExternalOutput").ap()
    N_s = nc.dram_tensor("N_s", [8, 128, 8, 512], BF16).ap()
    H_s = nc.dram_tensor("H_s", [4, 128, 8, 512], F32).ap()
    KW_s = nc.dram_tensor("KW_s", [64, 4, 4096], BF16).ap()
    VW_s = nc.dram_tensor("VW_s", [128, 32, 512], BF16).ap()
    WS = nc.dram_tensor("WS", [18, 128, 8, 512], BF16).ap()
    W1s = nc.dram_tensor("W1s", [128, 8, 2 * DFF], BF16).ap()
    W2s = nc.dram_tensor("W2s", [128, NJ, 1024], BF16).ap()
    WPGs = nc.dram_tensor("WPGs", [8, 128, 8, 128], BF16).ap()
    bWPGs = [Buf() for _ in range(8)]
    bW1s = [Buf() for _ in range(NJ)]; bW2s = [Buf() for _ in range(NJ)]
    bWS = [Buf() for _ in range(18)]
    bN = [Buf() for _ in range(8)]; bH = [[Buf() for _ in range(2)] for _ in range(4)]
    bKW = [Buf() for _ in range(8)]; bVW = [Buf() for _ in range(8)]

    st = ExitStack()
    with st:
        arena = {"p": 16640}
        barrier = {"snap": {}}

        def sb(name, shape, dt=F32):
            n = 1
            for d_ in shape[1:]:
                n *= d_
            nbytes = n * (4 if dt == F32 else 2)
            off = arena["p"]
            arena["p"] = off + ((nbytes + 63) // 64) * 64
            assert arena["p"] <= 229376, (name, arena["p"])
            return nc.alloc_sbuf_tensor_at(name, list(shape), dt, offset=off)

        def new_phase():
            barrier["snap"] = dict(P.cnt)

        def fresh():
            b = Buf(); b.w = dict(barrier["snap"]); return b

        def rebarrier(bufs):
            for b in bufs:
                for k_, v_ in barrier["snap"].items():
                    if b.w.get(k_, 0) < v_:
                        b.w[k_] = v_

        ps = [st.enter_context(nc.psum_tensor("ps%d" % i, [128, 512], F32)) for i in range(7)]
        psT = st.enter_context(nc.psum_tensor("psT", [128, 1024], BF16))
        bps = [Buf(excl=True) for _ in range(7)]; bpsT = Buf(excl=True)
        ring_state = {"i": 0, "n": 6}

        def slot():
            i = ring_state["i"] % ring_state["n"]
            ring_state["i"] += 1
            return ps[i], bps[i]

        def mm(out, lhsT, rhs, start, stop, reads, writes, sgc=False):
            if sgc:
                P.emit("pe", lambda e: e.matmul(out, lhsT=lhsT, rhs=rhs, start=start, stop=stop, skip_group_check=True), reads, writes)
            else:
                P.emit("pe", lambda e: e.matmul(out, lhsT=lhsT, rhs=rhs, start=start, stop=stop), reads, writes)

        def act(out, in_, func, reads, writes, **kw):
            P.emit("act", lambda e: e.activation(out=out, in_=in_, func=func, **kw), reads, writes)

        def tt(eng, out, in0, in1, op, reads, writes):
            P.emit(eng, lambda e: e.tensor_tensor(out=out, in0=in0, in1=in1, op=op), reads, writes)

        def tsc(eng, out, in0, s1, s2, op0, op1, reads, writes):
            if s2 is None:
                P.emit(eng, lambda e: e.tensor_scalar(out=out, in0=in0, scalar1=s1, scalar2=None, op0=op0), reads, writes)
            else:
                P.emit(eng, lambda e: e.tensor_scalar(out=out, in0=in0, scalar1=s1, scalar2=s2, op0=op0, op1=op1), reads, writes)

        def stt(eng, out, in0, scalar, in1, op0, op1, reads, writes):
            P.emit(eng, lambda e: e.scalar_tensor_tensor(out=out, in0=in0, scalar=scalar, in1=in1, op0=op0, op1=op1), reads, writes)

        def cp(eng, out, in_, reads, writes):
            P.emit(eng, lambda e: e.tensor_copy(out=out, in_=in_), reads, writes)

        def dma(eng, out, in_, reads, writes, stream):
            P.emit(eng, lambda e: e.dma_start(out=out, in_=in_), reads, writes, dma=stream)

        ident = sb("ident", [128, 128], BF16); onesb = sb("onesb", [128, 128], BF16)
        rmat = sb("rmat", [128, 128], BF16); vec = sb("vec", [128, 40])
        tfirst = sb("tfirst", [128, 128], BF16); tdiag = sb("tdiag", [128, 128], BF16)
        dfirst = sb("dfirst", [128, 128], BF16); dmid = sb("dmid", [128, 128], BF16)
        epsb = sb("epsb", [128, 1]); tinyb = sb("tinyb", [128, 1])
        bC = Buf()
        for t_, d_ in ((ident, c_ident), (rmat, c_rmat), (tfirst, c_tfirst), (tdiag, c_tdiag), (dfirst, c_dfirst), (dmid, c_dmid)):
            dma("pool", t_[:], d_, [], [bC], "c")
        dma("sp", vec[:], vecs, [], [bC], "i")
        P.emit("dve", lambda e: e.memset(onesb[:], 1.0), [], [bC])
        P.emit("dve", lambda e: e.memset(epsb[:], EPS), [], [bC])
        G1, GM, G2, GP, GF = 0, 8, 16, 24, 32
        mark0 = arena["p"]

        W1 = sb("W1", [128, 8, 2 * DFF], BF16)
        W2 = sb("W2", [128, NJ, 1024], BF16)
        bW1 = [Buf() for _ in range(NJ)]; bW2 = [Buf() for _ in range(NJ)]

        def load_ffn_scratch():
            for j0 in range(0, NJ, 2):
                for base in (0, DFF):
                    dma("sp", W1[:, :, base + j0 * 128: base + (j0 + 2) * 128], W1s[:, :, base + j0 * 128: base + (j0 + 2) * 128],
                        [bW1s[j0], bW1s[j0 + 1]], [bW1[j0], bW1[j0 + 1]], "wt")
            for j0 in range(0, NJ, 2):
                dma("sp", W2[:, j0:j0 + 2, :], W2s[:, j0:j0 + 2, :], [bW2s[j0], bW2s[j0 + 1]], [bW2[j0], bW2[j0 + 1]], "wt")

        def cast_ffn_to_scratch(win, wout):
            wv = win.rearrange("(c p) n -> p c n", p=128)
            for j0 in range(0, NJ, 2):
                for base in (0, DFF):
                    dma("pool", W1s[:, :, base + j0 * 128: base + (j0 + 2) * 128], wv[:, :, base + j0 * 128: base + (j0 + 2) * 128],
                        [], [bW1s[j0], bW1s[j0 + 1]], "w")
            wo = wout.rearrange("(j p) n -> p j n", p=128)
            for j0 in range(0, NJ, 2):
                dma("pool", W2s[:, j0:j0 + 2, :], wo[:, j0:j0 + 2, :], [], [bW2s[j0], bW2s[j0 + 1]], "w")

        def load_ffn(win, wout):
            wv = win.rearrange("(c p) n -> p c n", p=128)
            for j0 in range(0, NJ, 2):
                for base in (0, DFF):
                    dma("pool", W1[:, :, base + j0 * 128: base + (j0 + 2) * 128], wv[:, :, base + j0 * 128: base + (j0 + 2) * 128],
                        [], [bW1[j0], bW1[j0 + 1]], "w")
            wo = wout.rearrange("(j p) n -> p j n", p=128)
            for j0 in range(0, NJ, 2):
                dma("pool", W2[:, j0:j0 + 2, :], wo[:, j0:j0 + 2, :], [], [bW2[j0], bW2[j0 + 1]], "w")

        TB = 512
        rstd = sb("rstd", [128, TB]); brstd = Buf()
        nb_off = arena["p"]
        nb = [sb("nb0", [128, 8, TB], BF16)]; bnb = [Buf()]
        hid_off = arena["p"]
        hid = sb("hid", [128, NJ, TB], BF16); bhid = [Buf() for _ in range(NJ)]
        osa = nc.alloc_sbuf_tensor_at("osa", [128, 4, TB], F32, offset=nb_off)
        osb = nc.alloc_sbuf_tensor_at("osb", [128, 4, TB], F32, offset=hid_off + 8 * TB * 2)
        sgt = [sb("sgt%d" % i, [128, TB]) for i in range(3)]; bsgt = [Buf() for _ in range(3)]
        xr0 = sb("xr0", [128, 8, TB])
        xr1_off = arena["p"]
        xr1 = sb("xr1", [128, 8, TB])
        xr = [xr0, xr1]; bxr = [Buf(), Buf()]
        cnt = {"sg": 0, "nb": 0}
        ffn_end = arena["p"]
        print("SBUF FFN end", ffn_end)
        ffn_bufs = bW1 + bW2 + bxr + [brstd] + bnb + bhid + bsgt

        def rmsnorm(xt, bx, gcol, out_t, bout, T):
            tt("dve", hid[:, 0:8, 0:T], xt[:, :, 0:T], xt[:, :, 0:T], ALU.mult, [bx], bhid[0:8])
            pt_, bp = slot()
            for c in range(8):
                mm(pt_[:, 0:T], onesb[:], hid[:, c, 0:T], c == 0, c == 7, bhid[0:8] + [bC], [bp])
            act(rstd[:, 0:T], pt_[:, 0:T], AF.Sqrt, [bp, bC], [brstd], scale=1.0 / 1024, bias=EPS)
            P.emit("dve", lambda e: e.reciprocal(out=rstd[:, 0:T], in_=rstd[:, 0:T]), [brstd], [brstd])
            for c in range(8):
                stt("dve", out_t[:, c, 0:T], xt[:, c, 0:T], vec[:, gcol + c:gcol + c + 1], rstd[:, 0:T],
                    ALU.mult, ALU.mult, [bx, brstd, bC], [bout])

        def ffn(xt, bx, gcol, T):
            n1 = nb[0]; bn1 = bnb[0]
            rmsnorm(xt, bx, gcol, n1, bn1, T)
            for j in range(NJ):
                pt_, bp = slot()
                for k in range(8):
                    mm(pt_[:, 0:T], W1[:, k, j * 128:(j + 1) * 128], n1[:, k, 0:T], k == 0, k == 7, [bW1[j], bn1], [bp])
                pu, bpu = slot()
                for k in range(8):
                    mm(pu[:, 0:T], W1[:, k, DFF + j * 128:DFF + (j + 1) * 128], n1[:, k, 0:T], k == 0, k == 7, [bW1[j], bn1], [bpu])
                i = cnt["sg"] % 3; cnt["sg"] += 1
                act(sgt[i][:, 0:T], pt_[:, 0:T], AF.Silu, [bp], [bsgt[i]])
                tt("dve", hid[:, j, 0:T], sgt[i][:, 0:T], pu[:, 0:T], ALU.mult, [bsgt[i], bpu], [bhid[j]])
            for c in range(8):
                pt_, bp = slot()
                for j in range(NJ):
                    mm(pt_[:, 0:T], W2[:, j, c * 128:(c + 1) * 128], hid[:, j, 0:T], j == 0, j == NJ - 1, [bW2[j], bhid[j]], [bp])
                stt("dve", xt[:, c, 0:T], pt_[:, 0:T], 0.5, xt[:, c, 0:T], ALU.mult, ALU.add, [bp, bx], [bx])

        xv = xT.rearrange("(c p) t -> p c t", p=128)
        dma("sp", xr[0][:], xv[:, :, 0:TB], [], [bxr[0]], "i")
        load_ffn(f1_win, f1_wout)
        winv0 = w_in.rearrange("(c p) n -> p c n", p=128)
        wav0 = w_a.rearrange("(c p) n -> p c n", p=128); wbv0 = w_b.rearrange("(c p) n -> p c n", p=128); wov0 = w_o.rearrange("(c p) n -> p c n", p=128)
        ws_src = [(winv0, 0, 512), (winv0, 512, 512), (winv0, 1024, 512), (winv0, 1536, 512),
                  (winv0, 4656, 512), (wav0, 0, 512), (winv0, 5168, 512), (wav0, 512, 512),
                  (winv0, 2048, 256), (winv0, 2304, 256), (winv0, 2560, 256), (winv0, 2816, 256),
                  (winv0, 5680, 512), (wbv0, 0, 512), (winv0, 6192, 512), (wbv0, 512, 512),
                  (wov0, 0, 512), (wov0, 512, 512)]
        for ti, (src, c0, ncol) in enumerate(ws_src):
            dma("pool", WS[ti][:, :, 0:ncol], src[:, :, c0:c0 + ncol], [], [bWS[ti]], "w")
        cast_ffn_to_scratch(f2_win, f2_wout)
        wpgv = w_pg.rearrange("(c p) n -> p c n", p=128)
        for c in range(8):
            dma("pool", WPGs[c], wpgv[:, :, c * 128:(c + 1) * 128], [], [bWPGs[c]], "w")
        for blk in range(8):
            s = blk
            xt, bx = xr[blk % 2], bxr[blk % 2]
            if blk + 1 < 8:
                dma("sp", xr[(blk + 1) % 2][:], xv[:, :, (blk + 1) * TB:(blk + 2) * TB], [], [bxr[(blk + 1) % 2]], "i")
            ffn(xt, bx, G1, TB)
            if s % 2 == 1:
                dma("sp", H_s[s // 2], xt[:], [bx], bH[s // 2], "s")
            n2 = nb[0]; bn2 = bnb[0]
            rmsnorm(xt, bx, GM, n2, bn2, TB)
            dma("sp", N_s[s], n2[:], [bn2], [bN[s]], "s")

        KSTOP = ""
        if KSTOP == "A":
            P.wait_sems("sp", "dma_"); P.build(st); return nc
        arena["p"] = mark0
        new_phase()
        KS = sb("KS", [128, 4, 4096], BF16); bKS = fresh()
        VS = sb("VS", [128, 32, 4, 128], BF16); bVS = fresh()
        KCMP = sb("KCMP", [64, 4, 256], BF16); VCMP = sb("VCMP", [128, 2, 4, 128], BF16); bCMP = fresh()
        mark1 = arena["p"]
        for k in range(4):
            dma("pool", KS[64:128, k, :], c_E, [], [bKS], "c")
        P.emit("pool", lambda e: e.memset(VS[:, :, :, 64:128], 1.0), [], [bVS])
        P.emit("pool", lambda e: e.memset(VCMP[:, :, :, 64:128], 1.0), [], [bCMP])
        P.emit("pool", lambda e: e.memset(KCMP[:], 0.0), [], [bCMP])
        if True:
            sbB = sb
            ctab = sb("ctab", [128, 512]); stab = sb("stab", [128, 512]); btab = fresh()
            tq = [sb("tq%d" % i, [128, 512], BF16) for i in range(2)]; btq = [fresh(), fresh()]
            t1 = sb("t1", [128, 512]); t2 = sb("t2", [128, 512]); bt1 = fresh(); bt2 = fresh()
            Wkv = sbB("Wkv", [128, 8, 1536], BF16); bWkv = fresh()
            CW1 = [sbB("CW1k", [128, 16, 256], BF16), sbB("CW1v", [128, 16, 256], BF16)]
            CW2 = [sbB("CW2k", [128, 2, 64], BF16), sbB("CW2v", [128, 2, 64], BF16)]
            POS = [sbB("POSk", [128, 16], BF16), sbB("POSv", [128, 16], BF16)]
            bCW = fresh()
            Hpre = sbB("Hpre", [128, 16, 256]); bHp = fresh()
            HT = sbB("HT", [128, 16, 256], BF16); bHT = fresh()
            constv = sbB("constv", [128, 4]); bcv = fresh()
            ntB = [sbB("ntB%d" % i, [128, 8, 512], BF16) for i in range(2)]; bntB = [fresh(), fresh()]
            KCt = [sbB("KCt", [128, 4, 512], BF16), sbB("VCt", [128, 4, 512], BF16)]; bKCt = [fresh(), fresh()]
            KWst = sbB("KWst", [64, 4, 512], BF16); bKWst = fresh()
            VWst = sbB("VWst", [128, 4, 4, 128], BF16); bVWst = fresh()
            dma("pool", Wkv[:], w_in.rearrange("(c p) n -> p c n", p=128)[:, :, 3072:4608], [], [bWkv], "w")
            dma("pool", CW1[0][:], cw1k, [], [bCW], "w"); dma("pool", CW1[1][:], cw1v, [], [bCW], "w")
            dma("pool", CW2[0][:], cw2k.rearrange("(c p) n -> p c n", p=128), [], [bCW], "w")
            dma("pool", CW2[1][:], cw2v.rearrange("(c p) n -> p c n", p=128), [], [bCW], "w")
            dma("pool", POS[0][:], posk, [], [bCW], "w"); dma("pool", POS[1][:], posv, [], [bCW], "w")
            P.emit("pool", lambda e: e.memset(Hpre[:], 0.0), [], [bHp])
            P.emit("pool", lambda e: e.memset(VWst[:, :, :, 64:128], 1.0), [], [bVWst])
            KB = 99
            def kb_stop(level):
                if KB == level:
                    P.wait_sems("sp", "dma_"); P.build(st); return True
                return False
            if kb_stop(0): return nc
            for s in range(8):
                nt_, bnt = ntB[s % 2], bntB[s % 2]
                dma("sp", nt_[:], N_s[s], [bN[s]], [bnt], "i")
                dma("sp", ctab[:], c_cos[:, s * 512:(s + 1) * 512], [], [btab], "i")
                dma("sp", stab[:], c_sin[:, s * 512:(s + 1) * 512], [], [btab], "i")
                if kb_stop(1): return nc
                for kv, col0 in ((0, 0), (1, 256)):
                    for cc in range(2):
                        pt_, bp = slot()
                        for k in range(8):
                            mm(pt_[:], Wkv[:, k, col0 + cc * 128:col0 + (cc + 1) * 128], nt_[:, k, :], k == 0, k == 7, [bWkv, bnt], [bp])
                        act(KCt[kv][0:64, 2 * cc, :], pt_[0:64, :], AF.Copy, [bp], [bKCt[kv]])
                        cp("dve", KCt[kv][0:64, 2 * cc + 1, :], pt_[64:128, :], [bp], [bKCt[kv]])
                        act(KCt[kv][64:128, 2 * cc, 0:511], pt_[0:64, 1:512], AF.Copy, [bp], [bKCt[kv]])
                        cp("dve", KCt[kv][64:128, 2 * cc + 1, 0:511], pt_[64:128, 1:512], [bp], [bKCt[kv]])
                if kb_stop(2): return nc
                for half in range(2):
                    pt_, bp = slot()
                    for mc in range(2):
                        for kv in range(2):
                            for kh in range(4):
                                gi = (mc * 2 + kv) * 4 + kh
                                for lp in range(8):
                                    rhs = bass.AP(KCt[kv], kh * 512 + 2 * lp, [[4 * 512, 128], [16, 32]])
                                    mm(pt_[:, gi * 32:(gi + 1) * 32], CW1[kv][:, half * 8 + lp, mc * 128:(mc + 1) * 128], rhs,
                                       lp == 0, lp == 7, [bCW, bKCt[kv]], [bp])
                    if half == 0:
                        o_ap = Hpre[:, :, 32 * s:32 * s + 32]
                        i_ap = bass.AP(pt_, 0, [[512, 128], [32, 16], [1, 32]])
                    elif s == 0:
                        o_ap = Hpre[:, :, 0:31]
                        i_ap = bass.AP(pt_, 1, [[512, 128], [32, 16], [1, 31]])
                    else:
                        o_ap = Hpre[:, :, 32 * s - 1:32 * s + 31]
                        i_ap = bass.AP(pt_, 0, [[512, 128], [32, 16], [1, 32]])
                    tt("dve", o_ap, o_ap, i_ap, ALU.add, [bp, bHp], [bHp])
                if kb_stop(3): return nc
                for which, col0 in ((0, 512), (1, 1024)):
                    for cc in range(2):
                        pt_, bp = slot()
                        for k in range(8):
                            mm(pt_[:], Wkv[:, k, col0 + cc * 128:col0 + (cc + 1) * 128], nt_[:, k, :], k == 0, k == 7, [bWkv, bnt], [bp])
                        i = (which * 2 + cc) % 2
                        act(tq[i][:], pt_[:], AF.Copy, [bp], [btq[i]])
                        p2, bp2 = slot()
                        mm(p2[:], rmat[:], tq[i][:], True, True, [bC, btq[i]], [bp2])
                        tt("dve", t1[:], p2[:], stab[:], ALU.mult, [bp2, btab], [bt1])
                        tt("pool", t2[:], tq[i][:], ctab[:], ALU.mult, [btq[i], btab], [bt2])
                        if which == 0:
                            tt("dve", KS[0:64, 2 * cc, s * 512:(s + 1) * 512], t1[0:64, :], t2[0:64, :], ALU.add, [bt1, bt2], [bKS])
                            tt("pool", KS[0:64, 2 * cc + 1, s * 512:(s + 1) * 512], t1[64:128, :], t2[64:128, :], ALU.add, [bt1, bt2], [bKS])
                        else:
                            tt("dve", KWst[0:64, 2 * cc, :], t1[0:64, :], t2[0:64, :], ALU.add, [bt1, bt2], [bKWst])
                            tt("pool", KWst[0:64, 2 * cc + 1, :], t1[64:128, :], t2[64:128, :], ALU.add, [bt1, bt2], [bKWst])
                dma("sp", KW_s[:, :, s * 512:(s + 1) * 512], KWst[:], [bKWst], [bKW[s]], "s")
                if kb_stop(4): return nc
                for tti in range(4):
                    pt_, bp = slot()
                    for k in range(8):
                        mm(pt_[:, 0:256], nt_[:, k, tti * 128:(tti + 1) * 128], Wkv[:, k, 768:1024], k == 0, k == 7, [bWkv, bnt], [bp])
                    for k in range(8):
                        mm(pt_[:, 256:512], nt_[:, k, tti * 128:(tti + 1) * 128], Wkv[:, k, 1280:1536], k == 0, k == 7, [bWkv, bnt], [bp])
                    KV = ""
                    if "a" not in KV:
                        act(VS[:, 4 * s + tti, :, 0:64], bass.AP(pt_, 0, [[512, 128], [64, 4], [1, 64]]), AF.Copy, [bp], [bVS])
                    if "b" not in KV:
                        cp("dve", VWst[:, tti, :, 0:64], bass.AP(pt_, 256, [[512, 128], [64, 4], [1, 64]]), [bp], [bVWst])
                if "c" not in KV:
                    dma("sp", VW_s[:, 4 * s:4 * s + 4, :], VWst[:].rearrange("p a b c -> p a (b c)"), [bVWst], [bVW[s]], "s")
                if kb_stop(5): return nc
            if kb_stop(6): return nc
            for kv in range(2):
                for mc in range(2):
                    pt_, bp = slot()
                    for lp in range(16):
                        mm(pt_[:, 0:1], CW1[kv][:, lp, mc * 128:(mc + 1) * 128], POS[kv][:, lp:lp + 1], lp == 0, lp == 15, [bCW], [bp])
                    cp("dve", constv[:, mc * 2 + kv:mc * 2 + kv + 1], pt_[:, 0:1], [bp], [bcv])
            for kv in range(2):
                for mc in range(2):
                    g0 = (mc * 2 + kv) * 4
                    act(HT[:, g0:g0 + 4, :], Hpre[:, g0:g0 + 4, :], AF.Gelu_apprx_tanh, [bHp, bcv], [bHT], bias=constv[:, mc * 2 + kv:mc * 2 + kv + 1])
            for kh in range(4):
                pt_, bp = slot()
                for mc in range(2):
                    mm(pt_[0:64, 0:256], CW2[0][:, mc, :], HT[:, (mc * 2 + 0) * 4 + kh, :], mc == 0, mc == 1, [bCW, bHT], [bp])
                cp("dve", KCMP[0:64, kh, 0:255], pt_[0:64, 0:255], [bp], [bCMP])
                for nti in range(2):
                    pt_, bp = slot()
                    for mc in range(2):
                        mm(pt_[:, 0:64], HT[:, (mc * 2 + 1) * 4 + kh, nti * 128:(nti + 1) * 128], CW2[1][:, mc, :], mc == 0, mc == 1, [bCW, bHT], [bp])
                    cp("dve", VCMP[:, nti, kh, 0:64], pt_[:, 0:64], [bp], [bCMP])

        if KSTOP == "B":
            P.wait_sems("sp", "dma_"); P.build(st); return nc
        TC = 256
        arena["p"] = mark1
        new_phase()
        if True:
            sbC = sb
            WST = sbC("WST", [128, 8, 128], BF16); bsrow = sbC("bsrow", [1, 2, 1024], BF16)
            markc = arena["p"]
            wstf = sbC("wstf", [128, 8, 128]); tril = sbC("tril", [128, 128]); bsf = sbC("bsf", [1, 1024]); bsf2 = sbC("bsf2", [1, 1024])
            bCC = fresh()
            dma("sp", wstf[:], wsT, [], [bCC], "i"); dma("sp", tril[:], c_tril, [], [bCC], "i")
            dma("sp", bsf[:], b_s.rearrange("(o n) -> o n", o=1), [], [bCC], "i")
            tt("dve", WST[:], wstf[:], bass.AP(tril, 0, [[128, 128], [0, 8], [1, 128]]), ALU.mult, [bCC], [bCC])
            cp("dve", bsrow[0:1, 0, :], bsf[:], [bCC], [bCC])
            cp("dve", bsf2[:], bsrow[0:1, 0, :], [bCC], [bCC])
            tt("dve", bsrow[0:1, 1, :], bsf[:], bsf2[:], ALU.subtract, [bCC], [bCC])
            arena["p"] = markc
            new_phase()
            rebarrier([bCC])
            aaug = sbC("aaug", [128, 2 * 65], BF16)
            LNG = sbC("LNG", [128, 1024]); LNB = sbC("LNB", [128, 1024])
            Wg = sbC("Wg", [128, 8, 48], BF16)
            cmpbu = sbC("cmpbu", [128, 2, 256], BF16); keepu = sbC("keepu", [128, 128]); biasu = sbC("biasu", [128, 128]); validu = sbC("validu", [128, 128])
            bUT = fresh()
            tq = [sb("tqc%d" % i, [128, 256], BF16) for i in range(2)]; btq = [fresh(), fresh()]
            t1 = sb("t1c", [128, 256]); t2 = sb("t2c", [128, 256]); bt1 = fresh(); bt2 = fresh()
            dma("pool", aaug[:], c_aaug, [], [bCC], "c")
            dma("sp", LNG[:], ln_g.partition_broadcast(128), [], [bCC], "i"); dma("sp", LNB[:], ln_b.partition_broadcast(128), [], [bCC], "i")
            dma("pool", Wg[:], w_in.rearrange("(c p) n -> p c n", p=128)[:, :, 4608:4656], [], [bCC], "c")
            P.emit("dve", lambda e: e.memset(tinyb[:], 1e-30), [], [bCC])
            ntC = [sbC("ntC%d" % i, [128, 8, TC], BF16) for i in range(2)]; bntC = [fresh(), fresh()]
            U = sbC("U", [128, 8, TC], BF16); bU = fresh()
            vt = [sbC("vt0", [128, 1024])] * 2; bvt = [fresh()] * 2
            vn = [sbC("vn%d" % i, [128, 1024], BF16) for i in range(2)]; bvn = [fresh(), fresh()]
            stats = sbC("stats", [128, 2, 6]); mv = sbC("mv", [128, 2]); rs = sbC("rs", [128, 1]); bst = fresh()
            gtmp = sbC("gtmp", [128, TC]); bgtmp = fresh()
            gt = [sbC("gt%d" % i, [128, TC], BF16) for i in range(2)]; bgt = [fresh(), fresh()]
            M = sbC("M", [128, 8, TC], BF16); bM = fresh()
            SGq = sbC("SGq", [128, 2, 48]); bSG = fresh()
            QRAW = [sbC("QRAW%d" % i, [64, 4, TC], BF16) for i in range(2)]; bQRAW = [fresh(), fresh()]
            QR = [sbC("QR%d" % i, [128, 4, TC], BF16) for i in range(2)]; bQR = [fresh(), fresh()]
            PT = [sbC("PT%d" % i, [128, 512], BF16) for i in range(4)]; bPT = [fresh() for _ in range(4)]
            OB = sbC("OB", [128, 8, TC], BF16); bOB = fresh()
            KWw = sbC("KWw", [64, 4, 1024], BF16); VWw = sbC("VWw", [128, 8, 4, 128], BF16); bWw = fresh()
            Wt = [sbC("Wt%d" % i, [128, 8, 512], BF16) for i in range(4)]; bWt = [fresh() for _ in range(4)]
            HCb = [sbC("HCb%d" % i, [128, 8, TC]) for i in range(2)]; bHCb = [fresh(), fresh()]
            ctC = sbC("ctC", [128, TC]); stC_ = sbC("stC_", [128, TC]); btabC = fresh()
            rzq = sbC("rzq", [128, 24]); cq = sbC("cq", [128, 24]); bcq = fresh()
            accq = [sbC("accq%d" % i, [128, 256]) for i in range(2)]; tqa = sbC("tqa", [128, 256]); tqb = sbC("tqb", [128, 256])
            obq = [sbC("obq%d" % i, [128, 256], BF16) for i in range(2)]
            baccq = [fresh(), fresh()]; btqa = fresh(); btqb = fresh(); bobq = [fresh(), fresh()]
            rz4 = sbC("rz4", [128, 4]); imp = sbC("imp", [128, 64]); score = sbC("score", [128, 64]); wk = sbC("wk", [128, 64])
            m8 = sbC("m8", [128, 8]); m8b = sbC("m8b", [128, 8]); thr = sbC("thr", [128, 1]); MB = sbC("MB", [128, 128], BF16)
            MBq = [MB, sbC("MB1", [128, 128], BF16)]
            bsm = fresh(); bMBq = [fresh(), fresh()]
            print("SBUF phase C end", arena["p"])
            P.emit("pool", lambda e: e.memset(MBq[0][:], 0.0), [], [bMBq[0]])
            P.emit("pool", lambda e: e.memset(MBq[1][:], 0.0), [], [bMBq[1]])
            wcnt = {"i": 0, "pt": 0, "gt": 0, "hc": 0}
            winv = w_in.rearrange("(c p) n -> p c n", p=128)

            WORDER = [2, 3, 0, 1] + list(range(4, 18))
            wq = []
            wfree = [0, 1, 2, 3]
            wstate = {"next": 0}

            def wfill():
                while wfree and wstate["next"] < 8 * 18:
                    ti = WORDER[wstate["next"] % 18]; wstate["next"] += 1
                    i = wfree.pop(0)
                    ncols = 256 if 8 <= ti < 12 else 512
                    dma("sp", Wt[i][:, :, 0:ncols], WS[ti][:, :, 0:ncols], [bWS[ti]], [bWt[i]], "wt")
                    wq.append((ti, i))

            def wload(ti, ncols):
                if not wq:
                    wfill()
                t_, i = wq.pop(0)
                assert t_ == ti, (t_, ti)
                return Wt[i], (bWt[i], i)

            def wdone(bw):
                wfree.append(bw[1])
                wfill()

            def w1024(mat):
                return mat.rearrange("(c p) n -> p c n", p=128)

            def unit_loads(u):
                o, hf = u // 2, u % 2
                sl = 2 * o + 1
                pos0 = sl * 512 + hf * TC
                dma("sp", ntC[u % 2][:], N_s[sl][:, :, hf * TC:(hf + 1) * TC], [bN[sl]], [bntC[u % 2]], "i")
                dma("sp", ctC[:], c_cos[:, pos0:pos0 + TC], [], [btabC], "i")
                dma("sp", stC_[:], c_sin[:, pos0:pos0 + TC], [], [btabC], "i")
                if hf == 0:
                    dma("sp", KWw[:], KW_s[:, :, (2 * o) * 512:(2 * o + 2) * 512], [bKW[2 * o], bKW[2 * o + 1]], [bWw], "i")
                    dma("sp", VWw[:].rearrange("p a b c -> p a (b c)"), VW_s[:, 8 * o:8 * o + 8, :], [bVW[2 * o], bVW[2 * o + 1]], [bWw], "i")
                dma("pool", cmpbu[:], c_cmpb.rearrange("p (n t) -> p n t", n=2)[:, :, u * 256:(u + 1) * 256], [], [bUT], "c")
                dma("sp", HCb[u % 2][:], H_s[o][:, :, hf * TC:(hf + 1) * TC], [bH[o][hf]], [bHCb[u % 2]], "i")
                dma("sp", keepu[:], c_keep[:, 2 * u * 64:(2 * u + 2) * 64], [], [bUT], "i")
                dma("sp", biasu[:], c_bias[:, 2 * u * 64:(2 * u + 2) * 64], [], [bUT], "i")
                dma("sp", validu[:], c_valid[:, 2 * u * 64:(2 * u + 2) * 64], [], [bUT], "i")

            unit_loads(0)
            for u in range(8):
                o, hf = u // 2, u % 2
                sl = 2 * o + 1
                nt_, bnt = ntC[u % 2], bntC[u % 2]
                ring_state["n"] = 6
                wv0, bwv0x = wload(2, 512)
                wv1, bwv1x = wload(3, 512)
                for tti in range(2):
                    for half, (wt, bw) in enumerate(((wv0, bwv0x[0]), (wv1, bwv1x[0]))):
                        pt_, bp = slot()
                        for k in range(8):
                            mm(pt_[:], nt_[:, k, tti * 128:(tti + 1) * 128], wt[:, k, :], k == 0, k == 7, [bw, bnt], [bp])
                        act(vt[tti][:, half * 512:(half + 1) * 512], pt_[:], AF.Gelu_apprx_tanh, [bp], [bvt[tti]])
                    for hh in range(2):
                        P.emit("dve", lambda e, tti=tti, hh=hh: e.bn_stats(out=stats[:, hh, :], in_=vt[tti][:, hh * 512:(hh + 1) * 512]), [bvt[tti]], [bst])
                    P.emit("dve", lambda e: e.bn_aggr(out=mv[:], in_=stats[:]), [bst], [bst])
                    act(rs[:], mv[:, 1:2], AF.Sqrt, [bst, bC], [bst], bias=EPS)
                    P.emit("dve", lambda e: e.reciprocal(out=rs[:], in_=rs[:]), [bst], [bst])
                    tsc("dve", vt[tti][:], vt[tti][:], mv[:, 0:1], rs[:, 0:1], ALU.subtract, ALU.mult, [bvt[tti], bst], [bvt[tti]])
                    tt("pool", vt[tti][:], vt[tti][:], LNG[:], ALU.mult, [bvt[tti], bCC], [bvt[tti]])
                    tt("dve", vn[tti][:], vt[tti][:], LNB[:], ALU.add, [bvt[tti], bCC], [bvn[tti]])
                wdone(bwv0x); wdone(bwv1x)
                for half in range(2):
                    wt, bwx = wload(half, 512)
                    bw = bwx[0]
                    for c4 in range(4):
                        c = half * 4 + c4
                        pt_, bp = slot()
                        for k in range(8):
                            mm(pt_[:, 0:TC], wt[:, k, c4 * 128:(c4 + 1) * 128], nt_[:, k, :], k == 0, k == 7, [bw, bnt], [bp])
                        act(U[:, c, :], pt_[:, 0:TC], AF.Gelu_apprx_tanh, [bp], [bU])
                    wdone(bwx)
                for g in range(8):
                    pt_, bp = slot()
                    for tti in range(2):
                        mm(pt_[:, tti * 128:(tti + 1) * 128], vn[tti][:, g * 128:(g + 1) * 128], WST[:, g, :], True, False, [bvn[tti], bCC], [bp])
                        mm(pt_[:, tti * 128:(tti + 1) * 128], onesb[0:1, :], bsrow[0:1, 0, g * 128:(g + 1) * 128], False, False, [bC, bCC], [bp])
                        mm(pt_[:, tti * 128:(tti + 1) * 128], onesb[0:1, :], bsrow[0:1, 1, g * 128:(g + 1) * 128], False, True, [bC, bCC], [bp])
                    tt("dve", U[:, g, :], pt_[:, 0:TC], U[:, g, :], ALU.mult, [bp, bU], [bU])
                for half in range(2):
                    wtg, bwgx = wload(4 + 2 * half, 512)
                    wta, bwax = wload(5 + 2 * half, 512)
                    bwg, bwa = bwgx[0], bwax[0]
                    for c4 in range(4):
                        c = half * 4 + c4
                        pg, bpg = slot()
                        for k in range(8):
                            mm(pg[:, 0:TC], wtg[:, k, c4 * 128:(c4 + 1) * 128], nt_[:, k, :], k == 0, k == 7, [bwg, bnt], [bpg])
                        i = wcnt["gt"] % 2; wcnt["gt"] += 1
                        act(gt[i][:], pg[:, 0:TC], AF.Sigmoid, [bpg], [bgt[i]])
                        py, bpy = slot()
                        for k in range(8):
                            mm(py[:, 0:TC], wta[:, k, c4 * 128:(c4 + 1) * 128], U[:, k, :], k == 0, k == 7, [bwa, bU], [bpy])
                        tt("dve", M[:, c, :], py[:, 0:TC], gt[i][:], ALU.mult, [bpy, bgt[i]], [bM])
                    wdone(bwgx); wdone(bwax)
                for qq in range(2):
                    pt_, bp = slot()
                    for k in range(8):
                        mm(pt_[:, 0:48], nt_[:, k, qq * 128:(qq + 1) * 128], Wg[:, k, :], k == 0, k == 7, [bCC, bnt], [bp])
                    act(SGq[:, qq, :], pt_[:, 0:48], AF.Sigmoid, [bp], [bSG])
                ring_state["n"] = 3

                def q_proj(kh):
                    qraw, bqraw = QRAW[kh % 2], bQRAW[kh % 2]
                    qr, bqr = QR[kh % 2], bQR[kh % 2]
                    wq_, bwqx = wload(8 + kh, 256)
                    bwq = bwqx[0]
                    for cc in range(2):
                        pt_, bp = slot()
                        for k in range(8):
                            mm(pt_[:, 0:TC], wq_[:, k, cc * 128:(cc + 1) * 128], nt_[:, k, :], k == 0, k == 7, [bwq, bnt], [bp])
                        i = cc
                        act(tq[i][:, 0:TC], pt_[:, 0:TC], AF.Copy, [bp], [btq[i]])
                        cp("pool", qraw[0:64, 2 * cc, :], tq[i][0:64, 0:TC], [btq[i]], [bqraw])
                        cp("pool", qraw[0:64, 2 * cc + 1, :], tq[i][64:128, 0:TC], [btq[i]], [bqraw])
                        p2, bp2 = slot()
                        mm(p2[:, 0:TC], rmat[:], tq[i][:, 0:TC], True, True, [bC, btq[i]], [bp2])
                        tt("dve", t1[:, 0:TC], p2[:, 0:TC], stC_[:], ALU.mult, [bp2, btabC], [bt1])
                        tt("pool", t2[:, 0:TC], tq[i][:, 0:TC], ctC[:], ALU.mult, [btq[i], btabC], [bt2])
                        tt("dve", qr[0:64, 2 * cc, :], t1[0:64, 0:TC], t2[0:64, 0:TC], ALU.add, [bt1, bt2], [bqr])
                        tt("pool", qr[0:64, 2 * cc + 1, :], t1[64:128, 0:TC], t2[64:128, 0:TC], ALU.add, [bt1, bt2], [bqr])
                    wdone(bwqx)

                def attention(kh):
                    wfill()
                    qraw, bqraw = QRAW[kh % 2], bQRAW[kh % 2]
                    qr, bqr = QR[kh % 2], bQR[kh % 2]

                    def qap(t, npart, qs):
                        return bass.AP(t, qs, [[4 * TC, npart], [TC, 4], [1, 128]])

                    tiles = []
                    for qi_ in range(2):
                        qs = qi_ * 128
                        for nti in range(2):
                            tiles.append(dict(kind="cmp", q=qi_, nti=nti, lhsT=KCMP[0:64, kh, nti * 128:(nti + 1) * 128], rhs=qap(qraw, 64, qs),
                                              rr=[bCMP, bqraw], btab=cmpbu, boff=nti * 256 + qi_ * 128, v=VCMP[:, nti, kh, 0:65], vr=[bCMP],
                                              bank=3, br=0, first=nti == 0, last=nti == 1))
                    for qi_ in range(2):
                        qs = qi_ * 128
                        for off in range(5):
                            lt = hf * 2 + qi_ + off
                            if o == 0 and lt < 4:
                                btab_ = dfirst if off == 0 else dmid
                            elif off == 0:
                                btab_ = tfirst
                            elif off == 4:
                                btab_ = tdiag
                            else:
                                btab_ = None
                            tiles.append(dict(kind="win", q=qi_, lhsT=KWw[0:64, kh, lt * 128:(lt + 1) * 128], rhs=qap(qr, 64, qs), rr=[bWw, bqr],
                                              btab=btab_, boff=0, v=VWw[:, lt, kh, 0:65], vr=[bWw], bank=5, br=2, first=off == 0, last=off == 4))
                    for qi_ in range(2):
                        qs = qi_ * 128
                        QT = sl * 4 + hf * 2 + qi_
                        for kt in range(QT + 1):
                            tiles.append(dict(kind="sel", q=qi_, kt=kt, lhsT=KS[:, kh, kt * 128:(kt + 1) * 128], rhs=qap(qr, 128, qs), rr=[bKS, bqr],
                                              btab=tdiag if kt == QT else None, boff=0, v=VS[:, kt, kh, 0:65], vr=[bVS], bank=4, br=1,
                                              first=kt == 0, last=kt == QT))

                    def emit_S(t):
                        pS, bpS = slot()
                        mm(pS[:], t["lhsT"], t["rhs"], True, t["btab"] is None, t["rr"], [bpS])
                        if t["btab"] is not None:
                            ncol = 1
                            for d_ in t["btab"].shape[1:]:
                                ncol *= d_
                            mm(pS[:], ident[:], bass.AP(t["btab"], t["boff"], [[ncol, 128], [0, 4], [1, 128]]), False, True, [bC, bCC, bUT], [bpS])
                        i = wcnt["pt"] % 4; wcnt["pt"] += 1
                        act(PT[i][:], pS[:], AF.Exp, [bpS], [bPT[i]], scale=0.125)
                        t["pt"] = i

                    def emit_mask_dve(qi_):
                        tsc("dve", rz4[:], bass.AP(ps[6], 64, [[512, 128], [65, 4]]), 1e-30, None, ALU.max, None, [bps[6], bCC], [bsm])
                        P.emit("dve", lambda e: e.reciprocal(out=rz4[:], in_=rz4[:]), [bsm], [bsm])
                        tsc("dve", imp[:], ps[6][:, 0:64], rz4[:, 0:1], None, ALU.mult, None, [bps[6], bsm], [bsm])
                        for g in range(1, 4):
                            stt("dve", imp[:], ps[6][:, g * 65:g * 65 + 64], rz4[:, g:g + 1], imp[:], ALU.mult, ALU.add, [bps[6], bsm], [bsm])
                        tt("dve", score[:], imp[:], keepu[:, qi_ * 64:(qi_ + 1) * 64], ALU.mult, [bsm, bUT], [bsm])
                        tt("dve", score[:], score[:], biasu[:, qi_ * 64:(qi_ + 1) * 64], ALU.add, [bsm, bUT], [bsm])
                        P.emit("dve", lambda e: e.max(out=m8[:], in_=score[:]), [bsm], [bsm])
                        P.emit("dve", lambda e: e.match_replace(out=wk[:], in_to_replace=m8[:], in_values=score[:], imm_value=-3e9), [bsm], [bsm])
                        P.emit("dve", lambda e: e.max(out=m8b[:], in_=wk[:]), [bsm], [bsm])
                        P.emit("dve", lambda e: e.tensor_reduce(out=thr[:], in_=m8b[:], axis=AX.X, op=ALU.min), [bsm], [bsm])
                        stt("dve", wk[:], score[:], thr[:, 0:1], validu[:, qi_ * 64:(qi_ + 1) * 64], ALU.is_ge, ALU.mult, [bsm, bUT], [bsm])
                        tsc("dve", MBq[qi_][:, 64:128], wk[:], 1.0, -NEG, ALU.subtract, ALU.mult, [bsm], [bMBq[qi_]])

                    def emit_mask_pe(qi_):
                        qs = qi_ * 128
                        P.emit("pe", lambda e: e.transpose(out=psT[:, qi_ * 128:(qi_ + 1) * 128], in_=MBq[qi_][:], identity=ident[:]), [bMBq[qi_], bC], [bpsT])
                        cp("dve", bass.AP(qr, 64 * 4 * TC + qs, [[4 * TC, 64], [TC, 4], [1, 128]]),
                           bass.AP(psT, 64 * 1024 + qi_ * 128, [[1024, 64], [0, 4], [1, 128]]), [bpsT], [bqr])

                    pending = []

                    def emit_combine(qi_, br, bank):
                        rz_ = rzq[:, qi_ * 12 + br * 4:qi_ * 12 + br * 4 + 4]
                        c_ = cq[:, qi_ * 12 + br * 4:qi_ * 12 + br * 4 + 4]
                        tsc("dve", rz_, bass.AP(ps[bank], 64, [[512, 128], [65, 4]]), 1e-30, None, ALU.max, None, [bps[bank]], [bcq])
                        P.emit("dve", lambda e: e.reciprocal(out=rz_, in_=rz_), [bcq], [bcq])
                        tt("dve", c_, rz_, bass.AP(SGq, qi_ * 48 + 12 * kh + br, [[96, 128], [3, 4]]), ALU.mult, [bcq, bSG], [bcq])
                        ov = bass.AP(ps[bank], 0, [[512, 128], [65, 4], [1, 64]])
                        cb = bass.AP(cq, qi_ * 12 + br * 4, [[24, 128], [1, 4], [0, 64]])

                        def v3(t):
                            return bass.AP(t, 0, [[256, 128], [64, 4], [1, 64]])
                        if br == 0:
                            tt("dve", v3(accq[qi_]), ov, cb, ALU.mult, [bps[bank], bcq], [baccq[qi_]])
                        elif br == 2:
                            tt("dve", v3(tqa), ov, cb, ALU.mult, [bps[bank], bcq], [btqa])
                            tt("pool", accq[qi_][:], accq[qi_][:], tqa[:], ALU.add, [baccq[qi_], btqa], [baccq[qi_]])
                        else:
                            tt("dve", v3(tqb), ov, cb, ALU.mult, [bps[bank], bcq], [btqb])
                            tt("pool", obq[qi_][:], accq[qi_][:], tqb[:], ALU.add, [baccq[qi_], btqb], [bobq[qi_]])
                            pending.append(qi_)

                    def emit_writeback(qi_):
                        qs = qi_ * 128
                        for hp in range(2):
                            P.emit("pe", lambda e, hp=hp: e.transpose(out=psT[:, 256 + qi_ * 256 + hp * 128:256 + qi_ * 256 + (hp + 1) * 128],
                                                                    in_=obq[qi_][:, hp * 128:(hp + 1) * 128], identity=ident[:]), [bobq[qi_], bC], [bpsT])
                        cp("dve", bass.AP(OB, (2 * kh) * TC + qs, [[8 * TC, 128], [TC, 2], [1, 128]]),
                           bass.AP(psT, 256 + qi_ * 256, [[1024, 128], [128, 2], [1, 128]]), [bpsT], [bOB])

                    def emit_PV(t):
                        i = t["pt"]
                        for g in range(4):
                            mm(ps[t["bank"]][:, g * 65:(g + 1) * 65], PT[i][:, g * 128:(g + 1) * 128], t["v"],
                               t["first"] and g == 0, t["last"], t["vr"] + [bPT[i]], [bps[t["bank"]]], sgc=True)
                        if t["kind"] == "cmp":
                            nti = t["nti"]
                            for g in range(4):
                                mm(ps[6][:, g * 65:(g + 1) * 65], PT[i][:, g * 128:(g + 1) * 128], aaug[:, nti * 65:(nti + 1) * 65],
                                   nti == 0 and g == 0, nti == 1, [bPT[i], bCC], [bps[6]], sgc=True)
                            if nti == 1:
                                emit_mask_dve(t["q"])
                        if t["last"]:
                            emit_combine(t["q"], t["br"], t["bank"])

                    LA = 2
                    age = 0
                    for idx in range(len(tiles) + LA):
                        if idx < len(tiles):
                            if tiles[idx]["kind"] == "sel" and tiles[idx]["kt"] == 0:
                                emit_mask_pe(tiles[idx]["q"])
                            emit_S(tiles[idx])
                        if idx >= LA:
                            emit_PV(tiles[idx - LA])
                        if pending:
                            age += 1
                            if age >= 4:
                                emit_writeback(pending.pop(0)); age = 0
                    while pending:
                        emit_writeback(pending.pop(0))

                q_proj(0)
                for kh in range(4):
                    if kh + 1 < 4:
                        q_proj(kh + 1)
                    attention(kh)
                if u + 1 < 8:
                    unit_loads(u + 1)
                ring_state["n"] = 6
                for half in range(2):
                    wtg, bwgx = wload(12 + 2 * half, 512)
                    wtb, bwbx = wload(13 + 2 * half, 512)
                    bwg, bwb = bwgx[0], bwbx[0]
                    for c4 in range(4):
                        c = half * 4 + c4
                        pg, bpg = slot()
                        for k in range(8):
                            mm(pg[:, 0:TC], wtg[:, k, c4 * 128:(c4 + 1) * 128], nt_[:, k, :], k == 0, k == 7, [bwg, bnt], [bpg])
                        i = wcnt["gt"] % 2; wcnt["gt"] += 1
                        act(gt[i][:], pg[:, 0:TC], AF.Sigmoid, [bpg], [bgt[i]])
                        py, bpy = slot()
                        for k in range(8):
                            mm(py[:, 0:TC], wtb[:, k, c4 * 128:(c4 + 1) * 128], OB[:, k, :], k == 0, k == 7, [bwb, bOB], [bpy])
                        tt("dve", gtmp[:], py[:, 0:TC], gt[i][:], ALU.mult, [bpy, bgt[i]], [bgtmp])
                        tt("pool", M[:, c, :], M[:, c, :], gtmp[:], ALU.add, [bM, bgtmp], [bM])
                    wdone(bwgx); wdone(bwbx)
                for half in range(2):
                    wto, bwox = wload(16 + half, 512)
                    bwo = bwox[0]
                    for c4 in range(4):
                        c = half * 4 + c4
                        pt_, bp = slot()
                        for k in range(8):
                            mm(pt_[:, 0:TC], wto[:, k, c4 * 128:(c4 + 1) * 128], M[:, k, :], k == 0, k == 7, [bwo, bM], [bp])
                        tt("dve", HCb[u % 2][:, c, :], pt_[:, 0:TC], HCb[u % 2][:, c, :], ALU.add, [bp, bHCb[u % 2]], [bHCb[u % 2]])
                    wdone(bwox)
                dma("sp", H_s[o][:, :, hf * TC:(hf + 1) * TC], HCb[u % 2][:], [bHCb[u % 2]], [bH[o][hf]], "s")

        if KSTOP == "C":
            P.wait_sems("sp", "dma_"); P.build(st); return nc
        arena["p"] = xr1_off
        new_phase()
        rebarrier(ffn_bufs)
        load_ffn_scratch()
        if True:
            Wpp = sb("Wpp", [128, 2, 1024], BF16); bWp = fresh()
            Wpgt = [sb("Wpgt%d" % i, [128, 8, 128], BF16) for i in range(3)]; bWpgt = [fresh() for _ in range(3)]
            pTb = sb("pTb", [128, 2, TB], BF16); bpTb = fresh()
            print("SBUF phase D end", arena["p"])
            dma("pool", Wpp[:], w_pp.rearrange("(c p) n -> p c n", p=128), [], [bWp], "w")
            pv = pT.rearrange("(c p) t -> p c t", p=128)
            wpg_n = {"i": 0}
            dma("sp", xr[0][:], H_s[0], bH[0], [bxr[0]], "i")
            for blk in range(4):
                o = blk
                xt, bx = xr[0], bxr[0]
                dma("pool", pTb[:], pv[:, :, blk * TB:(blk + 1) * TB], [], [bpTb], "c")
                ffn(xt, bx, G2, TB)
                n3 = nb[0]; bn3 = bnb[0]
                rmsnorm(xt, bx, GP, n3, bn3, TB)
                for c in range(8):
                    wi = wpg_n["i"] % 3; wpg_n["i"] += 1
                    dma("sp", Wpgt[wi][:], WPGs[c], [bWPGs[c]], [bWpgt[wi]], "wt")
                    pg, bpg = slot()
                    for k in range(8):
                        mm(pg[:, 0:TB], Wpgt[wi][:, k, :], n3[:, k, :], k == 0, k == 7, [bWpgt[wi], bn3], [bpg])
                    pp, bpp = slot()
                    for k in range(2):
                        mm(pp[:, 0:TB], Wpp[:, k, c * 128:(c + 1) * 128], pTb[:, k, :], k == 0, k == 1, [bWp, bpTb], [bpp])
                    i = cnt["sg"] % 3; cnt["sg"] += 1
                    act(sgt[i][:], pg[:, 0:TB], AF.Sigmoid, [bpg], [bsgt[i]])
                    tt("dve", sgt[i][:], sgt[i][:], pp[:, 0:TB], ALU.mult, [bsgt[i], bpp], [bsgt[i]])
                    tt("pool", xt[:, c, :], xt[:, c, :], sgt[i][:], ALU.add, [bx, bsgt[i]], [bx])
                tt("dve", hid[:, 0:8, :], xt[:], xt[:], ALU.mult, [bx], bhid[0:8])
                pt_, bp = slot()
                for c in range(8):
                    mm(pt_[:, 0:TB], onesb[:], hid[:, c, :], c == 0, c == 7, bhid[0:8] + [bC], [bp])
                act(rstd[:], pt_[:, 0:TB], AF.Sqrt, [bp, bC], [brstd], scale=1.0 / 1024, bias=EPS)
                P.emit("dve", lambda e: e.reciprocal(out=rstd[:], in_=rstd[:]), [brstd], [brstd])
                for c in range(8):
                    if c < 4:
                        stt("dve", osa[:, c, :], xt[:, c, :], vec[:, GF + c:GF + c + 1], rstd[:], ALU.mult, ALU.mult,
                            [bx, brstd, bC], [bnb[0]])
                    else:
                        stt("dve", osb[:, c - 4, :], xt[:, c, :], vec[:, GF + c:GF + c + 1], rstd[:], ALU.mult, ALU.mult,
                            [bx, brstd, bC], bhid[8:16])
                if blk + 1 < 4:
                    dma("sp", xr[0][:], H_s[blk + 1], bH[blk + 1], [bxr[0]], "i")
                ov_ = outT.rearrange("(c p) t -> p c t", p=128)
                dma("sp", ov_[:, 0:4, blk * TB:(blk + 1) * TB], osa[:], [bnb[0]], [], "o")
                dma("sp", ov_[:, 4:8, blk * TB:(blk + 1) * TB], osb[:], bhid[8:16], [], "o")
        P.wait_sems("sp", "dma_")
        P.build(st)
    return nc


def _host_constants(r):
    shift = 512 if r == 0 else 0
    c = {}
    c["c_ident"] = np.eye(128, dtype=np.float32)
    R = np.zeros((128, 128), np.float32)
    for hb in (0, 64):
        for m in range(8):
            R[hb + m + 8, hb + m] = -1.0
            R[hb + m, hb + m + 8] = 1.0
    c["c_rmat"] = R
    sel = np.zeros((48, 48, 64), np.float32)
    for r_ in range(48):
        sel[r_, r_, :] = 1.0
    c["c_sel"] = sel.reshape(48, 48 * 64)
    A = np.zeros((256, 65), np.float32)
    for j in range(64):
        for s in range(5):
            i = 4 * j + s - 1
            if 0 <= i < 256:
                A[i, j] = 1.0
    A[:, 64] = 1.0
    c["c_aaug"] = A.reshape(2, 128, 65).transpose(1, 0, 2).reshape(128, 130).copy()
    E = np.zeros((64, 4096), np.float32)
    E[np.arange(4096) // 64, np.arange(4096)] = 1.0
    c["c_E"] = E
    ii = np.arange(128)[:, None]; jj = np.arange(128)[None, :]
    tfirst = np.where(ii > jj, 0.0, NEG).astype(np.float32)
    tdiag = np.where(ii <= jj, 0.0, NEG).astype(np.float32)
    c["c_tfirst"] = tfirst; c["c_tdiag"] = tdiag
    c["c_dfirst"] = np.full((128, 128), NEG, np.float32) if r == 0 else tfirst
    c["c_dmid"] = np.full((128, 128), NEG, np.float32) if r == 0 else np.zeros((128, 128), np.float32)
    c["c_tril"] = (ii <= jj).astype(np.float32)
    own_pos = np.concatenate([np.arange((2 * o + 1) * 512, (2 * o + 2) * 512) for o in range(4)])
    n_idx = np.arange(256)[:, None]
    ok = (16 * n_idx + 31 <= own_pos[None, :]) & (n_idx >= shift // 16) & (n_idx <= 254)
    cmpb = np.where(ok, 0.0, NEG).astype(np.float32)
    c["c_cmpb"] = cmpb.reshape(2, 128, 2048).transpose(1, 0, 2).reshape(128, 4096).copy()
    t_real = own_pos - shift
    cur = t_real // 64
    jp = np.arange(64)[None, :]
    j_real = jp - shift // 64
    dummy = j_real < 0
    forced = ((j_real == 0) | (j_real == cur[:, None]) | (j_real == cur[:, None] - 1)) & ~dummy
    future = j_real > cur[:, None]
    keep = (~(forced | future | dummy)).astype(np.float32)
    bias = np.where(forced, 1e9, np.where(future | dummy, -1e9, 0.0)).astype(np.float32)
    valid = (~(dummy | future)).astype(np.float32)

    def qlay(a):
        return a.reshape(16, 128, 64).transpose(1, 0, 2).reshape(128, 1024).copy()
    c["c_keep"] = qlay(keep); c["c_bias"] = qlay(bias); c["c_valid"] = qlay(valid)
    pos = (np.arange(4096) - shift).astype(np.float32)
    inv_freq = (500000.0 ** (-np.arange(0, 16, 2, dtype=np.float32) / 16)).astype(np.float32)
    ang = pos[None, :] * inv_freq[:, None]
    C = np.ones((64, 4096), np.float32); S = np.zeros((64, 4096), np.float32)
    C[0:8] = np.cos(ang); C[8:16] = np.cos(ang); S[0:8] = np.sin(ang); S[8:16] = np.sin(ang)
    c["c_cos"] = np.concatenate([C, C], 0); c["c_sin"] = np.concatenate([S, S], 0)
    return c


_NC_CACHE = {}


def kernel(x, p, ffn1_norm, ffn1_w_in, ffn1_w_out, mix_norm, w_in, gm_ln_g, gm_ln_b, gm_w_s, gm_b_s,
           w_branch_a, cmp_pos_k, cmp_k_w1, cmp_k_w2, cmp_pos_v, cmp_v_w1, cmp_v_w2, w_branch_b, w_out,
           ffn2_norm, ffn2_w_in, ffn2_w_out, ple_norm, ple_w_gate, ple_w_proj, final_norm):
    f = lambda a: np.ascontiguousarray(np.asarray(a, dtype=np.float32))
    x = f(x); p = f(p)
    if "nc" not in _NC_CACHE:
        _NC_CACHE["nc"] = build_program()
    nc = _NC_CACHE["nc"]

    def pcol(v):
        return f(v).reshape(8, 128).T
    vecs = np.ascontiguousarray(np.concatenate([pcol(ffn1_norm[0]), pcol(mix_norm[0]), pcol(ffn2_norm[0]),
                                                pcol(ple_norm[0]), pcol(final_norm)], axis=1))
    shared = {
        "f1_win": f(ffn1_w_in[0]), "f1_wout": f(ffn1_w_out[0]), "f2_win": f(ffn2_w_in[0]), "f2_wout": f(ffn2_w_out[0]),
        "w_in": f(w_in[0]), "vecs": vecs, "ln_g": f(gm_ln_g[0]), "ln_b": f(gm_ln_b[0]),
        "wsT": f(np.transpose(np.asarray(gm_w_s[0]), (2, 0, 1))),
        "b_s": f(np.asarray(gm_b_s[0]).reshape(1024)),
        "w_a": f(w_branch_a[0]), "w_b": f(w_branch_b[0]), "w_o": f(w_out[0]),
        "posk": f(np.asarray(cmp_pos_k[0]).reshape(16, 2, 64).transpose(1, 2, 0).reshape(128, 16)),
        "posv": f(np.asarray(cmp_pos_v[0]).reshape(16, 2, 64).transpose(1, 2, 0).reshape(128, 16)),
        "cw1k": f(np.asarray(cmp_k_w1[0]).reshape(16, 2, 64, 256).transpose(1, 2, 0, 3).reshape(128, 16, 256)),
        "cw1v": f(np.asarray(cmp_v_w1[0]).reshape(16, 2, 64, 256).transpose(1, 2, 0, 3).reshape(128, 16, 256)),
        "cw2k": f(cmp_k_w2[0]), "cw2v": f(cmp_v_w2[0]),
        "w_pg": f(ple_w_gate[0]), "w_pp": f(ple_w_proj[0]),
    }
    consts = [_host_constants(0), _host_constants(1)]
    in_maps = []
    for c in range(8):
        b, r = c // 2, c % 2
        xb = x[b]
        if r == 0:
            xs = np.concatenate([np.zeros((512, 1024), np.float32), xb[:3584]], 0)
            own = [0, 2, 4, 6]
        else:
            xs = xb
            own = [1, 3, 5, 7]
        pb = np.concatenate([p[0, b, o * 512:(o + 1) * 512] for o in own], 0)
        m = dict(shared)
        m.update(consts[r])
        m["xT"] = np.ascontiguousarray(xs.T)
        m["pT"] = np.ascontiguousarray(pb.T)
        in_maps.append(m)
    res = run_bass_kernel_spmd(nc, in_maps, core_ids=list(range(8)))
    out = np.empty((4, 4096, 1024), np.float32)
    for c in range(8):
        b, r = c // 2, c % 2
        oT = res.results[c]["outT"]
        own = [0, 2, 4, 6] if r == 0 else [1, 3, 5, 7]
        for i, o in enumerate(own):
            out[b, o * 512:(o + 1) * 512] = oT[:, i * 512:(i + 1) * 512].T
    return out
```
